# Optimizing a Trainium2 kernel written in Bass

```python
import math
import jax
import jax.numpy as jnp
from jax import lax
import numpy as np

D_MODEL = 4096
BATCH = 2
SEQ = 4096
DEPTH = 2

PLE_DIM = 256
D_FF = 8192
EPS = 1e-6
ROPE_THETA = 10000.0

RET_HEADS = 8
RET_DK = 128
RET_DV = 256
RET_CHUNK = 128

GDN_HEADS = 16
GDN_DK = 128
GDN_DV = 128
GDN_CHUNK = 64
CONV_WIDTH = 5

DIFF_HEADS = 8
DIFF_D = 128
DIFF_DV = 2 * DIFF_D
Q_BLOCK = 128

N_BRANCH = 3
BRANCH_WIDTH = 2048

IN_SIZES = (
    RET_HEADS * RET_DK, RET_HEADS * RET_DK, RET_HEADS * RET_DV, RET_HEADS * RET_DV,
    GDN_HEADS * (2 * GDN_DK + GDN_DV), GDN_HEADS * GDN_DV, 2 * GDN_HEADS, 2 * GDN_HEADS,
    DIFF_HEADS * 2 * DIFF_D, DIFF_HEADS * 2 * DIFF_D, DIFF_HEADS * DIFF_DV,
    N_BRANCH * D_MODEL,
)
W_IN_COLS = sum(IN_SIZES)

kernel_name = 'hybrid_bidir_retention_gdn_diffattn_block'


def _rms(x, w=None):
    xf = x.astype(jnp.float32)
    y = xf * lax.rsqrt(jnp.mean(xf * xf, axis=-1, keepdims=True) + EPS)
    if w is not None:
        y = y * w.astype(jnp.float32)
    return y.astype(x.dtype)


def _l2norm(x):
    xf = x.astype(jnp.float32)
    return xf * lax.rsqrt(jnp.sum(xf * xf, axis=-1, keepdims=True) + EPS)


def _rope_tables(positions, dim):
    inv = 1.0 / (ROPE_THETA ** (jnp.arange(0, dim, 2, dtype=jnp.float32) / dim))
    ang = positions.astype(jnp.float32)[..., None] * inv
    ang = jnp.concatenate([ang, ang], axis=-1)[:, :, None, :]
    return jnp.cos(ang), jnp.sin(ang)


def _apply_rope(x, cos, sin):
    half = x.shape[-1] // 2
    rot = jnp.concatenate([-x[..., half:], x[..., :half]], axis=-1)
    return (x.astype(jnp.float32) * cos + rot.astype(jnp.float32) * sin).astype(x.dtype)


def _split_cols(proj):
    offs = np.cumsum(IN_SIZES)[:-1].tolist()
    return jnp.split(proj, offs, axis=-1)


def _swiglu(h, w_gate, w_up, w_down):
    return (jax.nn.silu(h @ w_gate) * (h @ w_up)) @ w_down


def _retention_one_dir(q, k, v, log_gamma, include_diag):
    b, h, s, dk = q.shape
    dv = v.shape[-1]
    c = RET_CHUNK
    n = s // c
    idx = jnp.arange(c, dtype=jnp.float32)
    dist = idx[:, None] - idx[None, :]
    mask = (dist >= 0) if include_diag else (dist > 0)
    intra_decay = jnp.where(mask, jnp.exp(log_gamma[:, None, None] * jnp.where(mask, dist, 0.0)), 0.0)
    q_decay = jnp.exp(log_gamma[:, None] * (idx + 1.0))
    k_decay = jnp.exp(log_gamma[:, None] * (c - 1.0 - idx))
    chunk_decay = jnp.exp(log_gamma * c)[None, :, None, None]
    qc = q.reshape(b, h, n, c, dk)
    kc = k.reshape(b, h, n, c, dk)
    vc = v.reshape(b, h, n, c, dv)
    scores = jnp.einsum('bhncd,bhnmd->bhncm', qc, kc) * intra_decay[None, :, None]
    intra = jnp.einsum('bhncm,bhnme->bhnce', scores, vc)
    kv = jnp.einsum('bhncd,hc,bhnce->nbhde', kc, k_decay, vc)

    def step(state, kv_n):
        return chunk_decay * state + kv_n, state

    _, prev = lax.scan(step, jnp.zeros((b, h, dk, dv), jnp.float32), kv)
    cross = jnp.einsum('bhncd,hc,nbhde->bhnce', qc, q_decay, prev)
    return (intra + cross).reshape(b, h, s, dv)


def _retention_branch(q, k, v, g, cos, sin):
    b, s, _ = q.shape
    q = _apply_rope(q.reshape(b, s, RET_HEADS, RET_DK), cos, sin) * (RET_DK ** -0.5)
    k = _apply_rope(k.reshape(b, s, RET_HEADS, RET_DK), cos, sin)
    v = v.reshape(b, s, RET_HEADS, RET_DV)
    q, k, v = (t.transpose(0, 2, 1, 3).astype(jnp.float32) for t in (q, k, v))
    log_gamma = jnp.log(1.0 - 2.0 ** (-5.0 - jnp.arange(RET_HEADS, dtype=jnp.float32)))
    fwd = _retention_one_dir(q, k, v, log_gamma, True)
    bwd = jnp.flip(_retention_one_dir(jnp.flip(q, 2), jnp.flip(k, 2), jnp.flip(v, 2), log_gamma, False), 2)
    o = _rms(fwd + bwd).transpose(0, 2, 1, 3).reshape(b, s, RET_HEADS * RET_DV)
    return o.astype(g.dtype) * jax.nn.silu(g)


def _centred_depthwise_conv(x, w):
    c = x.shape[-1]
    return lax.conv_general_dilated(
        x, w[:, None, :].astype(x.dtype), window_strides=(1,),
        padding=[(CONV_WIDTH // 2, CONV_WIDTH // 2)],
        dimension_numbers=('NWC', 'WIO', 'NWC'), feature_group_count=c)


def _gated_delta_one_dir(q, k, v, g, beta):
    b, h, s, dk = q.shape
    dv = v.shape[-1]
    c = GDN_CHUNK
    n = s // c
    qc = q.reshape(b, h, n, c, dk)
    kc = k.reshape(b, h, n, c, dk)
    vc = v.reshape(b, h, n, c, dv)
    gc = jnp.cumsum(g.reshape(b, h, n, c), axis=-1)
    bc = beta.reshape(b, h, n, c, 1)
    causal = jnp.tril(jnp.ones((c, c), dtype=bool))
    strict = jnp.tril(jnp.ones((c, c), dtype=bool), -1)
    gdiff = gc[..., :, None] - gc[..., None, :]
    decay = jnp.where(causal, jnp.exp(jnp.where(causal, gdiff, 0.0)), 0.0)
    kb = kc * bc
    a_mat = jnp.where(strict, jnp.einsum('bhnid,bhnjd->bhnij', kb, kc) * decay, 0.0)
    rhs = jnp.concatenate([vc * bc, kb * jnp.exp(gc)[..., None]], axis=-1)
    sol = lax.linalg.triangular_solve(a_mat, rhs, left_side=True, lower=True, unit_diagonal=True)
    u, w = sol[..., :dv], sol[..., dv:]
    qk = jnp.einsum('bhnid,bhnjd->bhnij', qc, kc) * decay
    q_in = qc * jnp.exp(gc)[..., None]
    k_out = kc * jnp.exp(gc[..., -1:] - gc)[..., None]
    last = jnp.exp(gc[..., -1])
    xs = tuple(jnp.moveaxis(t, 2, 0) for t in (u, w, qk, q_in, k_out, last))

    def step(state, xs_n):
        u_n, w_n, qk_n, qin_n, kout_n, last_n = xs_n
        v_new = u_n - jnp.einsum('bhck,bhkv->bhcv', w_n, state)
        o_n = jnp.einsum('bhck,bhkv->bhcv', qin_n, state) + jnp.einsum('bhij,bhjv->bhiv', qk_n, v_new)
        state = state * last_n[..., None, None] + jnp.einsum('bhck,bhcv->bhkv', kout_n, v_new)
        return state, o_n

    _, o = lax.scan(step, jnp.zeros((b, h, dk, dv), jnp.float32), xs)
    return jnp.moveaxis(o, 0, 2).reshape(b, h, s, dv)


def _gdn_branch(qkv, z, a, bt, conv_w, a_log, dt_bias, norm_w):
    b, s, _ = qkv.shape
    qkv = jax.nn.silu(_centred_depthwise_conv(qkv, conv_w))
    q, k, v = jnp.split(qkv, [GDN_HEADS * GDN_DK, 2 * GDN_HEADS * GDN_DK], axis=-1)
    q = _l2norm(q.reshape(b, s, GDN_HEADS, GDN_DK)) * (GDN_DK ** -0.5)
    k = _l2norm(k.reshape(b, s, GDN_HEADS, GDN_DK))
    v = v.reshape(b, s, GDN_HEADS, GDN_DV).astype(jnp.float32)
    q, k, v = (t.transpose(0, 2, 1, 3) for t in (q, k, v))
    a = a.reshape(b, s, 2, GDN_HEADS).astype(jnp.float32)
    bt = bt.reshape(b, s, 2, GDN_HEADS).astype(jnp.float32)
    g = -jnp.exp(a_log.astype(jnp.float32)) * jax.nn.softplus(a + dt_bias.astype(jnp.float32))
    beta = jax.nn.sigmoid(bt)
    g = g.transpose(2, 0, 3, 1)
    beta = beta.transpose(2, 0, 3, 1)
    fwd = _gated_delta_one_dir(q, k, v, g[0], beta[0])
    bwd = jnp.flip(_gated_delta_one_dir(jnp.flip(q, 2), jnp.flip(k, 2), jnp.flip(v, 2),
                                        jnp.flip(g[1], 2), jnp.flip(beta[1], 2)), 2)
    o = (fwd + bwd).transpose(0, 2, 1, 3)
    o = _rms(o, norm_w).astype(z.dtype) * jax.nn.silu(z.reshape(b, s, GDN_HEADS, GDN_DV))
    return o.reshape(b, s, GDN_HEADS * GDN_DV)


def _diff_branch(q, k, v, cos, sin, lam_params, subln_w, lambda_init):
    b, s, _ = q.shape
    q = _apply_rope(q.reshape(b, s, 2 * DIFF_HEADS, DIFF_D), cos, sin) * (DIFF_D ** -0.5)
    k = _apply_rope(k.reshape(b, s, 2 * DIFF_HEADS, DIFF_D), cos, sin)
    q = q.reshape(b, s, DIFF_HEADS, 2, DIFF_D).transpose(0, 2, 3, 1, 4)
    k = k.reshape(b, s, DIFF_HEADS, 2, DIFF_D).transpose(0, 2, 3, 1, 4)
    v = v.reshape(b, s, DIFF_HEADS, DIFF_DV).transpose(0, 2, 1, 3)
    lp = lam_params.astype(jnp.float32)
    lam = (jnp.exp(jnp.sum(lp[0] * lp[1])) - jnp.exp(jnp.sum(lp[2] * lp[3])) + lambda_init).astype(v.dtype)
    n_blk = s // Q_BLOCK
    q_blocks = jnp.moveaxis(q.reshape(b, DIFF_HEADS, 2, n_blk, Q_BLOCK, DIFF_D), 3, 0)

    def attend(qb):
        scores = jnp.einsum('bhtqd,bhtkd->bhtqk', qb, k).astype(jnp.float32)
        probs = jax.nn.softmax(scores, axis=-1).astype(v.dtype)
        o = jnp.einsum('bhtqk,bhke->bhtqe', probs, v)
        return o[:, :, 0] - lam * o[:, :, 1]

    o = lax.map(attend, q_blocks)
    o = o.transpose(1, 0, 3, 2, 4).reshape(b, s, DIFF_HEADS, DIFF_DV)
    o = _rms(o, subln_w) * (1.0 - lambda_init)
    return o.reshape(b, s, DIFF_HEADS * DIFF_DV)


def _token_mix(h, cos, sin, w_in, conv_w, a_log, dt_bias, gdn_norm_w,
               diff_lambda, diff_subln_w, w_branch, w_out, lambda_init):
    b, s, _ = h.shape
    (rq, rk, rv, rg, gqkv, gz, ga, gb, dq, dk_, dv_, gate) = _split_cols(h @ w_in)
    y_ret = _retention_branch(rq, rk, rv, rg, cos, sin)
    y_gdn = _gdn_branch(gqkv, gz, ga, gb, conv_w, a_log, dt_bias, gdn_norm_w)
    y_diff = _diff_branch(dq, dk_, dv_, cos, sin, diff_lambda, diff_subln_w, lambda_init)
    gates = jax.nn.sigmoid(gate).reshape(b, s, N_BRANCH, D_MODEL)
    merged = gates[:, :, 0] * (y_ret @ w_branch[0])
    merged = merged + gates[:, :, 1] * (y_gdn @ w_branch[1])
    merged = merged + gates[:, :, 2] * (y_diff @ w_branch[2])
    return merged @ w_out


def setup_inputs(seed: int = 0) -> dict:
    key = jax.random.key(seed)
    ks = jax.random.split(key, 24)
    f32 = jnp.float32

    def nrm(k, shape, scale):
        return jax.random.normal(k, shape, f32) * scale

    def gain(k, shape):
        return 1.0 + 0.01 * jax.random.normal(k, shape, f32)

    x = nrm(ks[0], (BATCH, SEQ, D_MODEL), 1.0)
    p = nrm(ks[1], (DEPTH, BATCH, SEQ, PLE_DIM), 1.0)
    positions = (jnp.arange(SEQ, dtype=jnp.int32)[None, :]
                 + jax.random.randint(ks[2], (BATCH, 1), 0, SEQ, dtype=jnp.int32))
    ln_ffn = gain(ks[3], (DEPTH, 2, D_MODEL))
    ffn_w_gate = nrm(ks[4], (DEPTH, 2, D_MODEL, D_FF), D_MODEL ** -0.5)
    ffn_w_up = nrm(ks[5], (DEPTH, 2, D_MODEL, D_FF), D_MODEL ** -0.5)
    ffn_w_down = nrm(ks[6], (DEPTH, 2, D_FF, D_MODEL), D_FF ** -0.5)
    ln_mix = gain(ks[7], (DEPTH, D_MODEL))
    w_in = nrm(ks[8], (DEPTH, D_MODEL, W_IN_COLS), D_MODEL ** -0.5)
    conv_w = nrm(ks[9], (DEPTH, CONV_WIDTH, GDN_HEADS * (2 * GDN_DK + GDN_DV)), CONV_WIDTH ** -0.5)
    gdn_a_log = jnp.log(jax.random.uniform(ks[10], (DEPTH, 2, GDN_HEADS), f32, 1.0, 16.0))
    dt = jnp.exp(jax.random.uniform(ks[11], (DEPTH, 2, GDN_HEADS), f32, math.log(1e-3), math.log(1e-1)))
    gdn_dt_bias = dt + jnp.log(-jnp.expm1(-dt))
    gdn_norm_w = gain(ks[12], (DEPTH, GDN_DV))
    diff_lambda = nrm(ks[13], (DEPTH, 4, DIFF_D), 0.1)
    diff_subln_w = gain(ks[14], (DEPTH, DIFF_DV))
    w_branch = nrm(ks[15], (DEPTH, N_BRANCH, BRANCH_WIDTH, D_MODEL), BRANCH_WIDTH ** -0.5)
    w_out = nrm(ks[16], (DEPTH, D_MODEL, D_MODEL), D_MODEL ** -0.5)
    ln_ple = gain(ks[17], (DEPTH, D_MODEL))
    w_ple_gate = nrm(ks[18], (DEPTH, D_MODEL, D_MODEL), D_MODEL ** -0.5)
    w_ple_proj = nrm(ks[19], (DEPTH, PLE_DIM, D_MODEL), PLE_DIM ** -0.5)
    final_norm = gain(ks[20], (D_MODEL,))
    return {'x': x, 'p': p, 'positions': positions, 'ln_ffn': ln_ffn,
            'ffn_w_gate': ffn_w_gate, 'ffn_w_up': ffn_w_up, 'ffn_w_down': ffn_w_down,
            'ln_mix': ln_mix, 'w_in': w_in, 'conv_w': conv_w, 'gdn_a_log': gdn_a_log,
            'gdn_dt_bias': gdn_dt_bias, 'gdn_norm_w': gdn_norm_w, 'diff_lambda': diff_lambda,
            'diff_subln_w': diff_subln_w, 'w_branch': w_branch, 'w_out': w_out,
            'ln_ple': ln_ple, 'w_ple_gate': w_ple_gate, 'w_ple_proj': w_ple_proj,
            'final_norm': final_norm}


def reference(x, p, positions, ln_ffn, ffn_w_gate, ffn_w_up, ffn_w_down, ln_mix, w_in,
              conv_w, gdn_a_log, gdn_dt_bias, gdn_norm_w, diff_lambda, diff_subln_w,
              w_branch, w_out, ln_ple, w_ple_gate, w_ple_proj, final_norm):
    cos, sin = _rope_tables(positions, RET_DK)
    for i in range(DEPTH):
        lambda_init = 0.8 - 0.6 * math.exp(-0.3 * i)
        h = _rms(x, ln_ffn[i, 0])
        x = x + 0.5 * _swiglu(h, ffn_w_gate[i, 0], ffn_w_up[i, 0], ffn_w_down[i, 0])
        h = _rms(x, ln_mix[i])
        x = x + _token_mix(h, cos, sin, w_in[i], conv_w[i], gdn_a_log[i], gdn_dt_bias[i],
                           gdn_norm_w[i], diff_lambda[i], diff_subln_w[i], w_branch[i],
                           w_out[i], lambda_init)
        h = _rms(x, ln_ffn[i, 1])
        x = x + 0.5 * _swiglu(h, ffn_w_gate[i, 1], ffn_w_up[i, 1], ffn_w_down[i, 1])
        h = _rms(x, ln_ple[i])
        x = x + jax.nn.sigmoid(h @ w_ple_gate[i]) * (p[i] @ w_ple_proj[i])
    return _rms(x, final_norm)
```

```python
import math
import numpy as np
import concourse.bass as bass
import concourse.mybir as mybir
from concourse.bass_utils import run_bass_kernel_spmd

F32 = mybir.dt.float32
BF16 = mybir.dt.bfloat16
I32 = mybir.dt.int32
ALU = mybir.AluOpType
AF = mybir.ActivationFunctionType

D = 4096
DFF = 8192
SEQ = 4096
NB = 2
DEPTH = 2
PLE = 256
EPS = 1e-6
NCORES = 8
SEM_LIM = 20000


class Sched:
    def __init__(self, nc):
        self.nc = nc
        self.E = dict(pe=nc.tensor, dve=nc.vector, act=nc.scalar, pool=nc.gpsimd, sp=nc.sync)
        self.ops = []
        self.esems = {e: [] for e in self.E}
        self.ecount = {e: 0 for e in self.E}
        self.RING = 12
        self.dsems = {}
        self.dcount = {e: 0 for e in self.E}
        self.seen = {e: {} for e in self.E}
        self.nsem = 0
        self.last_w = {}
        self.readers = {}
        self.tok = {}
        self.uid = 0

    def _newsem(self, tag):
        self.nsem += 1
        return self.nc.semaphore(f"{tag}{self.nsem}").__enter__()

    def op(self, eng, fn, reads=(), writes=(), dma=False):
        self.ops.append((eng, fn, tuple(reads), tuple(writes), dma))

    def emit(self):
        ops = self.ops
        self.ops = []
        n = len(ops)
        base = self.uid
        deps = []
        needs = [False] * n
        last_w, readers = self.last_w, self.readers
        for i, (eng, fn, R, W, dma) in enumerate(ops):
            u = base + i
            d = set()
            for k in R:
                if k in last_w:
                    d.add(last_w[k])
            for k in W:
                if k in last_w:
                    d.add(last_w[k])
                rs = readers.get(k)
                if rs:
                    d.update(rs)
            d.discard(u)
            dd = []
            for j in d:
                if j >= base:
                    je, _, _, _, jd = ops[j - base]
                    if je == eng and eng == 'pe' and not jd:
                        continue
                    needs[j - base] = True
                    dd.append(j)
                else:
                    t = self.tok.get(j)
                    if t is not None:
                        if t[3] == eng and eng == 'pe' and not t[4]:
                            continue
                        dd.append(j)
            deps.append(dd)
            for k in R:
                readers.setdefault(k, []).append(u)
            for k in W:
                last_w[k] = u
                readers[k] = []
        lastidx = {}
        for i, (eng, fn, R, W, dma) in enumerate(ops):
            lastidx[eng] = i
        for eng, i in lastidx.items():
            needs[i] = True
        for i, (eng, fn, R, W, dma) in enumerate(ops):
            u = base + i
            e = self.E[eng]
            seen = self.seen[eng]
            for j in deps[i]:
                t = self.tok[j]
                name, sem, val = t[0], t[1], t[2]
                if seen.get(name, 0) >= val:
                    continue
                e.wait_ge(sem, val)
                seen[name] = val
            if dma:
                ring = self.dsems.setdefault(eng, [])
                c = self.dcount[eng]
                slot = c % self.RING
                if slot >= len(ring):
                    ring.append([self._newsem(f"d{eng}"), 0])
                sem, uses = ring[slot]
                name = f"d{eng}{slot}"
                if uses > 0 and seen.get(name, 0) < 16 * uses:
                    e.wait_ge(sem, 16 * uses)
                    seen[name] = 16 * uses
                inst = fn()
                inst.then_inc(sem, 16)
                ring[slot][1] = uses + 1
                self.dcount[eng] = c + 1
                self.tok[u] = (name, sem, 16 * (uses + 1), eng, True)
            else:
                inst = fn()
                if needs[i]:
                    c = self.ecount[eng]
                    k = c // SEM_LIM
                    sl = self.esems[eng]
                    if k >= len(sl):
                        sl.append(self._newsem(f"e{eng}"))
                    sem = sl[k]
                    inst.then_inc(sem, 1)
                    self.ecount[eng] = c + 1
                    self.tok[u] = (f"e{eng}{k}", sem, c % SEM_LIM + 1, eng, False)
        self.uid = base + n
        live = set(last_w.values())
        for rs in readers.values():
            live.update(rs)
        self.tok = {k: v for k, v in self.tok.items() if k in live or k >= self.uid - 64}

    def barrier(self):
        self.emit()
        waits = []
        for eng, ring in self.dsems.items():
            for slot, (sem, uses) in enumerate(ring):
                if uses > 0:
                    waits.append((f"d{eng}{slot}", sem, 16 * uses))
        for eng, sl in self.esems.items():
            c = self.ecount[eng]
            if c > 0:
                k = (c - 1) // SEM_LIM
                waits.append((f"e{eng}{k}", sl[k], (c - 1) % SEM_LIM + 1))
        for eng, e in self.E.items():
            seen = self.seen[eng]
            for name, sem, val in waits:
                if seen.get(name, 0) < val:
                    e.wait_ge(sem, val)
                    seen[name] = val
        self.last_w.clear()
        self.readers.clear()
        self.tok.clear()

    def finish(self):
        self.emit()
        sp = self.E['sp']
        for eng, ring in self.dsems.items():
            for slot, (sem, uses) in enumerate(ring):
                if uses > 0:
                    sp.wait_ge(sem, 16 * uses)
        for eng, sl in self.esems.items():
            c = self.ecount[eng]
            if c > 0:
                k = (c - 1) // SEM_LIM
                sp.wait_ge(sl[k], (c - 1) % SEM_LIM + 1)


class Ctx:
    def __init__(self, nc):
        self.nc = nc
        self.s = Sched(nc)
        self.nbuf = 0
        self.live = []
        self.stack = []
        self.ps = [self.sb_psum(f"ps{i}") for i in range(8)]
        self.psrr = 0

    def sb(self, shape, dt, name=None):
        self.nbuf += 1
        cm = self.nc.sbuf_tensor((name or "b") + f"_{self.nbuf}", list(shape), dt)
        t = cm.__enter__()
        self.live.append(cm)
        return t

    def push(self):
        self.stack.append(len(self.live))

    def pop(self):
        self.s.barrier()
        n = self.stack.pop()
        while len(self.live) > n:
            self.live.pop().__exit__(None, None, None)

    def sb_psum(self, name):
        return self.nc.psum_tensor(name, [128, 512], F32).__enter__()


def dram_in(nc, name, shape, dt=F32):
    return nc.dram_tensor(name, list(shape), dt, kind="ExternalInput").ap()


def dram_out(nc, name, shape, dt=F32):
    return nc.dram_tensor(name, list(shape), dt, kind="ExternalOutput").ap()


def dram_tmp(nc, name, shape, dt=F32):
    return nc.dram_tensor(name, list(shape), dt, kind="Internal").ap()


class Blocks:
    def __init__(self, cx, ident_d):
        self.cx = cx
        nc, s = cx.nc, cx.s
        self.nc, self.s = nc, s
        self.lnT = cx.sb([128, 32], F32, "lnT")
        self.stat = cx.sb([128, 8], F32, "stat")
        self.identf = cx.sb([128, 128], F32, "identf")
        self.identb = cx.sb([128, 128], BF16, "identb")
        if getattr(cx, 'want_identf', False):
            s.op('sp', lambda: nc.sync.dma_start(out=self.identf[:], in_=ident_d[:, :]), writes=['identf'], dma=True)
        s.op('pool', lambda: nc.gpsimd.dma_start(out=self.identb[:], in_=ident_d[:, :]), writes=['identb'], dma=True)
        self.slabs = [cx.sb([128, 32, 512], BF16, f"slab{i}") for i in range(2)]
        self.slab_i = 0
        self.hT = cx.sb([128, 32, 512], BF16, "hT")
        self.xin = cx.sb([128, D], F32, "xin")
        self.xn = cx.sb([128, D], BF16, "xn")
        self.ps_i = 0
        self.LNW = 32
        self.tmp_i = 0
        self.tmpf = [cx.sb([128, 512], F32, f"tmpf{i}") for i in range(4)]
        self.xres = [cx.sb([128, 512], F32, f"xres{i}") for i in range(2)]
        self.xres_i = 0
        self.xo = [cx.sb([128, 512], F32, f"xo{i}") for i in range(2)]
        self.xo_i = 0

    def next_ps(self, lo=0, hi=8):
        i = lo + self.ps_i % (hi - lo)
        self.ps_i += 1
        return i

    def next_tmp(self):
        i = self.tmp_i % len(self.tmpf)
        self.tmp_i += 1
        return i

    def load_slab(self, w_ap, kc, ncols):
        nc, s = self.nc, self.s
        i = self.slab_i % len(self.slabs)
        self.slab_i += 1
        slab = self.slabs[i]
        src = w_ap.rearrange("(k p) n -> p k n", p=128)
        step = 8
        for k0 in range(0, kc, step):
            k1 = min(kc, k0 + step)
            s.op('pool', (lambda k0=k0, k1=k1: nc.gpsimd.dma_start(out=slab[:, k0:k1, 0:ncols], in_=src[:, k0:k1, :])),
                 writes=[('slab', i, k) for k in range(k0, k1)], dma=True)
        return i

    def load_ln(self, ln_ap):
        nc, s = self.nc, self.s
        s.op('sp', lambda: nc.sync.dma_start(out=self.lnT[:], in_=ln_ap), writes=['lnT'], dma=True)

    def norm_group(self, x_rows_ap, g, hT=None, rk=None):
        nc, s = self.nc, self.s
        hT = hT if hT is not None else self.hT
        xin, xn, stat = self.xin, self.xn, self.stat
        s.op('sp', lambda: nc.sync.dma_start(out=xin[:], in_=x_rows_ap), writes=['xin'],
             reads=([('dr', rk[0], rk[1], c) for c in range(8)] if rk else []), dma=True)
        s.op('act', lambda: nc.scalar.activation(out=xn[:], in_=xin[:], func=AF.Square, accum_out=stat[:, 0:1]),
             reads=['xin'], writes=['xn', 'stat0'])
        s.op('dve', lambda: nc.vector.tensor_scalar(out=stat[:, 1:2], in0=stat[:, 0:1], scalar1=1.0 / D, scalar2=EPS,
                                                    op0=ALU.mult, op1=ALU.add), reads=['stat0'], writes=['stat1'])
        s.op('act', lambda: nc.scalar.activation(out=stat[:, 2:3], in_=stat[:, 1:2], func=AF.Sqrt),
             reads=['stat1'], writes=['stat2'])
        s.op('dve', lambda: nc.vector.reciprocal(out=stat[:, 3:4], in_=stat[:, 2:3]), reads=['stat2'], writes=['stat3'])
        s.op('dve', lambda: nc.vector.tensor_scalar(out=xn[:], in0=xin[:], scalar1=stat[:, 3:4], scalar2=None,
                                                    op0=ALU.mult), reads=['xin', 'stat3'], writes=['xn'])
        for c4 in range(getattr(self, 'NR', 8)):
            pi = self.next_ps(6, 8)
            psb = self.cx.ps[pi][:].bitcast(BF16)
            for j in range(4):
                c = c4 * 4 + j
                s.op('pe', (lambda c=c, j=j, psb=psb: nc.tensor.transpose(out=psb[:, j * 128:(j + 1) * 128],
                                                                         in_=xn[:, c * 128:(c + 1) * 128],
                                                                         identity=self.identb[:])),
                     reads=['xn', 'identb'], writes=[('ps', pi)])
            for j in range(4):
                c = c4 * 4 + j
                if c4 % 2 == 0:
                    s.op('dve', (lambda c=c, j=j, psb=psb: nc.vector.tensor_scalar(
                        out=hT[:, c, g * 128:(g + 1) * 128], in0=psb[:, j * 128:(j + 1) * 128],
                        scalar1=self.lnT[:, (c % self.LNW):(c % self.LNW) + 1], scalar2=None, op0=ALU.mult)),
                         reads=[('ps', pi), 'lnT'], writes=[('hT', c, g)])
                else:
                    s.op('act', (lambda c=c, j=j, psb=psb: nc.scalar.activation(
                        out=hT[:, c, g * 128:(g + 1) * 128], in_=psb[:, j * 128:(j + 1) * 128],
                        func=AF.Copy, scale=self.lnT[:, (c % self.LNW):(c % self.LNW) + 1])),
                         reads=[('ps', pi), 'lnT'], writes=[('hT', c, g)])

    def hT_keys(self, kc=32, ng=4):
        return [('hT', c, g) for c in range(kc) for g in range(ng)]

    def ffn(self, x_ap, out_ap, ln_ap, wg, wu, wd, gT, xk=None, ok=None):
        nc, s = self.nc, self.s
        self.load_ln(ln_ap)
        for g in range(4):
            self.norm_group(x_ap[g * 128:(g + 1) * 128, :], g, rk=(xk[0], xk[1] * 4 + g) if xk else None)
        hk = self.hT_keys()
        for sl in range(DFF // 512):
            ig = self.load_slab(wg[:, sl * 512:(sl + 1) * 512], 32, 512)
            iu = self.load_slab(wu[:, sl * 512:(sl + 1) * 512], 32, 512)
            for c in range(4):
                pg, pu = self.next_ps(0, 6), self.next_ps(0, 6)
                for (pi, si) in ((pg, ig), (pu, iu)):
                    slab = self.slabs[si]
                    for k in range(32):
                        s.op('pe', (lambda k=k, c=c, pi=pi, slab=slab: nc.tensor.matmul(
                            self.cx.ps[pi][:], slab[:, k, c * 128:(c + 1) * 128], self.hT[:, k, :],
                            start=(k == 0), stop=(k == 31))),
                             reads=[('slab', si, k)] + ([('hT', k, g) for g in range(4)]),
                             writes=[('ps', pi)])
                ti = self.next_tmp()
                tmp = self.tmpf[ti]
                s.op('act', (lambda pg=pg, tmp=tmp: nc.scalar.activation(out=tmp[:], in_=self.cx.ps[pg][:], func=AF.Silu)),
                     reads=[('ps', pg)], writes=[('tmpf', ti)])
                fc = sl * 4 + c
                s.op('dve', (lambda pu=pu, tmp=tmp, fc=fc: nc.vector.tensor_tensor(
                    out=gT[:, fc, :], in0=tmp[:], in1=self.cx.ps[pu][:], op=ALU.mult)),
                     reads=[('ps', pu), ('tmpf', ti)], writes=[('gT', fc)])
        nF = DFF // 128
        npiece = max(1, nF // 32)
        kcp = nF // npiece
        for dsl in range(D // 512):
            pis = [self.next_ps(0, 6) for _ in range(4)]
            for piece in range(npiece):
                si = self.load_slab(wd[piece * kcp * 128:(piece + 1) * kcp * 128, dsl * 512:(dsl + 1) * 512], kcp, 512)
                slab = self.slabs[si]
                for g in range(4):
                    for k in range(kcp):
                        fk = piece * kcp + k
                        s.op('pe', (lambda g=g, k=k, fk=fk, slab=slab, pi=pis[g]: nc.tensor.matmul(
                            self.cx.ps[pi][:], gT[:, fk, g * 128:(g + 1) * 128], slab[:, k, :],
                            start=(fk == 0), stop=(fk == nF - 1))),
                             reads=[('slab', si, k), ('gT', fk)], writes=[('ps', pis[g])])
            for g in range(4):
                self.residual_out(x_ap[g * 128:(g + 1) * 128, dsl * 512:(dsl + 1) * 512],
                                  out_ap[g * 128:(g + 1) * 128, dsl * 512:(dsl + 1) * 512], pis[g], 0.5,
                                  sk=('dr', xk[0], xk[1] * 4 + g, dsl) if xk else None,
                                  dk=('dr', ok[0], ok[1] * 4 + g, dsl) if ok else None)

    def mix_merge(self, x_ap, yT_ap, ln_ap, wgate, wbr, gT, xk=None):
        nc, s = self.nc, self.s
        self.load_ln(ln_ap)
        for g in range(4):
            self.norm_group(x_ap[g * 128:(g + 1) * 128, :], g, rk=(xk[0], xk[1] * 4 + g) if xk else None)
        ybufs = [gT[:, 32:48, :], gT[:, 48:64, :]]
        macc = [self.xin[:, c * 512:(c + 1) * 512] for c in range(4)]
        yb_i = 0
        for dsl in range(D // 512):
            for b in range(3):
                yi = yb_i % 2
                yb_i += 1
                ybuf = ybufs[yi]
                if dsl == 0 or True:
                    src = yT_ap[b * 2048:(b + 1) * 2048, :].rearrange("(k p) n -> p k n", p=128)
                    for k0 in (0, 8):
                        s.op('pool', (lambda k0=k0, ybuf=ybuf, src=src: nc.gpsimd.dma_start(
                            out=ybuf[:, k0:k0 + 8, :], in_=src[:, k0:k0 + 8, :])),
                             writes=[('gT', 32 + 16 * yi + k) for k in range(k0, k0 + 8)], dma=True)
                ig = self.load_slab(wgate[:, b * D + dsl * 512: b * D + (dsl + 1) * 512], 32, 512)
                ib = self.load_slab(wbr[b, :, dsl * 512:(dsl + 1) * 512], 16, 512)
                sg, sbr = self.slabs[ig], self.slabs[ib]
                for c in range(4):
                    pg, py = self.next_ps(0, 6), self.next_ps(0, 6)
                    for k in range(32):
                        s.op('pe', (lambda k=k, c=c, pg=pg, sg=sg: nc.tensor.matmul(
                            self.cx.ps[pg][:], sg[:, k, c * 128:(c + 1) * 128], self.hT[:, k, :],
                            start=(k == 0), stop=(k == 31))),
                             reads=[('slab', ig, k)] + [('hT', k, g) for g in range(4)], writes=[('ps', pg)])
                    for k in range(16):
                        s.op('pe', (lambda k=k, c=c, py=py, sbr=sbr, ybuf=ybuf: nc.tensor.matmul(
                            self.cx.ps[py][:], sbr[:, k, c * 128:(c + 1) * 128], ybuf[:, k, :],
                            start=(k == 0), stop=(k == 15))),
                             reads=[('slab', ib, k), ('gT', 32 + 16 * yi + k)], writes=[('ps', py)])
                    ti = self.next_tmp()
                    tmp = self.tmpf[ti]
                    s.op('act', (lambda pg=pg, tmp=tmp: nc.scalar.activation(out=tmp[:], in_=self.cx.ps[pg][:], func=AF.Sigmoid)),
                         reads=[('ps', pg)], writes=[('tmpf', ti)])
                    if b == 0:
                        s.op('dve', (lambda py=py, tmp=tmp, c=c: nc.vector.tensor_tensor(
                            out=macc[c], in0=tmp[:], in1=self.cx.ps[py][:], op=ALU.mult)),
                             reads=[('ps', py), ('tmpf', ti)], writes=['xin'])
                    else:
                        s.op('dve', (lambda py=py, tmp=tmp: nc.vector.tensor_tensor(
                            out=tmp[:], in0=tmp[:], in1=self.cx.ps[py][:], op=ALU.mult)),
                             reads=[('ps', py), ('tmpf', ti)], writes=[('tmpf', ti)])
                        s.op('dve', (lambda tmp=tmp, c=c: nc.vector.tensor_tensor(
                            out=macc[c], in0=macc[c], in1=tmp[:], op=ALU.add)),
                             reads=[('tmpf', ti), 'xin'], writes=['xin'])
                    if b == 2:
                        mc = dsl * 4 + c
                        s.op('act', (lambda c=c, mc=mc: nc.scalar.copy(out=gT[:, mc, :], in_=macc[c])),
                             reads=['xin'], writes=[('gT', mc)])

    def wo_proj(self, x_ap, out_ap, wo, gT, xk=None, ok=None):
        nc, s = self.nc, self.s
        for dsl in range(D // 512):
            si = self.load_slab(wo[:, dsl * 512:(dsl + 1) * 512], 32, 512)
            slab = self.slabs[si]
            for g in range(4):
                pi = self.next_ps(0, 6)
                for k in range(32):
                    s.op('pe', (lambda g=g, k=k, slab=slab, pi=pi: nc.tensor.matmul(
                        self.cx.ps[pi][:], gT[:, k, g * 128:(g + 1) * 128], slab[:, k, :],
                        start=(k == 0), stop=(k == 31))),
                         reads=[('slab', si, k), ('gT', k)], writes=[('ps', pi)])
                self.residual_out(x_ap[g * 128:(g + 1) * 128, dsl * 512:(dsl + 1) * 512],
                                  out_ap[g * 128:(g + 1) * 128, dsl * 512:(dsl + 1) * 512], pi, 1.0,
                                  sk=('dr', xk[0], xk[1] * 4 + g, dsl) if xk else None,
                                  dk=('dr', ok[0], ok[1] * 4 + g, dsl) if ok else None)

    def ple(self, x_ap, out_ap, ln_ap, pT_ap, wpg, wpp, gT, xk=None, ok=None):
        nc, s = self.nc, self.s
        self.load_ln(ln_ap)
        for g in range(4):
            self.norm_group(x_ap[g * 128:(g + 1) * 128, :], g, rk=(xk[0], xk[1] * 4 + g) if xk else None)
        pT = gT[:, 32:34, :]
        s.op('pool', lambda: nc.gpsimd.dma_start(out=pT, in_=pT_ap.rearrange("(k p) n -> p k n", p=128)),
             writes=[('gT', 32), ('gT', 33)], dma=True)
        for dsl in range(D // 512):
            ig = self.load_slab(wpg[:, dsl * 512:(dsl + 1) * 512], 32, 512)
            ip = self.load_slab(wpp[:, dsl * 512:(dsl + 1) * 512], 2, 512)
            sg, sp_ = self.slabs[ig], self.slabs[ip]
            for g in range(4):
                pg, pp = self.next_ps(0, 6), self.next_ps(0, 6)
                for k in range(32):
                    s.op('pe', (lambda g=g, k=k, sg=sg, pg=pg: nc.tensor.matmul(
                        self.cx.ps[pg][:], self.hT[:, k, g * 128:(g + 1) * 128], sg[:, k, :],
                        start=(k == 0), stop=(k == 31))),
                         reads=[('slab', ig, k), ('hT', k, g)], writes=[('ps', pg)])
                for k in range(2):
                    s.op('pe', (lambda g=g, k=k, sp_=sp_, pp=pp: nc.tensor.matmul(
                        self.cx.ps[pp][:], pT[:, k, g * 128:(g + 1) * 128], sp_[:, k, :],
                        start=(k == 0), stop=(k == 1))),
                         reads=[('slab', ip, k), ('gT', 32 + k)], writes=[('ps', pp)])
                ti = self.next_tmp()
                tmp = self.tmpf[ti]
                s.op('act', (lambda pg=pg, tmp=tmp: nc.scalar.activation(out=tmp[:], in_=self.cx.ps[pg][:], func=AF.Sigmoid)),
                     reads=[('ps', pg)], writes=[('tmpf', ti)])
                s.op('dve', (lambda pp=pp, tmp=tmp: nc.vector.tensor_tensor(
                    out=tmp[:], in0=tmp[:], in1=self.cx.ps[pp][:], op=ALU.mult)),
                     reads=[('ps', pp), ('tmpf', ti)], writes=[('tmpf', ti)])
                ri = self.xres_i % 2
                self.xres_i += 1
                oi = self.xo_i % 2
                self.xo_i += 1
                xr, xo = self.xres[ri], self.xo[oi]
                xs = x_ap[g * 128:(g + 1) * 128, dsl * 512:(dsl + 1) * 512]
                ds = out_ap[g * 128:(g + 1) * 128, dsl * 512:(dsl + 1) * 512]
                s.op('sp', (lambda xr=xr, xs=xs: nc.sync.dma_start(out=xr[:], in_=xs)), writes=[('xres', ri)],
                     reads=([('dr', xk[0], xk[1] * 4 + g, dsl)] if xk else []), dma=True)
                s.op('dve', (lambda xr=xr, xo=xo, tmp=tmp: nc.vector.tensor_tensor(out=xo[:], in0=tmp[:], in1=xr[:], op=ALU.add)),
                     reads=[('tmpf', ti), ('xres', ri)], writes=[('xo', oi)])
                s.op('sp', (lambda xo=xo, ds=ds: nc.sync.dma_start(out=ds, in_=xo[:])), reads=[('xo', oi)],
                     writes=([('dr', ok[0], ok[1] * 4 + g, dsl)] if ok else []), dma=True)

    def final_norm(self, x_ap, out_ap, fn_ap, gT, ntok, xname=None):
        nc, s = self.nc, self.s
        fnb = gT[:, 0:16, :].rearrange("p a b -> p (a b)").bitcast(F32)
        s.op('sp', lambda: nc.sync.dma_start(out=fnb, in_=fn_ap.partition_broadcast(128)),
             writes=[('gT', k) for k in range(16)], dma=True)
        xin, stat = self.xin, self.stat
        for g in range(ntok // 128):
            xs = x_ap[g * 128:(g + 1) * 128, :]
            ds = out_ap[g * 128:(g + 1) * 128, :]
            s.op('sp', (lambda xs=xs: nc.sync.dma_start(out=xin[:], in_=xs)), writes=['xin'],
                 reads=([('dr', xname, g, c) for c in range(8)] if xname else []), dma=True)
            s.op('act', lambda: nc.scalar.activation(out=self.xn[:], in_=xin[:], func=AF.Square, accum_out=stat[:, 0:1]),
                 reads=['xin'], writes=['xn', 'stat0'])
            s.op('dve', lambda: nc.vector.tensor_scalar(out=stat[:, 1:2], in0=stat[:, 0:1], scalar1=1.0 / D, scalar2=EPS,
                                                        op0=ALU.mult, op1=ALU.add), reads=['stat0'], writes=['stat1'])
            s.op('act', lambda: nc.scalar.activation(out=stat[:, 2:3], in_=stat[:, 1:2], func=AF.Sqrt),
                 reads=['stat1'], writes=['stat2'])
            s.op('dve', lambda: nc.vector.reciprocal(out=stat[:, 3:4], in_=stat[:, 2:3]), reads=['stat2'], writes=['stat3'])
            s.op('dve', lambda: nc.vector.scalar_tensor_tensor(out=xin[:], in0=xin[:], scalar=stat[:, 3:4], in1=fnb,
                                                               op0=ALU.mult, op1=ALU.mult),
                 reads=['xin', 'stat3'] + [('gT', k) for k in range(16)], writes=['xin'])
            s.op('sp', (lambda ds=ds: nc.sync.dma_start(out=ds, in_=xin[:])), reads=['xin'], dma=True)

    def residual_out(self, xsrc_ap, dst_ap, pi, scale, sk=None, dk=None):
        nc, s = self.nc, self.s
        ri = self.xres_i % 2
        self.xres_i += 1
        oi = self.xo_i % 2
        self.xo_i += 1
        xr, xo = self.xres[ri], self.xo[oi]
        s.op('sp', lambda: nc.sync.dma_start(out=xr[:], in_=xsrc_ap), writes=[('xres', ri)],
             reads=([sk] if sk else []), dma=True)
        s.op('dve', lambda: nc.vector.scalar_tensor_tensor(out=xo[:], in0=self.cx.ps[pi][:], scalar=scale, in1=xr[:],
                                                           op0=ALU.mult, op1=ALU.add),
             reads=[('ps', pi), ('xres', ri)], writes=[('xo', oi)])
        s.op('sp', lambda: nc.sync.dma_start(out=dst_ap, in_=xo[:]), reads=[('xo', oi)],
             writes=([dk] if dk else []), dma=True)


def build_ffn_prog(ntok):
    nc = bass.Bass("TRN2", target_bir_lowering=False)
    x = dram_in(nc, "x", [ntok, D])
    ln = dram_in(nc, "ln", [128, 32])
    wg = dram_in(nc, "wg", [D, DFF])
    wu = dram_in(nc, "wu", [D, DFF])
    wd = dram_in(nc, "wd", [DFF, D])
    ident = dram_in(nc, "ident", [128, 128])
    y = dram_out(nc, "y", [ntok, D])
    cx = Ctx(nc)
    b = Blocks(cx, ident)
    gT = cx.sb([128, 64, 512], BF16, "gT")
    for t in range(ntok // 512):
        b.ffn(x[t * 512:(t + 1) * 512, :], y[t * 512:(t + 1) * 512, :], ln, wg, wu, wd, gT)
    cx.s.finish()
    return nc


def build_l3_prog(ntok, last):
    nc = bass.Bass("TRN2", target_bir_lowering=False)
    x = dram_in(nc, "x", [ntok, D])
    yT = dram_in(nc, "yT", [3 * 2048, ntok])
    pT = dram_in(nc, "pT", [PLE, ntok])
    ident = dram_in(nc, "ident", [128, 128])
    ln_mix = dram_in(nc, "ln_mix", [128, 32])
    wgate = dram_in(nc, "wgate", [D, 3 * D])
    wbr = dram_in(nc, "wbr", [3, 2048, D])
    wo = dram_in(nc, "wo", [D, D])
    ln1 = dram_in(nc, "ln1", [128, 32])
    wg1 = dram_in(nc, "wg1", [D, DFF]); wu1 = dram_in(nc, "wu1", [D, DFF]); wd1 = dram_in(nc, "wd1", [DFF, D])
    ln_ple = dram_in(nc, "ln_ple", [128, 32])
    wpg = dram_in(nc, "wpg", [D, D]); wpp = dram_in(nc, "wpp", [PLE, D])
    if last:
        fn = dram_in(nc, "fn", [D])
    else:
        ln2 = dram_in(nc, "ln2", [128, 32])
        wg2 = dram_in(nc, "wg2", [D, DFF]); wu2 = dram_in(nc, "wu2", [D, DFF]); wd2 = dram_in(nc, "wd2", [DFF, D])
    y = dram_out(nc, "y", [ntok, D])
    x2 = dram_tmp(nc, "x2", [ntok, D]); x3 = dram_tmp(nc, "x3", [ntok, D]); x4 = dram_tmp(nc, "x4", [ntok, D])
    cx = Ctx(nc)
    b = Blocks(cx, ident)
    gT = cx.sb([128, 64, 512], BF16, "gT")
    for t in range(ntok // 512):
        sl = slice(t * 512, (t + 1) * 512)
        b.mix_merge(x[sl, :], yT[:, sl], ln_mix, wgate, wbr, gT, xk=('x', t))
        b.wo_proj(x[sl, :], x2[sl, :], wo, gT, xk=('x', t), ok=('x2', t))
        b.ffn(x2[sl, :], x3[sl, :], ln1, wg1, wu1, wd1, gT, xk=('x2', t), ok=('x3', t))
        b.ple(x3[sl, :], x4[sl, :], ln_ple, pT[:, sl], wpg, wpp, gT, xk=('x3', t), ok=('x4', t))
        if not last:
            b.ffn(x4[sl, :], y[sl, :], ln2, wg2, wu2, wd2, gT, xk=('x4', t), ok=('y', t))
    if last:
        b.final_norm(x4, y, fn, gT, ntok, xname='x4')
    cx.s.finish()
    return nc


def _pT(v):
    return np.ascontiguousarray(np.asarray(v, np.float32).reshape(-1, 128).T)


def _launch(nc, in_maps):
    res = run_bass_kernel_spmd(nc, in_maps, core_ids=list(range(NCORES)))
    return res.results


def _c(a):
    return np.ascontiguousarray(np.asarray(a, np.float32))


def kernel(**inputs):
    TS = NB * SEQ // NCORES
    x = _c(inputs["x"]).reshape(NB * SEQ, D)
    ident = np.eye(128, dtype=np.float32)
    pos = np.asarray(inputs["positions"]).astype(np.int32)
    nc1 = build_ffn_prog(TS)
    w = {"ln": _pT(inputs["ln_ffn"][0, 0]), "wg": _c(inputs["ffn_w_gate"][0, 0]), "wu": _c(inputs["ffn_w_up"][0, 0]),
         "wd": _c(inputs["ffn_w_down"][0, 0]), "ident": ident}
    res = _launch(nc1, [dict(w, x=x[c * TS:(c + 1) * TS]) for c in range(NCORES)])
    x1 = np.concatenate([np.asarray(r["y"], np.float32) for r in res], axis=0)
    del w
    for i in range(DEPTH):
        last = (i == DEPTH - 1)
        w_in = np.asarray(inputs["w_in"][i], np.float32)
        nc2 = build_l2_prog(SEQ, i)
        ln_mix = _pT(inputs["ln_mix"][i])
        groups = []
        for g in range(4):
            d = l2_consts(inputs, i, g)
            d["wsl"] = l2_weight_slabs(w_in, g)
            d["ln"] = ln_mix
            groups.append(d)
        in_maps = []
        for c in range(NCORES):
            b, g = c // 4, c % 4
            in_maps.append(dict(groups[g], x=x1[b * SEQ:(b + 1) * SEQ], pos=np.ascontiguousarray(pos[b])))
        res = _launch(nc2, in_maps)
        del groups, in_maps
        yfull = np.zeros((NB, SEQ, 3 * 2048), np.float32)
        for c in range(NCORES):
            b, g = c // 4, c % 4
            yc = np.asarray(res[c]["y"], np.float32)
            yfull[b, :, g * 512:(g + 1) * 512] = yc[:, 0:512]
            yfull[b, :, 2048 + g * 512:2048 + (g + 1) * 512] = yc[:, 512:1024]
            yfull[b, :, 4096 + g * 512:4096 + (g + 1) * 512] = yc[:, 1024:1536]
        nc3 = build_l3_prog(TS, last)
        w = {"ident": ident, "ln_mix": ln_mix, "wgate": np.ascontiguousarray(w_in[:, O_GATE:]),
             "wbr": _c(inputs["w_branch"][i]), "wo": _c(inputs["w_out"][i]),
             "ln1": _pT(inputs["ln_ffn"][i, 1]), "wg1": _c(inputs["ffn_w_gate"][i, 1]), "wu1": _c(inputs["ffn_w_up"][i, 1]),
             "wd1": _c(inputs["ffn_w_down"][i, 1]), "ln_ple": _pT(inputs["ln_ple"][i]),
             "wpg": _c(inputs["w_ple_gate"][i]), "wpp": _c(inputs["w_ple_proj"][i])}
        if last:
            w["fn"] = _c(inputs["final_norm"])
        else:
            w.update({"ln2": _pT(inputs["ln_ffn"][i + 1, 0]), "wg2": _c(inputs["ffn_w_gate"][i + 1, 0]),
                      "wu2": _c(inputs["ffn_w_up"][i + 1, 0]), "wd2": _c(inputs["ffn_w_down"][i + 1, 0])})
        del w_in
        p_i = np.asarray(inputs["p"][i], np.float32)
        in_maps = []
        for c in range(NCORES):
            b, t0 = c // 4, (c % 4) * TS
            in_maps.append(dict(w, x=x1[c * TS:(c + 1) * TS],
                                yT=np.ascontiguousarray(yfull[b, t0:t0 + TS, :].T),
                                pT=np.ascontiguousarray(p_i[b, t0:t0 + TS, :].T)))
        res = _launch(nc3, in_maps)
        x1 = np.concatenate([np.asarray(r["y"], np.float32) for r in res], axis=0)
        del w, in_maps, yfull
    return x1.reshape(NB, SEQ, D)


NFM = 36
NTM = 2064
YW = 1536
TWO_PI = 2.0 * math.pi


def l2_slab_plan():
    plan = []
    for j in range(2):
        plan.append([('fm', c * 128, j * 8 + c) for c in range(4)])
        plan.append([('fm', c * 128, j * 8 + 4 + c) for c in range(4)])
        plan.append([('tm', 0, 256, j * 256)])
    for j in range(2):
        plan.append([('fm', c * 128, 16 + j * 4 + c) for c in range(4)])
        plan.append([('tm', 0, 512, 512 + j * 512)])
    for j in range(4):
        plan.append([('fm', c * 128, 24 + j * 3 + c) for c in range(3)] + [('tm', 384, 128, 1536 + j * 128)])
    plan.append([('tm', 0, 16, 2048)])
    return plan


class Mix:
    def __init__(self, cx, S):
        self.cx, self.nc, self.s, self.S = cx, cx.nc, cx.s, S
        self.bi = {'d': 0, 'a': 0}

    def bank(self, kind='d'):
        if kind == 'd':
            i = self.bi['d'] % 6
        else:
            i = 6 + self.bi['a'] % 2
        self.bi[kind] += 1
        return i

    def mm(self, out, lhsT, rhs, r, w, start=True, stop=True):
        nc = self.nc
        self.s.op('pe', lambda: nc.tensor.matmul(out, lhsT, rhs, start=start, stop=stop), r, w)

    def tr(self, out, in_, ident, r, w):
        nc = self.nc
        self.s.op('pe', lambda: nc.tensor.transpose(out=out, in_=in_, identity=ident), r, w)

    def tt(self, out, in0, in1, op, r, w, eng='dve'):
        e = self.nc.vector if eng == 'dve' else self.nc.gpsimd
        self.s.op(eng, lambda: e.tensor_tensor(out=out, in0=in0, in1=in1, op=op), r, w)

    def ts(self, out, in0, s1, op0, r, w, s2=None, op1=None):
        nc = self.nc
        if op1 is None:
            self.s.op('dve', lambda: nc.vector.tensor_scalar(out=out, in0=in0, scalar1=s1, scalar2=None, op0=op0), r, w)
        else:
            self.s.op('dve', lambda: nc.vector.tensor_scalar(out=out, in0=in0, scalar1=s1, scalar2=s2, op0=op0, op1=op1), r, w)

    def stt(self, out, in0, scalar, in1, op0, op1, r, w):
        nc = self.nc
        self.s.op('dve', lambda: nc.vector.scalar_tensor_tensor(out=out, in0=in0, scalar=scalar, in1=in1, op0=op0, op1=op1), r, w)

    def act(self, out, in_, func, r, w, scale=None, bias=None, accum=None):
        nc = self.nc
        kw = {}
        if scale is not None:
            kw['scale'] = scale
        if bias is not None:
            kw['bias'] = bias
        if accum is not None:
            kw['accum_out'] = accum
        self.s.op('act', lambda: nc.scalar.activation(out=out, in_=in_, func=func, **kw), r, w)

    def vcopy(self, out, in_, r, w):
        nc = self.nc
        self.s.op('dve', lambda: nc.vector.tensor_copy(out=out, in_=in_), r, w)

    def memset(self, ap, val, w, r=()):
        nc = self.nc
        self.s.op('dve', lambda: nc.vector.memset(ap, val), r, w)

    def recip(self, out, in_, r, w):
        nc = self.nc
        self.s.op('dve', lambda: nc.vector.reciprocal(out=out, in_=in_), r, w)

    def dma(self, out, in_, r, w, q='sp'):
        nc = self.nc
        e = {'sp': nc.sync, 'pool': nc.gpsimd, 'act': nc.scalar}[q]
        self.s.op(q, lambda: e.dma_start(out=out, in_=in_), r, w, dma=True)

    def phase0(self, b, x, ln_ap, hTd):
        cx, S = self.cx, self.S
        b.load_ln(ln_ap)
        for t in range(S // 512):
            for g in range(4):
                b.norm_group(x[t * 512 + g * 128: t * 512 + (g + 1) * 128, :], g)
            self.dma(hTd[t], b.hT[:].rearrange("p k n -> p (k n)"), b.hT_keys(), [('hTd', t)])

    def rope_tables(self, pos_ap, tab_ap, cosd, sind):
        cx, S = self.cx, self.S
        cx.push()
        tab = cx.sb([128, 4], F32, "tab")
        self.dma(tab[:], tab_ap, [], ['tab'])
        posi = cx.sb([128, S], I32, "posi")
        u = cx.sb([128, S], F32, "u")
        kf = cx.sb([128, S], F32, "kf")
        ki = cx.sb([128, S], I32, "ki")
        m = cx.sb([128, S], F32, "m")
        self.dma(posi[:], pos_ap.partition_broadcast(128), [], ['posi'])
        self.vcopy(u[:], posi[:], ['posi'], ['u'])
        self.ts(u[:], u[:], tab[:, 0:1], ALU.mult, ['u', 'tab'], ['u'])
        for which, shift, dst in (('sin', 0.0, sind), ('cos', 0.25, cosd)):
            self.ts(kf[:], u[:], 1.0 / TWO_PI, ALU.mult, ['u'], ['kf'], s2=shift, op1=ALU.add)
            self.vcopy(ki[:], kf[:], ['kf'], ['ki'])
            self.vcopy(m[:], ki[:], ['ki'], ['m'])
            self.tt(kf[:], kf[:], m[:], ALU.subtract, ['kf', 'm'], ['kf'])
            self.ts(m[:], kf[:], 0.5, ALU.is_gt, ['kf'], ['m'])
            self.tt(kf[:], kf[:], m[:], ALU.subtract, ['kf', 'm'], ['kf'])
            self.ts(m[:], kf[:], -0.5, ALU.is_lt, ['kf'], ['m'])
            self.tt(kf[:], kf[:], m[:], ALU.add, ['kf', 'm'], ['kf'])
            self.act(m[:], kf[:], AF.Sin, ['kf'], ['m'], scale=6.283185)
            if which == 'sin':
                self.ts(m[:], m[:], tab[:, 1:2], ALU.mult, ['m', 'tab'], ['m'])
            self.dma(dst, m[:], ['m'], [which + 'd'])
        cx.pop()

    def phase1(self, b, wsl, hTd, Pfm, Ptm):
        cx, S, nc = self.cx, self.S, self.nc
        plan = l2_slab_plan()
        ev = 0
        for si_, spec in enumerate(plan):
            ncols = max((e[1] + (128 if e[0] == 'fm' else e[2])) for e in spec)
            si = b.load_slab(wsl[si_, :, 0:ncols], 32, ncols)
            slab = b.slabs[si]
            for t in range(S // 512):
                self.dma(b.hT[:].rearrange("p k n -> p (k n)"), hTd[t], [('hTd', t)], b.hT_keys())
                for e in spec:
                    if e[0] == 'fm':
                        _, c0, ch = e
                        pi = self.bank('d' if ev % 2 == 0 else 'a')
                        for k in range(32):
                            self.mm(cx.ps[pi][:], slab[:, k, c0:c0 + 128], b.hT[:, k, :],
                                    [('slab', si, k)] + [('hT', k, g) for g in range(4)], [('ps', pi)],
                                    start=(k == 0), stop=(k == 31))
                        ti = b.next_tmp()
                        tmp = b.tmpf[ti]
                        if ev % 2 == 0:
                            self.vcopy(tmp[:], cx.ps[pi][:], [('ps', pi)], [('tmpf', ti)])
                        else:
                            self.act(tmp[:], cx.ps[pi][:], AF.Copy, [('ps', pi)], [('tmpf', ti)])
                        ev += 1
                        self.dma(Pfm[ch, :, t * 512:(t + 1) * 512], tmp[:], [('tmpf', ti)], [('Pfm', ch, t)])
                    else:
                        _, c0, n, d0 = e
                        for g in range(4):
                            pi = self.bank('d' if ev % 2 == 0 else 'a')
                            for k in range(32):
                                self.mm(cx.ps[pi][:, 0:n], b.hT[:, k, g * 128:(g + 1) * 128], slab[:, k, c0:c0 + n],
                                        [('slab', si, k), ('hT', k, g)], [('ps', pi)],
                                        start=(k == 0), stop=(k == 31))
                            ti = b.next_tmp()
                            tmp = b.tmpf[ti]
                            if ev % 2 == 0:
                                self.vcopy(tmp[:, 0:n], cx.ps[pi][:, 0:n], [('ps', pi)], [('tmpf', ti)])
                            else:
                                self.act(tmp[:, 0:n], cx.ps[pi][:, 0:n], AF.Copy, [('ps', pi)], [('tmpf', ti)])
                            ev += 1
                            r0 = t * 512 + g * 128
                            self.dma(Ptm[r0:r0 + 128, d0:d0 + n], tmp[:, 0:n], [('tmpf', ti)], [('Ptm', d0, r0 // 128)])

    def rope_load(self, dst, chx, chp, Pfm, cos, sin, R1, R2, tag):
        self.dma(R1[:], Pfm[chx], [], ['R1'])
        self.dma(R2[:], Pfm[chp], [], ['R2'], q='act')
        self.tt(R1[:], R1[:], cos[:], ALU.mult, ['R1', 'cos'], ['R1'])
        self.tt(R2[:], R2[:], sin[:], ALU.mult, ['R2', 'sin'], ['R2'], eng='pool')
        self.tt(dst, R1[:], R2[:], ALU.add, ['R1', 'R2'], [tag])

    def rms_rows(self, stat, src, n, r, extra_scale=1.0):
        cx = self.cx
        junk = self.junk
        self.act(junk[:, 0:n], src, AF.Square, r, ['junk', 'st0'], accum=stat[:, 0:1])
        self.ts(stat[:, 1:2], stat[:, 0:1], extra_scale * extra_scale / n, ALU.mult, ['st0'], ['st1'], s2=EPS, op1=ALU.add)
        self.act(stat[:, 2:3], stat[:, 1:2], AF.Sqrt, ['st1'], ['st2'])
        self.recip(stat[:, 3:4], stat[:, 2:3], ['st2'], ['st3'])
        if extra_scale != 1.0:
            self.ts(stat[:, 3:4], stat[:, 3:4], extra_scale, ALU.mult, ['st3'], ['st3'])

    def diff_phase(self, Pfm, Ptm, cosd, sind, dl_ap, sub_ap, yout, lam_init):
        cx, S, nc = self.cx, self.S, self.nc
        NQ, NKB = S // 512, S // 128
        cx.push()
        cos = cx.sb([128, S], F32, "cos"); sin = cx.sb([128, S], F32, "sin")
        R1 = cx.sb([128, S], F32, "R1"); R2 = cx.sb([128, S], F32, "R2")
        qk = [cx.sb([128, S], BF16, f"qk{i}") for i in range(4)]
        V = cx.sb([128, NKB, 258], BF16, "Vext")
        E = [cx.sb([128, 512], BF16, f"E{i}") for i in range(3)]
        dl = cx.sb([128, 512], F32, "dl"); sub = cx.sb([128, 256], F32, "sub")
        st = cx.sb([128, 16], F32, "dst"); self.junk = cx.sb([128, 256], F32, "junk")
        o0 = [cx.sb([128, 256], F32, f"o0_{i}") for i in range(4)]
        o1 = [cx.sb([128, 256], F32, f"o1_{i}") for i in range(2)]
        self.dma(cos[:], cosd, [], ['cos']); self.dma(sin[:], sind, [], ['sin'])
        self.dma(dl[:], dl_ap.partition_broadcast(128), [], ['dl'])
        self.dma(sub[:], sub_ap.partition_broadcast(128), [], ['sub'])
        self.tt(self.junk[:, 0:128], dl[:, 0:128], dl[:, 128:256], ALU.mult, ['dl'], ['junk'])
        self.s.op('dve', lambda: nc.vector.reduce_sum(out=st[:, 4:5], in_=self.junk[:, 0:128], axis=mybir.AxisListType.X), ['junk'], ['l4'])
        self.tt(self.junk[:, 128:256], dl[:, 256:384], dl[:, 384:512], ALU.mult, ['dl'], ['junk'])
        self.s.op('dve', lambda: nc.vector.reduce_sum(out=st[:, 5:6], in_=self.junk[:, 128:256], axis=mybir.AxisListType.X), ['junk'], ['l5'])
        self.act(st[:, 6:7], st[:, 4:5], AF.Exp, ['l4'], ['l6'])
        self.act(st[:, 7:8], st[:, 5:6], AF.Exp, ['l5'], ['l7'])
        self.tt(st[:, 8:9], st[:, 7:8], st[:, 6:7], ALU.subtract, ['l6', 'l7'], ['l8'])
        self.ts(st[:, 8:9], st[:, 8:9], -lam_init, ALU.add, ['l8'], ['l8'])
        scale = 128.0 ** -0.5
        ei = 0
        for j in range(2):
            base = j * 8
            self.rope_load(qk[0][:], base + 0, base + 1, Pfm, cos, sin, R1, R2, 'qk0')
            self.rope_load(qk[1][:], base + 2, base + 3, Pfm, cos, sin, R1, R2, 'qk1')
            self.rope_load(qk[2][:], base + 4, base + 5, Pfm, cos, sin, R1, R2, 'qk2')
            self.rope_load(qk[3][:], base + 6, base + 7, Pfm, cos, sin, R1, R2, 'qk3')
            self.dma(V[:, :, 0:256], Ptm[:, j * 256:(j + 1) * 256].rearrange("(kb p) c -> p kb c", p=128), [], ['V'], q='pool')
            self.memset(V[:, :, 256:257], 1.0, ['V1'])
            for qb in range(NQ):
                for t in range(2):
                    accs = [0, 1, 2, 3]
                    qT, kT = qk[t], qk[2 + t]
                    for kb in range(NKB):
                        ps = 4 + (ei % 4)
                        self.mm(cx.ps[ps][:], kT[:, kb * 128:(kb + 1) * 128], qT[:, qb * 512:(qb + 1) * 512],
                                [f'qk{t}', f'qk{2 + t}'], [('ps', ps)])
                        e = ei % 3
                        ei += 1
                        self.act(E[e][:], cx.ps[ps][:], AF.Exp, [('ps', ps)], [('E', e)], scale=scale)
                        for qs in range(4):
                            self.mm(cx.ps[accs[qs]][:, 0:257], E[e][:, qs * 128:(qs + 1) * 128], V[:, kb, 0:257],
                                    [('E', e), 'V', 'V1'], [('ps', accs[qs])], start=(kb == 0), stop=(kb == NKB - 1))
                    for qs in range(4):
                        acc = cx.ps[accs[qs]]
                        self.recip(st[:, 9:10], acc[:, 256:257], [('ps', accs[qs])], ['r0'])
                        if t == 0:
                            self.ts(o0[qs][:], acc[:, 0:256], st[:, 9:10], ALU.mult, [('ps', accs[qs]), 'r0'], [('o0', qs)])
                        else:
                            oo = o1[qs % 2]
                            self.ts(oo[:], acc[:, 0:256], st[:, 9:10], ALU.mult, [('ps', accs[qs]), 'r0'], [('o1', qs % 2)])
                            self.stt(oo[:], oo[:], st[:, 8:9], o0[qs][:], ALU.mult, ALU.add,
                                     [('o1', qs % 2), ('o0', qs), 'l8'], [('o1', qs % 2)])
                            self.rms_rows(st, oo[:], 256, [('o1', qs % 2)])
                            self.ts(oo[:], oo[:], st[:, 3:4], ALU.mult, [('o1', qs % 2), 'st3'], [('o1', qs % 2)],
                                    s2=(1.0 - lam_init), op1=ALU.mult)
                            self.tt(oo[:], oo[:], sub[:], ALU.mult, [('o1', qs % 2), 'sub'], [('o1', qs % 2)])
                            r0 = qb * 512 + qs * 128
                            self.dma(yout[r0:r0 + 128, 1024 + j * 256:1024 + (j + 1) * 256], oo[:], [('o1', qs % 2)], [])
        cx.pop()

    def ret_phase(self, Pfm, Ptm, cosd, sind, lng_ap, yout):
        cx, S, nc = self.cx, self.S, self.nc
        NQ, NKB = S // 512, S // 128
        GW = 2 * S - 128
        OFF = S - 128
        cx.push()
        cos = cx.sb([128, S], F32, "cos"); sin = cx.sb([128, S], F32, "sin")
        R1 = cx.sb([128, S], F32, "R1"); R2 = cx.sb([128, S], F32, "R2")
        qT = cx.sb([128, S], BF16, "rq"); kT = cx.sb([128, S], BF16, "rk")
        V = cx.sb([128, NKB, 256], BF16, "rV")
        G = cx.sb([128, GW], F32, "G")
        E = [cx.sb([128, 512], BF16, f"E{i}") for i in range(3)]
        lng = cx.sb([128, 2], F32, "lng")
        st = cx.sb([128, 16], F32, "rst"); self.junk = cx.sb([128, 256], F32, "junk")
        oo = [cx.sb([128, 256], F32, f"ro{i}") for i in range(2)]
        gg = [cx.sb([128, 256], F32, f"rg{i}") for i in range(2)]
        self.dma(cos[:], cosd, [], ['cos']); self.dma(sin[:], sind, [], ['sin'])
        self.dma(lng[:], lng_ap.partition_broadcast(128), [], ['lng'])
        ei = 0
        oi = 0
        for j in range(2):
            base = 16 + j * 4
            self.rope_load(qT[:], base + 0, base + 1, Pfm, cos, sin, R1, R2, 'rq')
            self.rope_load(kT[:], base + 2, base + 3, Pfm, cos, sin, R1, R2, 'rk')
            c0 = 512 + j * 512
            self.dma(V[:], Ptm[:, c0:c0 + 256].rearrange("(kb p) c -> p kb c", p=128), [], ['V'], q='pool')
            self.s.op('pool', lambda: nc.gpsimd.iota(G[:], pattern=[[1, GW]], base=-OFF, channel_multiplier=-1,
                                                     allow_small_or_imprecise_dtypes=True), [], ['G'])
            self.stt(G[:], G[:], -1.0, G[:], ALU.mult, ALU.max, ['G'], ['G'])
            self.act(G[:], G[:], AF.Exp, ['G', 'lng'], ['G'], scale=lng[:, j:j + 1])
            for qb in range(NQ):
                accs = [0, 1, 2, 3]
                for kb in range(NKB):
                    ps = 4 + (ei % 4)
                    self.mm(cx.ps[ps][:], kT[:, kb * 128:(kb + 1) * 128], qT[:, qb * 512:(qb + 1) * 512],
                            ['rq', 'rk'], [('ps', ps)])
                    e = ei % 3
                    ei += 1
                    g0 = qb * 512 - kb * 128 + OFF
                    self.tt(E[e][:], cx.ps[ps][:], G[:, g0:g0 + 512], ALU.mult, [('ps', ps), 'G'], [('E', e)])
                    for qs in range(4):
                        self.mm(cx.ps[accs[qs]][:, 0:256], E[e][:, qs * 128:(qs + 1) * 128], V[:, kb, :],
                                [('E', e), 'V'], [('ps', accs[qs])], start=(kb == 0), stop=(kb == NKB - 1))
                for qs in range(4):
                    acc = cx.ps[accs[qs]]
                    o = oo[oi % 2]; gt = gg[oi % 2]; ok_ = ('ro', oi % 2); gk = ('rg', oi % 2)
                    oi += 1
                    self.act(o[:], acc[:, 0:256], AF.Copy, [('ps', accs[qs])], [ok_])
                    self.rms_rows(st, o[:], 256, [ok_], extra_scale=128.0 ** -0.5)
                    r0 = qb * 512 + qs * 128
                    self.dma(gt[:], Ptm[r0:r0 + 128, c0 + 256:c0 + 512], [], [gk])
                    self.act(gt[:], gt[:], AF.Silu, [gk], [gk])
                    self.stt(o[:], o[:], st[:, 3:4], gt[:], ALU.mult, ALU.mult, [ok_, gk, 'st3'], [ok_])
                    self.dma(yout[r0:r0 + 128, j * 256:(j + 1) * 256], o[:], [ok_], [])
        cx.pop()

    def gdn_phase(self, Pfm, Ptm, cw_ap, alog_ap, dt_ap, gnw_ap, masks_ap, ident_ap, yout):
        cx, S, nc = self.cx, self.S, self.nc
        NDC = S // 128
        NC4 = NDC * 4
        cx.push()
        MK = cx.sb([128, 9, 128], F32, "MK")
        IDF = cx.sb([128, 128], F32, "IDF"); ONES = cx.sb([128, 128], F32, "ONES")
        cw = cx.sb([128, 60], F32, "cw"); alog = cx.sb([128, 8], F32, "alog"); dtb = cx.sb([128, 8], F32, "dtb")
        nega = cx.sb([128, 8], F32, "nega"); gnw = cx.sb([128, 128], F32, "gnw")
        AB = cx.sb([128, NDC, 16], F32, "AB"); Gm = cx.sb([128, NDC, 8], F32, "Gm"); Bm = cx.sb([128, NDC, 8], F32, "Bm")
        tA = cx.sb([128, NDC], F32, "tA")
        def gt(name):
            return [cx.sb([128, NDC, 4], F32, f"{name}{d}") for d in range(2)]
        GC, GLb, GL0, GL1, EG, EK0, EK1, EL0, EL1, BE = (gt(n) for n in
                                                       ("GC", "GLb", "GL0", "GL1", "EG", "EK0", "EK1", "EL0", "EL1", "BE"))
        R = cx.sb([128, S + 4], F32, "R"); X = cx.sb([128, S], F32, "X")
        Q = cx.sb([128, S], F32, "Q"); Kf = cx.sb([128, S], F32, "Kf")
        Ktm = cx.sb([128, NDC, 128], F32, "Ktm"); Vtm = cx.sb([128, NDC, 128], F32, "Vtm")
        O = cx.sb([128, NDC, 128], F32, "O")
        st = cx.sb([128, 16], F32, "gst"); self.junk = cx.sb([128, 256], F32, "junk")
        sq = [cx.sb([128, 512], F32, f"sq{i}") for i in range(2)]
        names = ("A", "B", "P0", "P1", "Q0", "Q1", "TT", "Tm", "DG", "DEC", "DQ", "QKT", "VB", "KBG", "KO0", "KO1",
                 "U", "WT", "VN", "OA", "Sst", "zt", "ot")
        T = {n: cx.sb([128, 128], F32, "g" + n) for n in names}

        self.dma(MK[:], masks_ap.rearrange("m p f -> p m f"), [], ['MK'])
        self.dma(IDF[:], ident_ap, [], ['IDF'])
        self.memset(ONES[:], 1.0, ['ONES'])
        self.dma(cw[:], cw_ap, [], ['cw'])
        self.dma(alog[:], alog_ap.partition_broadcast(128), [], ['alog'])
        self.dma(dtb[:], dt_ap.partition_broadcast(128), [], ['dtb'])
        self.dma(gnw[:], gnw_ap.partition_broadcast(128), [], ['gnw'])
        self.dma(AB[:], Ptm[:, 2048:2064].rearrange("(dc p) c -> p dc c", p=128), [], ['AB'])
        self.act(nega[:], alog[:], AF.Exp, ['alog'], ['nega'])
        self.ts(nega[:], nega[:], -1.0, ALU.mult, ['nega'], ['nega'])
        for c in range(8):
            self.act(tA[:], AB[:, :, c], AF.Exp, ['AB', 'dtb'], ['tA'], bias=dtb[:, c:c + 1])
            self.ts(tA[:], tA[:], 1.0, ALU.add, ['tA'], ['tA'])
            self.act(tA[:], tA[:], AF.Ln, ['tA'], ['tA'])
            self.ts(Gm[:, :, c], tA[:], nega[:, c:c + 1], ALU.mult, ['tA', 'nega'], ['Gm'])
            self.act(Bm[:, :, c], AB[:, :, 8 + c], AF.Sigmoid, ['AB'], ['Bm'])
        rm0, rm1 = MK[:, 3, 0:1], MK[:, 4, 0:1]
        for d in range(2):
            rhs = Gm[:, :, 4 * d:4 * d + 4]
            for (mi, dst, nm) in ((d, GC[d], 'GC'), (2, GLb[d], 'GLb'), (3, GL0[d], 'GL0'), (4, GL1[d], 'GL1')):
                pi = self.bank('d')
                self.mm(cx.ps[pi][:, 0:NC4], MK[:, mi, :], rhs, ['MK', 'Gm'], [('ps', pi)])
                self.vcopy(dst[:].rearrange("p a b -> p (a b)"), cx.ps[pi][:, 0:NC4], [('ps', pi)], [(nm, d)])
            fl = lambda t_: t_[:].rearrange("p a b -> p (a b)")
            self.act(fl(EG[d]), fl(GC[d]), AF.Exp, [('GC', d)], [('EG', d)])
            self.tt(fl(EK0[d]), fl(GLb[d]), fl(GC[d]), ALU.subtract, [('GLb', d), ('GC', d)], [('EK0', d)])
            self.act(fl(EK0[d]), fl(EK0[d]), AF.Exp, [('EK0', d)], [('EK0', d)])
            self.ts(fl(EK1[d]), fl(EK0[d]), rm1, ALU.mult, [('EK0', d), 'MK'], [('EK1', d)])
            self.ts(fl(EK0[d]), fl(EK0[d]), rm0, ALU.mult, [('EK0', d), ('EK1', d), 'MK'], [('EK0', d)])
            self.act(fl(EL0[d]), fl(GL0[d]), AF.Exp, [('GL0', d)], [('EL0', d)])
            self.act(fl(EL1[d]), fl(GL1[d]), AF.Exp, [('GL1', d)], [('EL1', d)])
            self.tt(BE[d][:], Bm[:, :, 4 * d:4 * d + 4], EG[d][:], ALU.mult, ['Bm', ('EG', d)], [('BE', d)])
        self.memset(R[:, 0:2], 0.0, ['Rpad'])
        self.memset(R[:, S + 2:S + 4], 0.0, ['Rpad'])

        def conv_silu(j, r, dst, dk_):
            self.dma(R[:, 2:S + 2], Pfm[24 + j * 3 + r], [dk_], ['R'])
            w0 = (j * 3 + r) * 5
            self.ts(dst[:], R[:, 0:S], cw[:, w0:w0 + 1], ALU.mult, ['R', 'Rpad', 'cw'], [dk_])
            for t in range(1, 5):
                self.stt(dst[:], R[:, t:S + t], cw[:, w0 + t:w0 + t + 1], dst[:], ALU.mult, ALU.add, ['R', 'Rpad', 'cw', dk_], [dk_])
            self.act(dst[:], dst[:], AF.Silu, [dk_], [dk_])

        def l2norm(dst, dk_, scale):
            for t in range(S // 512):
                sl = slice(t * 512, (t + 1) * 512)
                sqt = sq[t % 2]
                self.act(sqt[:], dst[:, sl], AF.Square, [dk_], [('sq', t % 2)])
                pi = self.bank('d')
                self.mm(cx.ps[pi][:], ONES[:], sqt[:], ['ONES', ('sq', t % 2)], [('ps', pi)])
                self.ts(sqt[:], cx.ps[pi][:], EPS, ALU.add, [('ps', pi)], [('sq', t % 2)])
                self.act(sqt[:], sqt[:], AF.Sqrt, [('sq', t % 2)], [('sq', t % 2)])
                self.recip(sqt[:], sqt[:], [('sq', t % 2)], [('sq', t % 2)])
                self.stt(dst[:, sl], dst[:, sl], scale, sqt[:], ALU.mult, ALU.mult, [dk_, ('sq', t % 2)], [dk_])

        def to_tm(src, sk_, dst, dk_):
            for d4 in range(NDC // 4):
                pi = self.bank('d')
                for i in range(4):
                    dc = d4 * 4 + i
                    self.tr(cx.ps[pi][:, i * 128:(i + 1) * 128], src[:, dc * 128:(dc + 1) * 128], IDF[:], [sk_, 'IDF'], [('ps', pi)])
                self.vcopy(dst[:, d4 * 4:(d4 + 1) * 4, :].rearrange("p a b -> p (a b)"), cx.ps[pi][:], [('ps', pi)], [dk_])

        def mmsb(dst, lhsT, rhs, r, w, acc=None, accop=None):
            pi = self.bank('d')
            self.mm(cx.ps[pi][:, 0:128], lhsT, rhs, r, [('ps', pi)])
            if acc is None:
                self.vcopy(dst, cx.ps[pi][:, 0:128], [('ps', pi)], w)
            else:
                self.tt(dst, acc, cx.ps[pi][:, 0:128], accop, [('ps', pi)] + r[:0] + w, w)

        for j in range(4):
            conv_silu(j, 2, X, 'X')
            to_tm(X, 'X', Vtm, 'Vtm')
            conv_silu(j, 0, Q, 'Q')
            l2norm(Q, 'Q', 128.0 ** -0.5)
            conv_silu(j, 1, Kf, 'Kf')
            l2norm(Kf, 'Kf', 1.0)
            to_tm(Kf, 'Kf', Ktm, 'Ktm')
            for d in range(2):
                col = slice(j, j + 1)
                self.memset(T['Sst'][:], 0.0, ['Sst'])
                order = list(range(NDC)) if d == 0 else list(range(NDC - 1, -1, -1))
                halves = (0, 1) if d == 0 else (1, 0)
                for dc in order:
                    tok = slice(dc * 128, (dc + 1) * 128)
                    gcp = GC[d][:, dc, col]; beta = Bm[:, dc, 4 * d + j:4 * d + j + 1]
                    A, B_, TT, Tm, DG, DEC, DQ, QKT = (T[n] for n in ("A", "B", "TT", "Tm", "DG", "DEC", "DQ", "QKT"))
                    self.ts(DG[:], IDF[:], gcp, ALU.mult, ['IDF', ('GC', d)], ['DG'])
                    p2 = self.bank('d')
                    self.mm(cx.ps[p2][:, 0:128], ONES[:], DG[:], ['ONES', 'DG'], [('ps', p2)])
                    self.ts(DEC[:], cx.ps[p2][:, 0:128], gcp, ALU.subtract, [('ps', p2), ('GC', d)], ['DEC'], s2=0.0, op1=ALU.max)
                    self.ts(DQ[:], cx.ps[p2][:, 0:128], gcp, ALU.subtract, [('ps', p2), ('GC', d)], ['DQ'], s2=0.0, op1=ALU.min)
                    self.act(DEC[:], DEC[:], AF.Exp, ['DEC'], ['DEC'], scale=-1.0)
                    self.act(DQ[:], DQ[:], AF.Exp, ['DQ'], ['DQ'])
                    self.stt(DEC[:], DEC[:], beta, MK[:, 5 + d, :], ALU.mult, ALU.mult, ['DEC', 'Bm', 'MK'], ['DEC'])
                    self.tt(DQ[:], DQ[:], MK[:, 7 + d, :], ALU.mult, ['DQ', 'MK'], ['DQ'])
                    p1 = self.bank('d')
                    self.mm(cx.ps[p1][:, 0:128], Kf[:, tok], Kf[:, tok], ['Kf'], [('ps', p1)])
                    self.tt(A[:], cx.ps[p1][:, 0:128], DEC[:], ALU.mult, [('ps', p1), 'DEC'], ['A'])
                    p3 = self.bank('d')
                    self.mm(cx.ps[p3][:, 0:128], Kf[:, tok], Q[:, tok], ['Kf', 'Q'], [('ps', p3)])
                    self.tt(QKT[:], cx.ps[p3][:, 0:128], DQ[:], ALU.mult, [('ps', p3), 'DQ'], ['QKT'])
                    p4 = self.bank('d')
                    self.tr(cx.ps[p4][:, 0:128], A[:], IDF[:], ['A', 'IDF'], [('ps', p4)])
                    self.vcopy(B_[:], cx.ps[p4][:, 0:128], [('ps', p4)], ['B'])
                    self.tt(TT[:], IDF[:], B_[:], ALU.subtract, ['IDF', 'B'], ['TT'])
                    self.tt(Tm[:], IDF[:], A[:], ALU.subtract, ['IDF', 'A'], ['Tm'])
                    P_, Q_, pk, qk_ = B_, A, 'B', 'A'
                    for lvl in range(5):
                        Pn, pnk = T[f"P{lvl % 2}"], f"P{lvl % 2}"
                        Qn, qnk = T[f"Q{lvl % 2}"], f"Q{lvl % 2}"
                        mmsb(Pn[:], Q_[:], P_[:], [pk, qk_], [pnk])
                        if lvl < 4:
                            mmsb(Qn[:], P_[:], Q_[:], [pk, qk_], [qnk])
                        pa = self.bank('d')
                        self.mm(cx.ps[pa][:, 0:128], Tm[:], Pn[:], ['Tm', pnk], [('ps', pa)])
                        if lvl < 4:
                            pb = self.bank('d')
                            self.mm(cx.ps[pb][:, 0:128], TT[:], Qn[:], ['TT', qnk], [('ps', pb)])
                        self.tt(TT[:], TT[:], cx.ps[pa][:, 0:128], ALU.add, ['TT', ('ps', pa)], ['TT'])
                        if lvl < 4:
                            self.tt(Tm[:], Tm[:], cx.ps[pb][:, 0:128], ALU.add, ['Tm', ('ps', pb)], ['Tm'])
                        P_, Q_, pk, qk_ = Pn, Qn, pnk, qnk
                    VB, KBG, KO0, KO1, U, WT, VN, OA, Sst = (T[n] for n in ("VB", "KBG", "KO0", "KO1", "U", "WT", "VN", "OA", "Sst"))
                    self.ts(VB[:], Vtm[:, dc, :], beta, ALU.mult, ['Vtm', 'Bm'], ['VB'])
                    self.ts(KBG[:], Ktm[:, dc, :], BE[d][:, dc, col], ALU.mult, ['Ktm', ('BE', d)], ['KBG'])
                    self.ts(KO0[:], Ktm[:, dc, :], EK0[d][:, dc, col], ALU.mult, ['Ktm', ('EK0', d)], ['KO0'])
                    self.ts(KO1[:], Ktm[:, dc, :], EK1[d][:, dc, col], ALU.mult, ['Ktm', ('EK1', d)], ['KO1'])
                    mmsb(U[:], TT[:], VB[:], ['TT', 'VB'], ['U'])
                    mmsb(WT[:], KBG[:], TT[:], ['KBG', 'TT'], ['WT'])
                    for hh in halves:
                        rows = slice(64 * hh, 64 * hh + 64)
                        KOh, kok = (KO0, 'KO0') if hh == 0 else (KO1, 'KO1')
                        ELh = (EL0 if hh == 0 else EL1)[d][:, dc, col]
                        elk = ('EL0' if hh == 0 else 'EL1', d)
                        pv = self.bank('d')
                        self.mm(cx.ps[pv][:, 0:128], WT[:], Sst[:], ['WT', 'Sst'], [('ps', pv)])
                        self.tt(VN[:], U[:], cx.ps[pv][:, 0:128], ALU.subtract, ['U', ('ps', pv)], ['VN'])
                        pa = self.bank('a')
                        self.mm(cx.ps[pa][:, 0:128], Q[:, tok], Sst[:], ['Q', 'Sst'], [('ps', pa)])
                        self.act(OA[:], cx.ps[pa][:, 0:128], AF.Copy, [('ps', pa), ('EG', d)], ['OA'], scale=EG[d][:, dc, col])
                        pb = self.bank('d')
                        self.mm(cx.ps[pb][:, 0:128], QKT[:], VN[:], ['QKT', 'VN'], [('ps', pb)])
                        if d == 0:
                            self.tt(O[rows, dc, :], OA[rows, :], cx.ps[pb][rows, 0:128], ALU.add, ['OA', ('ps', pb)], [('O', dc)])
                        else:
                            self.tt(OA[rows, :], OA[rows, :], cx.ps[pb][rows, 0:128], ALU.add, ['OA', ('ps', pb)], ['OA'])
                            self.tt(O[rows, dc, :], O[rows, dc, :], OA[rows, :], ALU.add, ['OA', ('O', dc)], [('O', dc)])
                        pss = self.bank('d')
                        self.mm(cx.ps[pss][:, 0:128], KOh[:], VN[:], [kok, 'VN'], [('ps', pss)])
                        self.stt(Sst[:], Sst[:], ELh, cx.ps[pss][:, 0:128], ALU.mult, ALU.add, ['Sst', elk, ('ps', pss)], ['Sst'])
            zt, ot = T['zt'], T['ot']
            for dc in range(NDC):
                r0 = dc * 128
                self.rms_rows(st, O[:, dc, :], 128, [('O', dc)])
                self.stt(ot[:], O[:, dc, :], st[:, 3:4], gnw[:], ALU.mult, ALU.mult, [('O', dc), 'st3', 'gnw'], ['ot'])
                self.dma(zt[:], Ptm[r0:r0 + 128, 1536 + j * 128:1536 + (j + 1) * 128], [], ['zt'])
                self.act(zt[:], zt[:], AF.Silu, ['zt'], ['zt'])
                self.tt(ot[:], ot[:], zt[:], ALU.mult, ['ot', 'zt'], ['ot'])
                self.dma(yout[r0:r0 + 128, 512 + j * 128:512 + (j + 1) * 128], ot[:], ['ot'], [])
        cx.pop()


def gdn_masks():
    p = np.arange(128)[:, None]
    f = np.arange(128)[None, :]
    same = (p // 64) == (f // 64)
    m = np.zeros((9, 128, 128), np.float32)
    m[0] = same & (p <= f)
    m[1] = same & (p >= f)
    m[2] = same
    m[3] = (p < 64) & (f >= 0)
    m[4] = (p >= 64) & (f >= 0)
    m[5] = same & (p > f)
    m[6] = same & (p < f)
    m[7] = same & (f >= p)
    m[8] = same & (f <= p)
    return m


def build_l2_prog(S, layer):
    lam_init = 0.8 - 0.6 * math.exp(-0.3 * layer)
    nc = bass.Bass("TRN2", target_bir_lowering=False)
    NT = S // 512
    x = dram_in(nc, "x", [S, D])
    ln = dram_in(nc, "ln", [128, 32])
    pos = dram_in(nc, "pos", [S], I32)
    wsl = dram_in(nc, "wsl", [15, D, 512])
    ident = dram_in(nc, "ident", [128, 128])
    tab = dram_in(nc, "tab", [128, 4])
    lng = dram_in(nc, "lng", [2])
    cw = dram_in(nc, "cw", [128, 60])
    alog = dram_in(nc, "alog", [8]); dtb = dram_in(nc, "dtb", [8]); gnw = dram_in(nc, "gnw", [128])
    masks = dram_in(nc, "masks", [9, 128, 128])
    dl = dram_in(nc, "dl", [512]); sub = dram_in(nc, "sub", [256])
    y = dram_out(nc, "y", [S, YW])
    hTd = dram_tmp(nc, "hTd", [NT, 128, 32 * 512], BF16)
    Pfm = dram_tmp(nc, "Pfm", [NFM, 128, S]); Ptm = dram_tmp(nc, "Ptm", [S, NTM])
    cosd = dram_tmp(nc, "cosd", [128, S]); sind = dram_tmp(nc, "sind", [128, S])
    cx = Ctx(nc)
    mx = Mix(cx, S)
    mx.rope_tables(pos, tab, cosd, sind)
    cx.push()
    b = Blocks(cx, ident)
    mx.phase0(b, x, ln, hTd)
    cx.s.barrier()
    mx.phase1(b, wsl, hTd, Pfm, Ptm)
    cx.pop()
    mx.diff_phase(Pfm, Ptm, cosd, sind, dl, sub, y, lam_init)
    mx.ret_phase(Pfm, Ptm, cosd, sind, lng, y)
    mx.gdn_phase(Pfm, Ptm, cw, alog, dtb, gnw, masks, ident, y)
    cx.s.finish()
    return nc


O_RQ, O_RK, O_RV, O_RG = 0, 1024, 2048, 4096
O_GQKV, O_GZ, O_GA, O_GB = 6144, 12288, 14336, 14368
O_DQ, O_DK, O_DV, O_GATE = 14400, 16448, 18496, 20544
_PERM = (np.arange(128) + 64) % 128


def l2_weight_slabs(w_in, g):
    out = np.zeros((15, D, 512), np.float32)
    si = 0
    for j in range(2):
        h = 2 * g + j
        for base in (O_DQ, O_DK):
            for t in range(2):
                c0 = base + (2 * h + t) * 128
                blk = w_in[:, c0:c0 + 128]
                out[si, :, (2 * t) * 128:(2 * t + 1) * 128] = blk
                out[si, :, (2 * t + 1) * 128:(2 * t + 2) * 128] = blk[:, _PERM]
            si += 1
        out[si, :, 0:256] = w_in[:, O_DV + h * 256:O_DV + (h + 1) * 256]
        si += 1
    for j in range(2):
        h = 2 * g + j
        for n, base in enumerate((O_RQ, O_RK)):
            blk = w_in[:, base + h * 128:base + (h + 1) * 128]
            out[si, :, (2 * n) * 128:(2 * n + 1) * 128] = blk
            out[si, :, (2 * n + 1) * 128:(2 * n + 2) * 128] = blk[:, _PERM]
        si += 1
        out[si, :, 0:256] = w_in[:, O_RV + h * 256:O_RV + (h + 1) * 256]
        out[si, :, 256:512] = w_in[:, O_RG + h * 256:O_RG + (h + 1) * 256]
        si += 1
    for j in range(4):
        h = 4 * g + j
        for r in range(3):
            out[si, :, r * 128:(r + 1) * 128] = w_in[:, O_GQKV + r * 2048 + h * 128:O_GQKV + r * 2048 + (h + 1) * 128]
        out[si, :, 384:512] = w_in[:, O_GZ + h * 128:O_GZ + (h + 1) * 128]
        si += 1
    for n, base in enumerate((O_GA, O_GB)):
        for d in range(2):
            c0 = base + d * 16 + 4 * g
            out[si, :, n * 8 + d * 4:n * 8 + d * 4 + 4] = w_in[:, c0:c0 + 4]
    return out


def l2_consts(inputs, layer, g):
    conv_w = np.asarray(inputs["conv_w"][layer], np.float32)
    cw = np.zeros((128, 60), np.float32)
    for j in range(4):
        h = 4 * g + j
        for r in range(3):
            for t in range(5):
                cw[:, (j * 3 + r) * 5 + t] = conv_w[t, r * 2048 + h * 128:r * 2048 + (h + 1) * 128]
    a_log = np.asarray(inputs["gdn_a_log"][layer], np.float32)
    dtb = np.asarray(inputs["gdn_dt_bias"][layer], np.float32)
    inv = (1.0 / (10000.0 ** (np.arange(0, 128, 2, dtype=np.float32) / 128))).astype(np.float32)
    tab = np.zeros((128, 4), np.float32)
    tab[:, 0] = np.concatenate([inv, inv])
    tab[:, 1] = np.where(np.arange(128) < 64, -1.0, 1.0)
    heads = np.arange(2 * g, 2 * g + 2, dtype=np.float32)
    lng = np.log(1.0 - 2.0 ** (-5.0 - heads)).astype(np.float32)
    return {
        "cw": cw,
        "alog": np.ascontiguousarray(a_log[:, 4 * g:4 * g + 4]).reshape(-1),
        "dtb": np.ascontiguousarray(dtb[:, 4 * g:4 * g + 4]).reshape(-1),
        "gnw": np.asarray(inputs["gdn_norm_w"][layer], np.float32),
        "masks": gdn_masks(),
        "dl": np.asarray(inputs["diff_lambda"][layer], np.float32).reshape(-1),
        "sub": np.asarray(inputs["diff_subln_w"][layer], np.float32),
        "tab": tab, "lng": lng, "ident": np.eye(128, dtype=np.float32),
    }
```

```python
import math
import numpy as np
import concourse.bass as bass
import concourse.mybir as mybir
from concourse.bass_utils import run_bass_kernel_spmd

F32 = mybir.dt.float32
BF16 = mybir.dt.bfloat16
I32 = mybir.dt.int32
ALU = mybir.AluOpType
AF = mybir.ActivationFunctionType

D = 4096
DFF = 8192
SEQ = 4096
NB = 2
DEPTH = 2
PLE = 256
EPS = 1e-6
NCORES = 8
SEM_LIM = 20000


class Sched:
    def __init__(self, nc):
        self.nc = nc
        self.E = dict(pe=nc.tensor, dve=nc.vector, act=nc.scalar, pool=nc.gpsimd, sp=nc.sync)
        self.ops = []
        self.keyfn = None
        self.esems = {e: [] for e in self.E}
        self.ecount = {e: 0 for e in self.E}
        self.RING = 12
        self.dsems = {}
        self.dcount = {e: 0 for e in self.E}
        self.seen = {e: {} for e in self.E}
        self.nsem = 0
        self.last_w = {}
        self.readers = {}
        self.tok = {}
        self.uid = 0

    def _newsem(self, tag):
        self.nsem += 1
        return self.nc.semaphore(f"{tag}{self.nsem}").__enter__()

    def op(self, eng, fn, reads=(), writes=(), dma=False):
        kf = self.keyfn
        if kf is not None:
            reads = [kf(k) for k in reads]
            writes = [kf(k) for k in writes]
        self.ops.append((eng, fn, tuple(reads), tuple(writes), dma))

    def emit(self):
        ops = self.ops
        self.ops = []
        n = len(ops)
        base = self.uid
        deps = []
        needs = [False] * n
        last_w, readers = self.last_w, self.readers
        for i, (eng, fn, R, W, dma) in enumerate(ops):
            u = base + i
            d = set()
            for k in R:
                if k in last_w:
                    d.add(last_w[k])
            for k in W:
                if k in last_w:
                    d.add(last_w[k])
                rs = readers.get(k)
                if rs:
                    d.update(rs)
            d.discard(u)
            dd = []
            for j in d:
                if j >= base:
                    je, _, _, _, jd = ops[j - base]
                    if je == eng and eng == 'pe' and not jd:
                        continue
                    needs[j - base] = True
                    dd.append(j)
                else:
                    t = self.tok.get(j)
                    if t is not None:
                        if t[3] == eng and eng == 'pe' and not t[4]:
                            continue
                        dd.append(j)
            deps.append(dd)
            for k in R:
                readers.setdefault(k, []).append(u)
            for k in W:
                last_w[k] = u
                readers[k] = []
        lastidx = {}
        for i, (eng, fn, R, W, dma) in enumerate(ops):
            lastidx[eng] = i
        for eng, i in lastidx.items():
            needs[i] = True
        for i, (eng, fn, R, W, dma) in enumerate(ops):
            u = base + i
            e = self.E[eng]
            seen = self.seen[eng]
            for j in deps[i]:
                t = self.tok[j]
                name, sem, val = t[0], t[1], t[2]
                if seen.get(name, 0) >= val:
                    continue
                e.wait_ge(sem, val)
                seen[name] = val
            if dma:
                ring = self.dsems.setdefault(eng, [])
                c = self.dcount[eng]
                slot = c % self.RING
                if slot >= len(ring):
                    ring.append([self._newsem(f"d{eng}"), 0])
                sem, uses = ring[slot]
                name = f"d{eng}{slot}"
                if uses > 0 and seen.get(name, 0) < 16 * uses:
                    e.wait_ge(sem, 16 * uses)
                    seen[name] = 16 * uses
                inst = fn()
                inst.then_inc(sem, 16)
                ring[slot][1] = uses + 1
                self.dcount[eng] = c + 1
                self.tok[u] = (name, sem, 16 * (uses + 1), eng, True)
            else:
                inst = fn()
                if needs[i]:
                    c = self.ecount[eng]
                    k = c // SEM_LIM
                    sl = self.esems[eng]
                    if k >= len(sl):
                        sl.append(self._newsem(f"e{eng}"))
                    sem = sl[k]
                    inst.then_inc(sem, 1)
                    self.ecount[eng] = c + 1
                    self.tok[u] = (f"e{eng}{k}", sem, c % SEM_LIM + 1, eng, False)
        self.uid = base + n
        live = set(last_w.values())
        for rs in readers.values():
            live.update(rs)
        self.tok = {k: v for k, v in self.tok.items() if k in live or k >= self.uid - 64}

    def barrier(self):
        self.emit()
        waits = []
        for eng, ring in self.dsems.items():
            for slot, (sem, uses) in enumerate(ring):
                if uses > 0:
                    waits.append((f"d{eng}{slot}", sem, 16 * uses))
        for eng, sl in self.esems.items():
            c = self.ecount[eng]
            if c > 0:
                k = (c - 1) // SEM_LIM
                waits.append((f"e{eng}{k}", sl[k], (c - 1) % SEM_LIM + 1))
        for eng, e in self.E.items():
            seen = self.seen[eng]
            for name, sem, val in waits:
                if seen.get(name, 0) < val:
                    e.wait_ge(sem, val)
                    seen[name] = val
        self.last_w.clear()
        self.readers.clear()
        self.tok.clear()

    def finish(self):
        self.emit()
        sp = self.E['sp']
        for eng, ring in self.dsems.items():
            for slot, (sem, uses) in enumerate(ring):
                if uses > 0:
                    sp.wait_ge(sem, 16 * uses)
        for eng, sl in self.esems.items():
            c = self.ecount[eng]
            if c > 0:
                k = (c - 1) // SEM_LIM
                sp.wait_ge(sl[k], (c - 1) % SEM_LIM + 1)


class Ctx:
    def __init__(self, nc):
        self.nc = nc
        self.s = Sched(nc)
        self.nbuf = 0
        self.live = []
        self.stack = []
        self.ps = [self.sb_psum(f"ps{i}") for i in range(8)]
        self.psrr = 0

    def sb(self, shape, dt, name=None):
        self.nbuf += 1
        cm = self.nc.sbuf_tensor((name or "b") + f"_{self.nbuf}", list(shape), dt)
        t = cm.__enter__()
        self.live.append(cm)
        return t

    def push(self):
        self.stack.append(len(self.live))

    def pop(self):
        self.s.barrier()
        n = self.stack.pop()
        while len(self.live) > n:
            self.live.pop().__exit__(None, None, None)

    def sb_psum(self, name):
        return self.nc.psum_tensor(name, [128, 512], F32).__enter__()


def dram_in(nc, name, shape, dt=F32):
    return nc.dram_tensor(name, list(shape), dt, kind="ExternalInput").ap()


def dram_out(nc, name, shape, dt=F32):
    return nc.dram_tensor(name, list(shape), dt, kind="ExternalOutput").ap()


def dram_tmp(nc, name, shape, dt=F32):
    return nc.dram_tensor(name, list(shape), dt, kind="Internal").ap()


class Blocks:
    def __init__(self, cx, ident_d):
        self.cx = cx
        nc, s = cx.nc, cx.s
        self.nc, self.s = nc, s
        self.lnT = cx.sb([128, 32], F32, "lnT")
        self.stat = cx.sb([128, 8], F32, "stat")
        self.identf = cx.sb([128, 128], F32, "identf")
        self.identb = cx.sb([128, 128], BF16, "identb")
        if getattr(cx, 'want_identf', False):
            s.op('sp', lambda: nc.sync.dma_start(out=self.identf[:], in_=ident_d[:, :]), writes=['identf'], dma=True)
        s.op('pool', lambda: nc.gpsimd.dma_start(out=self.identb[:], in_=ident_d[:, :]), writes=['identb'], dma=True)
        self.slabs = [cx.sb([128, 32, 512], BF16, f"slab{i}") for i in range(2)]
        self.slab_i = 0
        self.hT = cx.sb([128, 32, 512], BF16, "hT")
        self.xin = cx.sb([128, D], F32, "xin")
        self.xn = cx.sb([128, D], BF16, "xn")
        self.ps_i = 0
        self.LNW = 32
        self.tmp_i = 0
        self.tmpf = [cx.sb([128, 512], F32, f"tmpf{i}") for i in range(4)]
        self.xres = [cx.sb([128, 512], F32, f"xres{i}") for i in range(2)]
        self.xres_i = 0
        self.xo = [cx.sb([128, 512], F32, f"xo{i}") for i in range(2)]
        self.xo_i = 0

    def next_ps(self, lo=0, hi=8):
        i = lo + self.ps_i % (hi - lo)
        self.ps_i += 1
        return i

    def next_tmp(self):
        i = self.tmp_i % len(self.tmpf)
        self.tmp_i += 1
        return i

    def load_slab(self, w_ap, kc, ncols):
        nc, s = self.nc, self.s
        i = self.slab_i % len(self.slabs)
        self.slab_i += 1
        slab = self.slabs[i]
        src = w_ap.rearrange("(k p) n -> p k n", p=128)
        step = 8
        for k0 in range(0, kc, step):
            k1 = min(kc, k0 + step)
            s.op('pool', (lambda k0=k0, k1=k1: nc.gpsimd.dma_start(out=slab[:, k0:k1, 0:ncols], in_=src[:, k0:k1, :])),
                 writes=[('slab', i, k) for k in range(k0, k1)], dma=True)
        return i

    def load_ln(self, ln_ap):
        nc, s = self.nc, self.s
        s.op('sp', lambda: nc.sync.dma_start(out=self.lnT[:], in_=ln_ap), writes=['lnT'], dma=True)

    def norm_group(self, x_rows_ap, g, hT=None, rk=None):
        nc, s = self.nc, self.s
        hT = hT if hT is not None else self.hT
        xin, xn, stat = self.xin, self.xn, self.stat
        s.op('sp', lambda: nc.sync.dma_start(out=xin[:], in_=x_rows_ap), writes=['xin'],
             reads=([('dr', rk[0], rk[1], c) for c in range(8)] if rk else []), dma=True)
        s.op('act', lambda: nc.scalar.activation(out=xn[:], in_=xin[:], func=AF.Square, accum_out=stat[:, 0:1]),
             reads=['xin'], writes=['xn', 'stat0'])
        s.op('dve', lambda: nc.vector.tensor_scalar(out=stat[:, 1:2], in0=stat[:, 0:1], scalar1=1.0 / D, scalar2=EPS,
                                                    op0=ALU.mult, op1=ALU.add), reads=['stat0'], writes=['stat1'])
        s.op('act', lambda: nc.scalar.activation(out=stat[:, 2:3], in_=stat[:, 1:2], func=AF.Sqrt),
             reads=['stat1'], writes=['stat2'])
        s.op('dve', lambda: nc.vector.reciprocal(out=stat[:, 3:4], in_=stat[:, 2:3]), reads=['stat2'], writes=['stat3'])
        s.op('dve', lambda: nc.vector.tensor_scalar(out=xn[:], in0=xin[:], scalar1=stat[:, 3:4], scalar2=None,
                                                    op0=ALU.mult), reads=['xin', 'stat3'], writes=['xn'])
        for c4 in range(getattr(self, 'NR', 8)):
            pi = self.next_ps(6, 8)
            psb = self.cx.ps[pi][:].bitcast(BF16)
            for j in range(4):
                c = c4 * 4 + j
                s.op('pe', (lambda c=c, j=j, psb=psb: nc.tensor.transpose(out=psb[:, j * 128:(j + 1) * 128],
                                                                         in_=xn[:, c * 128:(c + 1) * 128],
                                                                         identity=self.identb[:])),
                     reads=['xn', 'identb'], writes=[('ps', pi)])
            for j in range(4):
                c = c4 * 4 + j
                if c4 % 2 == 0:
                    s.op('dve', (lambda c=c, j=j, psb=psb: nc.vector.tensor_scalar(
                        out=hT[:, c, g * 128:(g + 1) * 128], in0=psb[:, j * 128:(j + 1) * 128],
                        scalar1=self.lnT[:, (c % self.LNW):(c % self.LNW) + 1], scalar2=None, op0=ALU.mult)),
                         reads=[('ps', pi), 'lnT'], writes=[('hT', c, g)])
                else:
                    s.op('act', (lambda c=c, j=j, psb=psb: nc.scalar.activation(
                        out=hT[:, c, g * 128:(g + 1) * 128], in_=psb[:, j * 128:(j + 1) * 128],
                        func=AF.Copy, scale=self.lnT[:, (c % self.LNW):(c % self.LNW) + 1])),
                         reads=[('ps', pi), 'lnT'], writes=[('hT', c, g)])

    def hT_keys(self, kc=32, ng=4):
        return [('hT', c, g) for c in range(kc) for g in range(ng)]

    def ffn(self, x_ap, out_ap, ln_ap, wg, wu, wd, gT, xk=None, ok=None):
        nc, s = self.nc, self.s
        self.load_ln(ln_ap)
        for g in range(4):
            self.norm_group(x_ap[g * 128:(g + 1) * 128, :], g, rk=(xk[0], xk[1] * 4 + g) if xk else None)
        hk = self.hT_keys()
        for sl in range(DFF // 512):
            ig = self.load_slab(wg[:, sl * 512:(sl + 1) * 512], 32, 512)
            iu = self.load_slab(wu[:, sl * 512:(sl + 1) * 512], 32, 512)
            for c in range(4):
                pg, pu = self.next_ps(0, 6), self.next_ps(0, 6)
                for (pi, si) in ((pg, ig), (pu, iu)):
                    slab = self.slabs[si]
                    for k in range(32):
                        s.op('pe', (lambda k=k, c=c, pi=pi, slab=slab: nc.tensor.matmul(
                            self.cx.ps[pi][:], slab[:, k, c * 128:(c + 1) * 128], self.hT[:, k, :],
                            start=(k == 0), stop=(k == 31))),
                             reads=[('slab', si, k)] + ([('hT', k, g) for g in range(4)]),
                             writes=[('ps', pi)])
                ti = self.next_tmp()
                tmp = self.tmpf[ti]
                s.op('act', (lambda pg=pg, tmp=tmp: nc.scalar.activation(out=tmp[:], in_=self.cx.ps[pg][:], func=AF.Silu)),
                     reads=[('ps', pg)], writes=[('tmpf', ti)])
                fc = sl * 4 + c
                s.op('dve', (lambda pu=pu, tmp=tmp, fc=fc: nc.vector.tensor_tensor(
                    out=gT[:, fc, :], in0=tmp[:], in1=self.cx.ps[pu][:], op=ALU.mult)),
                     reads=[('ps', pu), ('tmpf', ti)], writes=[('gT', fc)])
        nF = DFF // 128
        npiece = max(1, nF // 32)
        kcp = nF // npiece
        for dsl in range(D // 512):
            pis = [self.next_ps(0, 6) for _ in range(4)]
            for piece in range(npiece):
                si = self.load_slab(wd[piece * kcp * 128:(piece + 1) * kcp * 128, dsl * 512:(dsl + 1) * 512], kcp, 512)
                slab = self.slabs[si]
                for g in range(4):
                    for k in range(kcp):
                        fk = piece * kcp + k
                        s.op('pe', (lambda g=g, k=k, fk=fk, slab=slab, pi=pis[g]: nc.tensor.matmul(
                            self.cx.ps[pi][:], gT[:, fk, g * 128:(g + 1) * 128], slab[:, k, :],
                            start=(fk == 0), stop=(fk == nF - 1))),
                             reads=[('slab', si, k), ('gT', fk)], writes=[('ps', pis[g])])
            for g in range(4):
                self.residual_out(x_ap[g * 128:(g + 1) * 128, dsl * 512:(dsl + 1) * 512],
                                  out_ap[g * 128:(g + 1) * 128, dsl * 512:(dsl + 1) * 512], pis[g], 0.5,
                                  sk=('dr', xk[0], xk[1] * 4 + g, dsl) if xk else None,
                                  dk=('dr', ok[0], ok[1] * 4 + g, dsl) if ok else None)

    def mix_merge(self, x_ap, yT_ap, ln_ap, wgate, wbr, gT, xk=None):
        nc, s = self.nc, self.s
        self.load_ln(ln_ap)
        for g in range(4):
            self.norm_group(x_ap[g * 128:(g + 1) * 128, :], g, rk=(xk[0], xk[1] * 4 + g) if xk else None)
        ybufs = [gT[:, 32:48, :], gT[:, 48:64, :]]
        macc = [self.xin[:, c * 512:(c + 1) * 512] for c in range(4)]
        yb_i = 0
        for dsl in range(D // 512):
            for b in range(3):
                yi = yb_i % 2
                yb_i += 1
                ybuf = ybufs[yi]
                if dsl == 0 or True:
                    src = yT_ap[b * 2048:(b + 1) * 2048, :].rearrange("(k p) n -> p k n", p=128)
                    for k0 in (0, 8):
                        s.op('pool', (lambda k0=k0, ybuf=ybuf, src=src: nc.gpsimd.dma_start(
                            out=ybuf[:, k0:k0 + 8, :], in_=src[:, k0:k0 + 8, :])),
                             writes=[('gT', 32 + 16 * yi + k) for k in range(k0, k0 + 8)], dma=True)
                ig = self.load_slab(wgate[:, b * D + dsl * 512: b * D + (dsl + 1) * 512], 32, 512)
                ib = self.load_slab(wbr[b, :, dsl * 512:(dsl + 1) * 512], 16, 512)
                sg, sbr = self.slabs[ig], self.slabs[ib]
                for c in range(4):
                    pg, py = self.next_ps(0, 6), self.next_ps(0, 6)
                    for k in range(32):
                        s.op('pe', (lambda k=k, c=c, pg=pg, sg=sg: nc.tensor.matmul(
                            self.cx.ps[pg][:], sg[:, k, c * 128:(c + 1) * 128], self.hT[:, k, :],
                            start=(k == 0), stop=(k == 31))),
                             reads=[('slab', ig, k)] + [('hT', k, g) for g in range(4)], writes=[('ps', pg)])
                    for k in range(16):
                        s.op('pe', (lambda k=k, c=c, py=py, sbr=sbr, ybuf=ybuf: nc.tensor.matmul(
                            self.cx.ps[py][:], sbr[:, k, c * 128:(c + 1) * 128], ybuf[:, k, :],
                            start=(k == 0), stop=(k == 15))),
                             reads=[('slab', ib, k), ('gT', 32 + 16 * yi + k)], writes=[('ps', py)])
                    ti = self.next_tmp()
                    tmp = self.tmpf[ti]
                    s.op('act', (lambda pg=pg, tmp=tmp: nc.scalar.activation(out=tmp[:], in_=self.cx.ps[pg][:], func=AF.Sigmoid)),
                         reads=[('ps', pg)], writes=[('tmpf', ti)])
                    if b == 0:
                        s.op('dve', (lambda py=py, tmp=tmp, c=c: nc.vector.tensor_tensor(
                            out=macc[c], in0=tmp[:], in1=self.cx.ps[py][:], op=ALU.mult)),
                             reads=[('ps', py), ('tmpf', ti)], writes=['xin'])
                    else:
                        s.op('dve', (lambda py=py, tmp=tmp: nc.vector.tensor_tensor(
                            out=tmp[:], in0=tmp[:], in1=self.cx.ps[py][:], op=ALU.mult)),
                             reads=[('ps', py), ('tmpf', ti)], writes=[('tmpf', ti)])
                        s.op('dve', (lambda tmp=tmp, c=c: nc.vector.tensor_tensor(
                            out=macc[c], in0=macc[c], in1=tmp[:], op=ALU.add)),
                             reads=[('tmpf', ti), 'xin'], writes=['xin'])
                    if b == 2:
                        mc = dsl * 4 + c
                        s.op('act', (lambda c=c, mc=mc: nc.scalar.copy(out=gT[:, mc, :], in_=macc[c])),
                             reads=['xin'], writes=[('gT', mc)])

    def wo_proj(self, x_ap, out_ap, wo, gT, xk=None, ok=None):
        nc, s = self.nc, self.s
        for dsl in range(D // 512):
            si = self.load_slab(wo[:, dsl * 512:(dsl + 1) * 512], 32, 512)
            slab = self.slabs[si]
            for g in range(4):
                pi = self.next_ps(0, 6)
                for k in range(32):
                    s.op('pe', (lambda g=g, k=k, slab=slab, pi=pi: nc.tensor.matmul(
                        self.cx.ps[pi][:], gT[:, k, g * 128:(g + 1) * 128], slab[:, k, :],
                        start=(k == 0), stop=(k == 31))),
                         reads=[('slab', si, k), ('gT', k)], writes=[('ps', pi)])
                self.residual_out(x_ap[g * 128:(g + 1) * 128, dsl * 512:(dsl + 1) * 512],
                                  out_ap[g * 128:(g + 1) * 128, dsl * 512:(dsl + 1) * 512], pi, 1.0,
                                  sk=('dr', xk[0], xk[1] * 4 + g, dsl) if xk else None,
                                  dk=('dr', ok[0], ok[1] * 4 + g, dsl) if ok else None)

    def ple(self, x_ap, out_ap, ln_ap, pT_ap, wpg, wpp, gT, xk=None, ok=None):
        nc, s = self.nc, self.s
        self.load_ln(ln_ap)
        for g in range(4):
            self.norm_group(x_ap[g * 128:(g + 1) * 128, :], g, rk=(xk[0], xk[1] * 4 + g) if xk else None)
        pT = gT[:, 32:34, :]
        s.op('pool', lambda: nc.gpsimd.dma_start(out=pT, in_=pT_ap.rearrange("(k p) n -> p k n", p=128)),
             writes=[('gT', 32), ('gT', 33)], dma=True)
        for dsl in range(D // 512):
            ig = self.load_slab(wpg[:, dsl * 512:(dsl + 1) * 512], 32, 512)
            ip = self.load_slab(wpp[:, dsl * 512:(dsl + 1) * 512], 2, 512)
            sg, sp_ = self.slabs[ig], self.slabs[ip]
            for g in range(4):
                pg, pp = self.next_ps(0, 6), self.next_ps(0, 6)
                for k in range(32):
                    s.op('pe', (lambda g=g, k=k, sg=sg, pg=pg: nc.tensor.matmul(
                        self.cx.ps[pg][:], self.hT[:, k, g * 128:(g + 1) * 128], sg[:, k, :],
                        start=(k == 0), stop=(k == 31))),
                         reads=[('slab', ig, k), ('hT', k, g)], writes=[('ps', pg)])
                for k in range(2):
                    s.op('pe', (lambda g=g, k=k, sp_=sp_, pp=pp: nc.tensor.matmul(
                        self.cx.ps[pp][:], pT[:, k, g * 128:(g + 1) * 128], sp_[:, k, :],
                        start=(k == 0), stop=(k == 1))),
                         reads=[('slab', ip, k), ('gT', 32 + k)], writes=[('ps', pp)])
                ti = self.next_tmp()
                tmp = self.tmpf[ti]
                s.op('act', (lambda pg=pg, tmp=tmp: nc.scalar.activation(out=tmp[:], in_=self.cx.ps[pg][:], func=AF.Sigmoid)),
                     reads=[('ps', pg)], writes=[('tmpf', ti)])
                s.op('dve', (lambda pp=pp, tmp=tmp: nc.vector.tensor_tensor(
                    out=tmp[:], in0=tmp[:], in1=self.cx.ps[pp][:], op=ALU.mult)),
                     reads=[('ps', pp), ('tmpf', ti)], writes=[('tmpf', ti)])
                ri = self.xres_i % 2
                self.xres_i += 1
                oi = self.xo_i % 2
                self.xo_i += 1
                xr, xo = self.xres[ri], self.xo[oi]
                xs = x_ap[g * 128:(g + 1) * 128, dsl * 512:(dsl + 1) * 512]
                ds = out_ap[g * 128:(g + 1) * 128, dsl * 512:(dsl + 1) * 512]
                s.op('sp', (lambda xr=xr, xs=xs: nc.sync.dma_start(out=xr[:], in_=xs)), writes=[('xres', ri)],
                     reads=([('dr', xk[0], xk[1] * 4 + g, dsl)] if xk else []), dma=True)
                s.op('dve', (lambda xr=xr, xo=xo, tmp=tmp: nc.vector.tensor_tensor(out=xo[:], in0=tmp[:], in1=xr[:], op=ALU.add)),
                     reads=[('tmpf', ti), ('xres', ri)], writes=[('xo', oi)])
                s.op('sp', (lambda xo=xo, ds=ds: nc.sync.dma_start(out=ds, in_=xo[:])), reads=[('xo', oi)],
                     writes=([('dr', ok[0], ok[1] * 4 + g, dsl)] if ok else []), dma=True)

    def final_norm(self, x_ap, out_ap, fn_ap, gT, ntok, xname=None):
        nc, s = self.nc, self.s
        fnb = gT[:, 0:16, :].rearrange("p a b -> p (a b)").bitcast(F32)
        s.op('sp', lambda: nc.sync.dma_start(out=fnb, in_=fn_ap.partition_broadcast(128)),
             writes=[('gT', k) for k in range(16)], dma=True)
        xin, stat = self.xin, self.stat
        for g in range(ntok // 128):
            xs = x_ap[g * 128:(g + 1) * 128, :]
            ds = out_ap[g * 128:(g + 1) * 128, :]
            s.op('sp', (lambda xs=xs: nc.sync.dma_start(out=xin[:], in_=xs)), writes=['xin'],
                 reads=([('dr', xname, g, c) for c in range(8)] if xname else []), dma=True)
            s.op('act', lambda: nc.scalar.activation(out=self.xn[:], in_=xin[:], func=AF.Square, accum_out=stat[:, 0:1]),
                 reads=['xin'], writes=['xn', 'stat0'])
            s.op('dve', lambda: nc.vector.tensor_scalar(out=stat[:, 1:2], in0=stat[:, 0:1], scalar1=1.0 / D, scalar2=EPS,
                                                        op0=ALU.mult, op1=ALU.add), reads=['stat0'], writes=['stat1'])
            s.op('act', lambda: nc.scalar.activation(out=stat[:, 2:3], in_=stat[:, 1:2], func=AF.Sqrt),
                 reads=['stat1'], writes=['stat2'])
            s.op('dve', lambda: nc.vector.reciprocal(out=stat[:, 3:4], in_=stat[:, 2:3]), reads=['stat2'], writes=['stat3'])
            s.op('dve', lambda: nc.vector.scalar_tensor_tensor(out=xin[:], in0=xin[:], scalar=stat[:, 3:4], in1=fnb,
                                                               op0=ALU.mult, op1=ALU.mult),
                 reads=['xin', 'stat3'] + [('gT', k) for k in range(16)], writes=['xin'])
            s.op('sp', (lambda ds=ds: nc.sync.dma_start(out=ds, in_=xin[:])), reads=['xin'], dma=True)

    def residual_out(self, xsrc_ap, dst_ap, pi, scale, sk=None, dk=None):
        nc, s = self.nc, self.s
        ri = self.xres_i % 2
        self.xres_i += 1
        oi = self.xo_i % 2
        self.xo_i += 1
        xr, xo = self.xres[ri], self.xo[oi]
        s.op('sp', lambda: nc.sync.dma_start(out=xr[:], in_=xsrc_ap), writes=[('xres', ri)],
             reads=([sk] if sk else []), dma=True)
        s.op('dve', lambda: nc.vector.scalar_tensor_tensor(out=xo[:], in0=self.cx.ps[pi][:], scalar=scale, in1=xr[:],
                                                           op0=ALU.mult, op1=ALU.add),
             reads=[('ps', pi), ('xres', ri)], writes=[('xo', oi)])
        s.op('sp', lambda: nc.sync.dma_start(out=dst_ap, in_=xo[:]), reads=[('xo', oi)],
             writes=([dk] if dk else []), dma=True)


def build_ffn_prog(ntok):
    nc = bass.Bass("TRN2", target_bir_lowering=False)
    x = dram_in(nc, "x", [ntok, D])
    ln = dram_in(nc, "ln", [128, 32])
    wg = dram_in(nc, "wg", [D, DFF])
    wu = dram_in(nc, "wu", [D, DFF])
    wd = dram_in(nc, "wd", [DFF, D])
    ident = dram_in(nc, "ident", [128, 128])
    y = dram_out(nc, "y", [ntok, D])
    cx = Ctx(nc)
    b = Blocks(cx, ident)
    gT = cx.sb([128, 64, 512], BF16, "gT")
    for t in range(ntok // 512):
        b.ffn(x[t * 512:(t + 1) * 512, :], y[t * 512:(t + 1) * 512, :], ln, wg, wu, wd, gT)
    cx.s.finish()
    return nc


def build_l3_prog(ntok, last):
    nc = bass.Bass("TRN2", target_bir_lowering=False)
    x = dram_in(nc, "x", [ntok, D])
    yT = dram_in(nc, "yT", [3 * 2048, ntok])
    pT = dram_in(nc, "pT", [PLE, ntok])
    ident = dram_in(nc, "ident", [128, 128])
    ln_mix = dram_in(nc, "ln_mix", [128, 32])
    wgate = dram_in(nc, "wgate", [D, 3 * D])
    wbr = dram_in(nc, "wbr", [3, 2048, D])
    wo = dram_in(nc, "wo", [D, D])
    ln1 = dram_in(nc, "ln1", [128, 32])
    wg1 = dram_in(nc, "wg1", [D, DFF]); wu1 = dram_in(nc, "wu1", [D, DFF]); wd1 = dram_in(nc, "wd1", [DFF, D])
    ln_ple = dram_in(nc, "ln_ple", [128, 32])
    wpg = dram_in(nc, "wpg", [D, D]); wpp = dram_in(nc, "wpp", [PLE, D])
    if last:
        fn = dram_in(nc, "fn", [D])
    else:
        ln2 = dram_in(nc, "ln2", [128, 32])
        wg2 = dram_in(nc, "wg2", [D, DFF]); wu2 = dram_in(nc, "wu2", [D, DFF]); wd2 = dram_in(nc, "wd2", [DFF, D])
    y = dram_out(nc, "y", [ntok, D])
    x2 = dram_tmp(nc, "x2", [ntok, D]); x3 = dram_tmp(nc, "x3", [ntok, D]); x4 = dram_tmp(nc, "x4", [ntok, D])
    cx = Ctx(nc)
    b = Blocks(cx, ident)
    gT = cx.sb([128, 64, 512], BF16, "gT")
    for t in range(ntok // 512):
        sl = slice(t * 512, (t + 1) * 512)
        b.mix_merge(x[sl, :], yT[:, sl], ln_mix, wgate, wbr, gT, xk=('x', t))
        b.wo_proj(x[sl, :], x2[sl, :], wo, gT, xk=('x', t), ok=('x2', t))
        b.ffn(x2[sl, :], x3[sl, :], ln1, wg1, wu1, wd1, gT, xk=('x2', t), ok=('x3', t))
        b.ple(x3[sl, :], x4[sl, :], ln_ple, pT[:, sl], wpg, wpp, gT, xk=('x3', t), ok=('x4', t))
        if not last:
            b.ffn(x4[sl, :], y[sl, :], ln2, wg2, wu2, wd2, gT, xk=('x4', t), ok=('y', t))
    if last:
        b.final_norm(x4, y, fn, gT, ntok, xname='x4')
    cx.s.finish()
    return nc


def _pT(v):
    return np.ascontiguousarray(np.asarray(v, np.float32).reshape(-1, 128).T)


def _launch(nc, in_maps):
    res = run_bass_kernel_spmd(nc, in_maps, core_ids=list(range(NCORES)))
    return res.results


def _c(a):
    return np.ascontiguousarray(np.asarray(a, np.float32))


def kernel(**inputs):
    TS = NB * SEQ // NCORES
    x = _c(inputs["x"]).reshape(NB * SEQ, D)
    ident = np.eye(128, dtype=np.float32)
    pos = np.asarray(inputs["positions"]).astype(np.int32)
    nc1 = build_ffn_prog(TS)
    w = {"ln": _pT(inputs["ln_ffn"][0, 0]), "wg": _c(inputs["ffn_w_gate"][0, 0]), "wu": _c(inputs["ffn_w_up"][0, 0]),
         "wd": _c(inputs["ffn_w_down"][0, 0]), "ident": ident}
    res = _launch(nc1, [dict(w, x=x[c * TS:(c + 1) * TS]) for c in range(NCORES)])
    x1 = np.concatenate([np.asarray(r["y"], np.float32) for r in res], axis=0)
    del w
    for i in range(DEPTH):
        last = (i == DEPTH - 1)
        w_in = np.asarray(inputs["w_in"][i], np.float32)
        nc2 = build_l2_prog(SEQ, i)
        ln_mix = _pT(inputs["ln_mix"][i])
        groups = []
        for g in range(4):
            d = l2_consts(inputs, i, g)
            d["wsl"] = l2_weight_slabs(w_in, g)
            d["ln"] = ln_mix
            groups.append(d)
        in_maps = []
        for c in range(NCORES):
            b, g = c // 4, c % 4
            in_maps.append(dict(groups[g], x=x1[b * SEQ:(b + 1) * SEQ], pos=np.ascontiguousarray(pos[b])))
        res = _launch(nc2, in_maps)
        del groups, in_maps
        yfull = np.zeros((NB, SEQ, 3 * 2048), np.float32)
        for c in range(NCORES):
            b, g = c // 4, c % 4
            yc = np.asarray(res[c]["y"], np.float32)
            yfull[b, :, g * 512:(g + 1) * 512] = yc[:, 0:512]
            yfull[b, :, 2048 + g * 512:2048 + (g + 1) * 512] = yc[:, 512:1024]
            yfull[b, :, 4096 + g * 512:4096 + (g + 1) * 512] = yc[:, 1024:1536]
        nc3 = build_l3_prog(TS, last)
        w = {"ident": ident, "ln_mix": ln_mix, "wgate": np.ascontiguousarray(w_in[:, O_GATE:]),
             "wbr": _c(inputs["w_branch"][i]), "wo": _c(inputs["w_out"][i]),
             "ln1": _pT(inputs["ln_ffn"][i, 1]), "wg1": _c(inputs["ffn_w_gate"][i, 1]), "wu1": _c(inputs["ffn_w_up"][i, 1]),
             "wd1": _c(inputs["ffn_w_down"][i, 1]), "ln_ple": _pT(inputs["ln_ple"][i]),
             "wpg": _c(inputs["w_ple_gate"][i]), "wpp": _c(inputs["w_ple_proj"][i])}
        if last:
            w["fn"] = _c(inputs["final_norm"])
        else:
            w.update({"ln2": _pT(inputs["ln_ffn"][i + 1, 0]), "wg2": _c(inputs["ffn_w_gate"][i + 1, 0]),
                      "wu2": _c(inputs["ffn_w_up"][i + 1, 0]), "wd2": _c(inputs["ffn_w_down"][i + 1, 0])})
        del w_in
        p_i = np.asarray(inputs["p"][i], np.float32)
        in_maps = []
        for c in range(NCORES):
            b, t0 = c // 4, (c % 4) * TS
            in_maps.append(dict(w, x=x1[c * TS:(c + 1) * TS],
                                yT=np.ascontiguousarray(yfull[b, t0:t0 + TS, :].T),
                                pT=np.ascontiguousarray(p_i[b, t0:t0 + TS, :].T)))
        res = _launch(nc3, in_maps)
        x1 = np.concatenate([np.asarray(r["y"], np.float32) for r in res], axis=0)
        del w, in_maps, yfull
    return x1.reshape(NB, SEQ, D)


NFM = 36
NTM = 2064
YW = 1536
TWO_PI = 2.0 * math.pi


def l2_slab_plan():
    plan = []
    for j in range(2):
        plan.append([('fm', c * 128, j * 8 + c) for c in range(4)])
        plan.append([('fm', c * 128, j * 8 + 4 + c) for c in range(4)])
        plan.append([('tm', 0, 256, j * 256)])
    for j in range(2):
        plan.append([('fm', c * 128, 16 + j * 4 + c) for c in range(4)])
        plan.append([('tm', 0, 512, 512 + j * 512)])
    for j in range(4):
        plan.append([('fm', c * 128, 24 + j * 3 + c) for c in range(3)] + [('tm', 384, 128, 1536 + j * 128)])
    plan.append([('tm', 0, 16, 2048)])
    return plan


class Mix:
    def __init__(self, cx, S):
        self.cx, self.nc, self.s, self.S = cx, cx.nc, cx.s, S
        self.bi = {'d': 0, 'a': 0}

    def bank(self, kind='d'):
        ch = getattr(self, 'chain', None)
        if ch is not None:
            if kind == 'd':
                i = ch * 3 + self.bi['d'] % 3
            else:
                i = 6 + ch
        elif kind == 'd':
            i = self.bi['d'] % 6
        else:
            i = 6 + self.bi['a'] % 2
        self.bi[kind] += 1
        return i

    def mm(self, out, lhsT, rhs, r, w, start=True, stop=True):
        nc = self.nc
        self.s.op('pe', lambda: nc.tensor.matmul(out, lhsT, rhs, start=start, stop=stop), r, w)

    def tr(self, out, in_, ident, r, w):
        nc = self.nc
        self.s.op('pe', lambda: nc.tensor.transpose(out=out, in_=in_, identity=ident), r, w)

    def tt(self, out, in0, in1, op, r, w, eng='dve'):
        e = self.nc.vector if eng == 'dve' else self.nc.gpsimd
        self.s.op(eng, lambda: e.tensor_tensor(out=out, in0=in0, in1=in1, op=op), r, w)

    def ts(self, out, in0, s1, op0, r, w, s2=None, op1=None):
        nc = self.nc
        if op1 is None:
            self.s.op('dve', lambda: nc.vector.tensor_scalar(out=out, in0=in0, scalar1=s1, scalar2=None, op0=op0), r, w)
        else:
            self.s.op('dve', lambda: nc.vector.tensor_scalar(out=out, in0=in0, scalar1=s1, scalar2=s2, op0=op0, op1=op1), r, w)

    def stt(self, out, in0, scalar, in1, op0, op1, r, w):
        nc = self.nc
        self.s.op('dve', lambda: nc.vector.scalar_tensor_tensor(out=out, in0=in0, scalar=scalar, in1=in1, op0=op0, op1=op1), r, w)

    def act(self, out, in_, func, r, w, scale=None, bias=None, accum=None):
        nc = self.nc
        kw = {}
        if scale is not None:
            kw['scale'] = scale
        if bias is not None:
            kw['bias'] = bias
        if accum is not None:
            kw['accum_out'] = accum
        self.s.op('act', lambda: nc.scalar.activation(out=out, in_=in_, func=func, **kw), r, w)

    def vcopy(self, out, in_, r, w):
        nc = self.nc
        self.s.op('dve', lambda: nc.vector.tensor_copy(out=out, in_=in_), r, w)

    def memset(self, ap, val, w, r=()):
        nc = self.nc
        self.s.op('dve', lambda: nc.vector.memset(ap, val), r, w)

    def recip(self, out, in_, r, w):
        nc = self.nc
        self.s.op('dve', lambda: nc.vector.reciprocal(out=out, in_=in_), r, w)

    def dma(self, out, in_, r, w, q='sp'):
        nc = self.nc
        e = {'sp': nc.sync, 'pool': nc.gpsimd, 'act': nc.scalar}[q]
        self.s.op(q, lambda: e.dma_start(out=out, in_=in_), r, w, dma=True)

    def phase0(self, b, x, ln_ap, hTd):
        cx, S = self.cx, self.S
        b.load_ln(ln_ap)
        for t in range(S // 512):
            for g in range(4):
                b.norm_group(x[t * 512 + g * 128: t * 512 + (g + 1) * 128, :], g)
            self.dma(hTd[t], b.hT[:].rearrange("p k n -> p (k n)"), b.hT_keys(), [('hTd', t)])

    def rope_tables(self, pos_ap, tab_ap, cosd, sind):
        cx, S = self.cx, self.S
        cx.push()
        tab = cx.sb([128, 4], F32, "tab")
        self.dma(tab[:], tab_ap, [], ['tab'])
        posi = cx.sb([128, S], I32, "posi")
        u = cx.sb([128, S], F32, "u")
        kf = cx.sb([128, S], F32, "kf")
        ki = cx.sb([128, S], I32, "ki")
        m = cx.sb([128, S], F32, "m")
        self.dma(posi[:], pos_ap.partition_broadcast(128), [], ['posi'])
        self.vcopy(u[:], posi[:], ['posi'], ['u'])
        self.ts(u[:], u[:], tab[:, 0:1], ALU.mult, ['u', 'tab'], ['u'])
        for which, shift, dst in (('sin', 0.0, sind), ('cos', 0.25, cosd)):
            self.ts(kf[:], u[:], 1.0 / TWO_PI, ALU.mult, ['u'], ['kf'], s2=shift, op1=ALU.add)
            self.vcopy(ki[:], kf[:], ['kf'], ['ki'])
            self.vcopy(m[:], ki[:], ['ki'], ['m'])
            self.tt(kf[:], kf[:], m[:], ALU.subtract, ['kf', 'm'], ['kf'])
            self.ts(m[:], kf[:], 0.5, ALU.is_gt, ['kf'], ['m'])
            self.tt(kf[:], kf[:], m[:], ALU.subtract, ['kf', 'm'], ['kf'])
            self.ts(m[:], kf[:], -0.5, ALU.is_lt, ['kf'], ['m'])
            self.tt(kf[:], kf[:], m[:], ALU.add, ['kf', 'm'], ['kf'])
            self.act(m[:], kf[:], AF.Sin, ['kf'], ['m'], scale=6.283185)
            if which == 'sin':
                self.ts(m[:], m[:], tab[:, 1:2], ALU.mult, ['m', 'tab'], ['m'])
            self.dma(dst, m[:], ['m'], [which + 'd'])
        cx.pop()

    def phase1(self, b, wsl, hTd, Pfm, Ptm):
        cx, S, nc = self.cx, self.S, self.nc
        plan = l2_slab_plan()
        hbufs = [(b.hT, 'hT'), (cx.sb([128, 32, 512], BF16, "hT2"), 'hT2')]
        ev = 0
        it = 0
        for si_, spec in enumerate(plan):
            ncols = max((e[1] + (128 if e[0] == 'fm' else e[2])) for e in spec)
            si = b.load_slab(wsl[si_, :, 0:ncols], 32, ncols)
            slab = b.slabs[si]
            for t in range(S // 512):
                hT, hk = hbufs[it % 2]
                it += 1
                self.dma(hT[:].rearrange("p k n -> p (k n)"), hTd[t], [('hTd', t)],
                         [(hk, c, g) for c in range(32) for g in range(4)], q='pool')
                for e in spec:
                    if e[0] == 'fm':
                        _, c0, ch = e
                        pi = self.bank('d' if ev % 2 == 0 else 'a')
                        for k in range(32):
                            self.mm(cx.ps[pi][:], slab[:, k, c0:c0 + 128], hT[:, k, :],
                                    [('slab', si, k)] + [(hk, k, g) for g in range(4)], [('ps', pi)],
                                    start=(k == 0), stop=(k == 31))
                        ti = b.next_tmp()
                        tmp = b.tmpf[ti]
                        if ev % 2 == 0:
                            self.vcopy(tmp[:], cx.ps[pi][:], [('ps', pi)], [('tmpf', ti)])
                        else:
                            self.act(tmp[:], cx.ps[pi][:], AF.Copy, [('ps', pi)], [('tmpf', ti)])
                        ev += 1
                        self.dma(Pfm[ch, :, t * 512:(t + 1) * 512], tmp[:], [('tmpf', ti)], [('Pfm', ch, t)])
                    else:
                        _, c0, n, d0 = e
                        for g in range(4):
                            pi = self.bank('d' if ev % 2 == 0 else 'a')
                            for k in range(32):
                                self.mm(cx.ps[pi][:, 0:n], hT[:, k, g * 128:(g + 1) * 128], slab[:, k, c0:c0 + n],
                                        [('slab', si, k), (hk, k, g)], [('ps', pi)],
                                        start=(k == 0), stop=(k == 31))
                            ti = b.next_tmp()
                            tmp = b.tmpf[ti]
                            if ev % 2 == 0:
                                self.vcopy(tmp[:, 0:n], cx.ps[pi][:, 0:n], [('ps', pi)], [('tmpf', ti)])
                            else:
                                self.act(tmp[:, 0:n], cx.ps[pi][:, 0:n], AF.Copy, [('ps', pi)], [('tmpf', ti)])
                            ev += 1
                            r0 = t * 512 + g * 128
                            self.dma(Ptm[r0:r0 + 128, d0:d0 + n], tmp[:, 0:n], [('tmpf', ti)], [('Ptm', d0, r0 // 128)])

    def rope_load(self, dst, chx, chp, Pfm, cos, sin, R1, R2, tag):
        self.dma(R1[:], Pfm[chx], [], ['R1'])
        self.dma(R2[:], Pfm[chp], [], ['R2'], q='act')
        self.tt(R1[:], R1[:], cos[:], ALU.mult, ['R1', 'cos'], ['R1'])
        self.tt(R2[:], R2[:], sin[:], ALU.mult, ['R2', 'sin'], ['R2'], eng='pool')
        self.tt(dst, R1[:], R2[:], ALU.add, ['R1', 'R2'], [tag])

    def rms_rows(self, stat, src, n, r, extra_scale=1.0):
        cx = self.cx
        junk = self.junk
        self.act(junk[:, 0:n], src, AF.Square, r, ['junk', 'st0'], accum=stat[:, 0:1])
        self.ts(stat[:, 1:2], stat[:, 0:1], extra_scale * extra_scale / n, ALU.mult, ['st0'], ['st1'], s2=EPS, op1=ALU.add)
        self.act(stat[:, 2:3], stat[:, 1:2], AF.Sqrt, ['st1'], ['st2'])
        self.recip(stat[:, 3:4], stat[:, 2:3], ['st2'], ['st3'])
        if extra_scale != 1.0:
            self.ts(stat[:, 3:4], stat[:, 3:4], extra_scale, ALU.mult, ['st3'], ['st3'])

    def diff_phase(self, Pfm, Ptm, cosd, sind, dl_ap, sub_ap, yout, lam_init):
        cx, S, nc = self.cx, self.S, self.nc
        NQ, NKB = S // 512, S // 128
        cx.push()
        cos = cx.sb([128, S], F32, "cos"); sin = cx.sb([128, S], F32, "sin")
        R1 = cx.sb([128, S], F32, "R1"); R2 = cx.sb([128, S], F32, "R2")
        qk = [cx.sb([128, S], BF16, f"qk{i}") for i in range(4)]
        V = cx.sb([128, NKB, 258], BF16, "Vext")
        E = [cx.sb([128, 512], BF16, f"E{i}") for i in range(3)]
        dl = cx.sb([128, 512], F32, "dl"); sub = cx.sb([128, 256], F32, "sub")
        st = cx.sb([128, 16], F32, "dst"); self.junk = cx.sb([128, 256], F32, "junk")
        o0 = [cx.sb([128, 256], F32, f"o0_{i}") for i in range(4)]
        o1 = [cx.sb([128, 256], F32, f"o1_{i}") for i in range(2)]
        self.dma(cos[:], cosd, [], ['cos']); self.dma(sin[:], sind, [], ['sin'])
        self.dma(dl[:], dl_ap.partition_broadcast(128), [], ['dl'])
        self.dma(sub[:], sub_ap.partition_broadcast(128), [], ['sub'])
        self.tt(self.junk[:, 0:128], dl[:, 0:128], dl[:, 128:256], ALU.mult, ['dl'], ['junk'])
        self.s.op('dve', lambda: nc.vector.reduce_sum(out=st[:, 4:5], in_=self.junk[:, 0:128], axis=mybir.AxisListType.X), ['junk'], ['l4'])
        self.tt(self.junk[:, 128:256], dl[:, 256:384], dl[:, 384:512], ALU.mult, ['dl'], ['junk'])
        self.s.op('dve', lambda: nc.vector.reduce_sum(out=st[:, 5:6], in_=self.junk[:, 128:256], axis=mybir.AxisListType.X), ['junk'], ['l5'])
        self.act(st[:, 6:7], st[:, 4:5], AF.Exp, ['l4'], ['l6'])
        self.act(st[:, 7:8], st[:, 5:6], AF.Exp, ['l5'], ['l7'])
        self.tt(st[:, 8:9], st[:, 7:8], st[:, 6:7], ALU.subtract, ['l6', 'l7'], ['l8'])
        self.ts(st[:, 8:9], st[:, 8:9], -lam_init, ALU.add, ['l8'], ['l8'])
        scale = 128.0 ** -0.5
        ei = 0
        for j in range(2):
            base = j * 8
            self.rope_load(qk[0][:], base + 0, base + 1, Pfm, cos, sin, R1, R2, 'qk0')
            self.rope_load(qk[1][:], base + 2, base + 3, Pfm, cos, sin, R1, R2, 'qk1')
            self.rope_load(qk[2][:], base + 4, base + 5, Pfm, cos, sin, R1, R2, 'qk2')
            self.rope_load(qk[3][:], base + 6, base + 7, Pfm, cos, sin, R1, R2, 'qk3')
            self.dma(V[:, :, 0:256], Ptm[:, j * 256:(j + 1) * 256].rearrange("(kb p) c -> p kb c", p=128), [], ['V'], q='pool')
            self.memset(V[:, :, 256:257], 1.0, ['V1'])
            for qb in range(NQ):
                for t in range(2):
                    accs = [0, 1, 2, 3]
                    qT, kT = qk[t], qk[2 + t]
                    pend = []
                    def pv(kb_, e_):
                        for qs in range(4):
                            self.mm(cx.ps[accs[qs]][:, 0:257], E[e_][:, qs * 128:(qs + 1) * 128], V[:, kb_, 0:257],
                                    [('E', e_), 'V', 'V1'], [('ps', accs[qs])], start=(kb_ == 0), stop=(kb_ == NKB - 1))
                    for kb in range(NKB):
                        ps = 4 + (ei % 4)
                        self.mm(cx.ps[ps][:], kT[:, kb * 128:(kb + 1) * 128], qT[:, qb * 512:(qb + 1) * 512],
                                [f'qk{t}', f'qk{2 + t}'], [('ps', ps)])
                        e = ei % 3
                        ei += 1
                        self.act(E[e][:], cx.ps[ps][:], AF.Exp, [('ps', ps)], [('E', e)], scale=scale)
                        pend.append((kb, e))
                        if len(pend) > 2:
                            pv(*pend.pop(0))
                    for it_ in pend:
                        pv(*it_)
                    for qs in range(4):
                        acc = cx.ps[accs[qs]]
                        self.recip(st[:, 9:10], acc[:, 256:257], [('ps', accs[qs])], ['r0'])
                        if t == 0:
                            self.ts(o0[qs][:], acc[:, 0:256], st[:, 9:10], ALU.mult, [('ps', accs[qs]), 'r0'], [('o0', qs)])
                        else:
                            oo = o1[qs % 2]
                            self.ts(oo[:], acc[:, 0:256], st[:, 9:10], ALU.mult, [('ps', accs[qs]), 'r0'], [('o1', qs % 2)])
                            self.stt(oo[:], oo[:], st[:, 8:9], o0[qs][:], ALU.mult, ALU.add,
                                     [('o1', qs % 2), ('o0', qs), 'l8'], [('o1', qs % 2)])
                            self.rms_rows(st, oo[:], 256, [('o1', qs % 2)])
                            self.ts(oo[:], oo[:], st[:, 3:4], ALU.mult, [('o1', qs % 2), 'st3'], [('o1', qs % 2)],
                                    s2=(1.0 - lam_init), op1=ALU.mult)
                            self.tt(oo[:], oo[:], sub[:], ALU.mult, [('o1', qs % 2), 'sub'], [('o1', qs % 2)])
                            r0 = qb * 512 + qs * 128
                            self.dma(yout[r0:r0 + 128, 1024 + j * 256:1024 + (j + 1) * 256], oo[:], [('o1', qs % 2)], [])
        cx.pop()

    def ret_phase(self, Pfm, Ptm, cosd, sind, lng_ap, yout):
        cx, S, nc = self.cx, self.S, self.nc
        NQ, NKB = S // 512, S // 128
        GW = 2 * S - 128
        OFF = S - 128
        cx.push()
        cos = cx.sb([128, S], F32, "cos"); sin = cx.sb([128, S], F32, "sin")
        R1 = cx.sb([128, S], F32, "R1"); R2 = cx.sb([128, S], F32, "R2")
        qT = cx.sb([128, S], BF16, "rq"); kT = cx.sb([128, S], BF16, "rk")
        V = cx.sb([128, NKB, 256], BF16, "rV")
        G = cx.sb([128, GW], F32, "G")
        E = [cx.sb([128, 512], BF16, f"E{i}") for i in range(3)]
        lng = cx.sb([128, 2], F32, "lng")
        st = cx.sb([128, 16], F32, "rst"); self.junk = cx.sb([128, 256], F32, "junk")
        oo = [cx.sb([128, 256], F32, f"ro{i}") for i in range(2)]
        gg = [cx.sb([128, 256], F32, f"rg{i}") for i in range(2)]
        self.dma(cos[:], cosd, [], ['cos']); self.dma(sin[:], sind, [], ['sin'])
        self.dma(lng[:], lng_ap.partition_broadcast(128), [], ['lng'])
        ei = 0
        oi = 0
        for j in range(2):
            base = 16 + j * 4
            self.rope_load(qT[:], base + 0, base + 1, Pfm, cos, sin, R1, R2, 'rq')
            self.rope_load(kT[:], base + 2, base + 3, Pfm, cos, sin, R1, R2, 'rk')
            c0 = 512 + j * 512
            self.dma(V[:], Ptm[:, c0:c0 + 256].rearrange("(kb p) c -> p kb c", p=128), [], ['V'], q='pool')
            self.s.op('pool', lambda: nc.gpsimd.iota(G[:], pattern=[[1, GW]], base=-OFF, channel_multiplier=-1,
                                                     allow_small_or_imprecise_dtypes=True), [], ['G'])
            self.stt(G[:], G[:], -1.0, G[:], ALU.mult, ALU.max, ['G'], ['G'])
            self.act(G[:], G[:], AF.Exp, ['G', 'lng'], ['G'], scale=lng[:, j:j + 1])
            for qb in range(NQ):
                accs = [0, 1, 2, 3]
                pend = []
                def pv(kb_, e_):
                    for qs in range(4):
                        self.mm(cx.ps[accs[qs]][:, 0:256], E[e_][:, qs * 128:(qs + 1) * 128], V[:, kb_, :],
                                [('E', e_), 'V'], [('ps', accs[qs])], start=(kb_ == 0), stop=(kb_ == NKB - 1))
                for kb in range(NKB):
                    ps = 4 + (ei % 4)
                    self.mm(cx.ps[ps][:], kT[:, kb * 128:(kb + 1) * 128], qT[:, qb * 512:(qb + 1) * 512],
                            ['rq', 'rk'], [('ps', ps)])
                    e = ei % 3
                    ei += 1
                    g0 = qb * 512 - kb * 128 + OFF
                    self.tt(E[e][:], cx.ps[ps][:], G[:, g0:g0 + 512], ALU.mult, [('ps', ps), 'G'], [('E', e)])
                    pend.append((kb, e))
                    if len(pend) > 2:
                        pv(*pend.pop(0))
                for it_ in pend:
                    pv(*it_)
                for qs in range(4):
                    acc = cx.ps[accs[qs]]
                    o = oo[oi % 2]; gt = gg[oi % 2]; ok_ = ('ro', oi % 2); gk = ('rg', oi % 2)
                    oi += 1
                    self.act(o[:], acc[:, 0:256], AF.Copy, [('ps', accs[qs])], [ok_])
                    self.rms_rows(st, o[:], 256, [ok_], extra_scale=128.0 ** -0.5)
                    r0 = qb * 512 + qs * 128
                    self.dma(gt[:], Ptm[r0:r0 + 128, c0 + 256:c0 + 512], [], [gk])
                    self.act(gt[:], gt[:], AF.Silu, [gk], [gk])
                    self.stt(o[:], o[:], st[:, 3:4], gt[:], ALU.mult, ALU.mult, [ok_, gk, 'st3'], [ok_])
                    self.dma(yout[r0:r0 + 128, j * 256:(j + 1) * 256], o[:], [ok_], [])
        cx.pop()

    def gdn_phase(self, Pfm, Ptm, cw_ap, alog_ap, dt_ap, gnw_ap, masks_ap, ident_ap, yout):
        cx, S, nc = self.cx, self.S, self.nc
        NDC = S // 128
        NC4 = NDC * 4
        cx.push()
        MK = cx.sb([128, 9, 128], F32, "MK")
        IDF = cx.sb([128, 128], F32, "IDF"); ONES = cx.sb([128, 128], F32, "ONES")
        cw = cx.sb([128, 60], F32, "cw"); alog = cx.sb([128, 8], F32, "alog"); dtb = cx.sb([128, 8], F32, "dtb")
        nega = cx.sb([128, 8], F32, "nega"); gnw = cx.sb([128, 128], F32, "gnw")
        AB = cx.sb([128, NDC, 16], F32, "AB"); Gm = cx.sb([128, NDC, 8], F32, "Gm"); Bm = cx.sb([128, NDC, 8], F32, "Bm")
        tA = cx.sb([128, NDC], F32, "tA")
        def gt(name):
            return [cx.sb([128, NDC, 4], F32, f"{name}{d}") for d in range(2)]
        GC, GLb, GL0, GL1, EG, EK0, EK1, EL0, EL1, BE = (gt(n) for n in
                                                       ("GC", "GLb", "GL0", "GL1", "EG", "EK0", "EK1", "EL0", "EL1", "BE"))
        R = cx.sb([128, S + 4], F32, "R"); X = cx.sb([128, S], F32, "X")
        Q = cx.sb([128, S], F32, "Q"); Kf = cx.sb([128, S], F32, "Kf")
        Ktm = cx.sb([128, NDC, 128], F32, "Ktm"); Vtm = cx.sb([128, NDC, 128], F32, "Vtm")
        Od = [cx.sb([128, NDC, 128], F32, f"O{d}") for d in range(2)]
        st = cx.sb([128, 16], F32, "gst"); self.junk = cx.sb([128, 256], F32, "junk")
        sq = [cx.sb([128, 512], F32, f"sq{i}") for i in range(2)]
        names = ("A", "B", "P0", "P1", "Q0", "Q1", "TT", "Tm", "DG", "DEC", "DQ", "QKT", "VB", "KBG", "KO0", "KO1",
                 "U", "WT", "VN", "OA", "Sst", "zt", "ot")
        Td = [{n: cx.sb([128, 128], F32, f"g{n}{d}") for n in names} for d in range(2)]
        local = set(names)

        self.dma(MK[:], masks_ap.rearrange("m p f -> p m f"), [], ['MK'])
        self.dma(IDF[:], ident_ap, [], ['IDF'])
        self.memset(ONES[:], 1.0, ['ONES'])
        self.dma(cw[:], cw_ap, [], ['cw'])
        self.dma(alog[:], alog_ap.partition_broadcast(128), [], ['alog'])
        self.dma(dtb[:], dt_ap.partition_broadcast(128), [], ['dtb'])
        self.dma(gnw[:], gnw_ap.partition_broadcast(128), [], ['gnw'])
        self.dma(AB[:], Ptm[:, 2048:2064].rearrange("(dc p) c -> p dc c", p=128), [], ['AB'])
        self.act(nega[:], alog[:], AF.Exp, ['alog'], ['nega'])
        self.ts(nega[:], nega[:], -1.0, ALU.mult, ['nega'], ['nega'])
        for c in range(8):
            self.act(tA[:], AB[:, :, c], AF.Exp, ['AB', 'dtb'], ['tA'], bias=dtb[:, c:c + 1])
            self.ts(tA[:], tA[:], 1.0, ALU.add, ['tA'], ['tA'])
            self.act(tA[:], tA[:], AF.Ln, ['tA'], ['tA'])
            self.ts(Gm[:, :, c], tA[:], nega[:, c:c + 1], ALU.mult, ['tA', 'nega'], ['Gm'])
            self.act(Bm[:, :, c], AB[:, :, 8 + c], AF.Sigmoid, ['AB'], ['Bm'])
        rm0, rm1 = MK[:, 3, 0:1], MK[:, 4, 0:1]
        for d in range(2):
            rhs = Gm[:, :, 4 * d:4 * d + 4]
            for (mi, dst, nm) in ((d, GC[d], 'GC'), (2, GLb[d], 'GLb'), (3, GL0[d], 'GL0'), (4, GL1[d], 'GL1')):
                pi = self.bank('d')
                self.mm(cx.ps[pi][:, 0:NC4], MK[:, mi, :], rhs, ['MK', 'Gm'], [('ps', pi)])
                self.vcopy(dst[:].rearrange("p a b -> p (a b)"), cx.ps[pi][:, 0:NC4], [('ps', pi)], [(nm, d)])
            fl = lambda t_: t_[:].rearrange("p a b -> p (a b)")
            self.act(fl(EG[d]), fl(GC[d]), AF.Exp, [('GC', d)], [('EG', d)])
            self.tt(fl(EK0[d]), fl(GLb[d]), fl(GC[d]), ALU.subtract, [('GLb', d), ('GC', d)], [('EK0', d)])
            self.act(fl(EK0[d]), fl(EK0[d]), AF.Exp, [('EK0', d)], [('EK0', d)])
            self.ts(fl(EK1[d]), fl(EK0[d]), rm1, ALU.mult, [('EK0', d), 'MK'], [('EK1', d)])
            self.ts(fl(EK0[d]), fl(EK0[d]), rm0, ALU.mult, [('EK0', d), ('EK1', d), 'MK'], [('EK0', d)])
            self.act(fl(EL0[d]), fl(GL0[d]), AF.Exp, [('GL0', d)], [('EL0', d)])
            self.act(fl(EL1[d]), fl(GL1[d]), AF.Exp, [('GL1', d)], [('EL1', d)])
            self.tt(BE[d][:], Bm[:, :, 4 * d:4 * d + 4], EG[d][:], ALU.mult, ['Bm', ('EG', d)], [('BE', d)])
        self.memset(R[:, 0:2], 0.0, ['Rpad'])
        self.memset(R[:, S + 2:S + 4], 0.0, ['Rpad'])

        def conv_silu(j, r, dst, dk_):
            self.dma(R[:, 2:S + 2], Pfm[24 + j * 3 + r], [dk_], ['R'])
            w0 = (j * 3 + r) * 5
            self.ts(dst[:], R[:, 0:S], cw[:, w0:w0 + 1], ALU.mult, ['R', 'Rpad', 'cw'], [dk_])
            for t in range(1, 5):
                self.stt(dst[:], R[:, t:S + t], cw[:, w0 + t:w0 + t + 1], dst[:], ALU.mult, ALU.add, ['R', 'Rpad', 'cw', dk_], [dk_])
            self.act(dst[:], dst[:], AF.Silu, [dk_], [dk_])

        def l2norm(dst, dk_, scale):
            for t in range(S // 512):
                sl = slice(t * 512, (t + 1) * 512)
                sqt = sq[t % 2]
                self.act(sqt[:], dst[:, sl], AF.Square, [dk_], [('sq', t % 2)])
                pi = self.bank('d')
                self.mm(cx.ps[pi][:], ONES[:], sqt[:], ['ONES', ('sq', t % 2)], [('ps', pi)])
                self.ts(sqt[:], cx.ps[pi][:], EPS, ALU.add, [('ps', pi)], [('sq', t % 2)])
                self.act(sqt[:], sqt[:], AF.Sqrt, [('sq', t % 2)], [('sq', t % 2)])
                self.recip(sqt[:], sqt[:], [('sq', t % 2)], [('sq', t % 2)])
                self.stt(dst[:, sl], dst[:, sl], scale, sqt[:], ALU.mult, ALU.mult, [dk_, ('sq', t % 2)], [dk_])

        def to_tm(src, sk_, dst, dk_):
            for d4 in range(NDC // 4):
                pi = self.bank('d')
                for i in range(4):
                    dc = d4 * 4 + i
                    self.tr(cx.ps[pi][:, i * 128:(i + 1) * 128], src[:, dc * 128:(dc + 1) * 128], IDF[:], [sk_, 'IDF'], [('ps', pi)])
                self.vcopy(dst[:, d4 * 4:(d4 + 1) * 4, :].rearrange("p a b -> p (a b)"), cx.ps[pi][:], [('ps', pi)], [dk_])

        def mmsb(dst, lhsT, rhs, r, w, acc=None, accop=None):
            pi = self.bank('d')
            self.mm(cx.ps[pi][:, 0:128], lhsT, rhs, r, [('ps', pi)])
            if acc is None:
                self.vcopy(dst, cx.ps[pi][:, 0:128], [('ps', pi)], w)
            else:
                self.tt(dst, acc, cx.ps[pi][:, 0:128], accop, [('ps', pi)] + r[:0] + w, w)

        for j in range(4):
            conv_silu(j, 2, X, 'X')
            to_tm(X, 'X', Vtm, 'Vtm')
            conv_silu(j, 0, Q, 'Q')
            l2norm(Q, 'Q', 128.0 ** -0.5)
            conv_silu(j, 1, Kf, 'Kf')
            l2norm(Kf, 'Kf', 1.0)
            to_tm(Kf, 'Kf', Ktm, 'Ktm')
            chains = []
            for d in range(2):
                T = Td[d]
                O = Od[d]
                saved_ops = self.s.ops
                self.s.ops = []
                self.chain = d
                self.s.keyfn = (lambda k, d=d: (k, d) if (isinstance(k, str) and k in local) else
                                (('O', d, k[1]) if (isinstance(k, tuple) and k[0] == 'O') else k))
                col = slice(j, j + 1)
                self.memset(T['Sst'][:], 0.0, ['Sst'])
                order = list(range(NDC)) if d == 0 else list(range(NDC - 1, -1, -1))
                halves = (0, 1) if d == 0 else (1, 0)
                for dc in order:
                    tok = slice(dc * 128, (dc + 1) * 128)
                    gcp = GC[d][:, dc, col]; beta = Bm[:, dc, 4 * d + j:4 * d + j + 1]
                    A, B_, TT, Tm, DG, DEC, DQ, QKT = (T[n] for n in ("A", "B", "TT", "Tm", "DG", "DEC", "DQ", "QKT"))
                    self.ts(DG[:], IDF[:], gcp, ALU.mult, ['IDF', ('GC', d)], ['DG'])
                    p2 = self.bank('d')
                    self.mm(cx.ps[p2][:, 0:128], ONES[:], DG[:], ['ONES', 'DG'], [('ps', p2)])
                    self.ts(DEC[:], cx.ps[p2][:, 0:128], gcp, ALU.subtract, [('ps', p2), ('GC', d)], ['DEC'], s2=0.0, op1=ALU.max)
                    self.ts(DQ[:], cx.ps[p2][:, 0:128], gcp, ALU.subtract, [('ps', p2), ('GC', d)], ['DQ'], s2=0.0, op1=ALU.min)
                    self.act(DEC[:], DEC[:], AF.Exp, ['DEC'], ['DEC'], scale=-1.0)
                    self.act(DQ[:], DQ[:], AF.Exp, ['DQ'], ['DQ'])
                    self.stt(DEC[:], DEC[:], beta, MK[:, 5 + d, :], ALU.mult, ALU.mult, ['DEC', 'Bm', 'MK'], ['DEC'])
                    self.tt(DQ[:], DQ[:], MK[:, 7 + d, :], ALU.mult, ['DQ', 'MK'], ['DQ'])
                    p1 = self.bank('d')
                    self.mm(cx.ps[p1][:, 0:128], Kf[:, tok], Kf[:, tok], ['Kf'], [('ps', p1)])
                    self.tt(A[:], cx.ps[p1][:, 0:128], DEC[:], ALU.mult, [('ps', p1), 'DEC'], ['A'])
                    p3 = self.bank('d')
                    self.mm(cx.ps[p3][:, 0:128], Kf[:, tok], Q[:, tok], ['Kf', 'Q'], [('ps', p3)])
                    self.tt(QKT[:], cx.ps[p3][:, 0:128], DQ[:], ALU.mult, [('ps', p3), 'DQ'], ['QKT'])
                    p4 = self.bank('d')
                    self.tr(cx.ps[p4][:, 0:128], A[:], IDF[:], ['A', 'IDF'], [('ps', p4)])
                    self.vcopy(B_[:], cx.ps[p4][:, 0:128], [('ps', p4)], ['B'])
                    self.tt(TT[:], IDF[:], B_[:], ALU.subtract, ['IDF', 'B'], ['TT'])
                    self.tt(Tm[:], IDF[:], A[:], ALU.subtract, ['IDF', 'A'], ['Tm'])
                    P_, Q_, pk, qk_ = B_, A, 'B', 'A'
                    for lvl in range(5):
                        Pn, pnk = T[f"P{lvl % 2}"], f"P{lvl % 2}"
                        Qn, qnk = T[f"Q{lvl % 2}"], f"Q{lvl % 2}"
                        mmsb(Pn[:], Q_[:], P_[:], [pk, qk_], [pnk])
                        if lvl < 4:
                            mmsb(Qn[:], P_[:], Q_[:], [pk, qk_], [qnk])
                        pa = self.bank('d')
                        self.mm(cx.ps[pa][:, 0:128], Tm[:], Pn[:], ['Tm', pnk], [('ps', pa)])
                        if lvl < 4:
                            pb = self.bank('d')
                            self.mm(cx.ps[pb][:, 0:128], TT[:], Qn[:], ['TT', qnk], [('ps', pb)])
                        self.tt(TT[:], TT[:], cx.ps[pa][:, 0:128], ALU.add, ['TT', ('ps', pa)], ['TT'])
                        if lvl < 4:
                            self.tt(Tm[:], Tm[:], cx.ps[pb][:, 0:128], ALU.add, ['Tm', ('ps', pb)], ['Tm'])
                        P_, Q_, pk, qk_ = Pn, Qn, pnk, qnk
                    VB, KBG, KO0, KO1, U, WT, VN, OA, Sst = (T[n] for n in ("VB", "KBG", "KO0", "KO1", "U", "WT", "VN", "OA", "Sst"))
                    self.ts(VB[:], Vtm[:, dc, :], beta, ALU.mult, ['Vtm', 'Bm'], ['VB'])
                    self.ts(KBG[:], Ktm[:, dc, :], BE[d][:, dc, col], ALU.mult, ['Ktm', ('BE', d)], ['KBG'])
                    self.ts(KO0[:], Ktm[:, dc, :], EK0[d][:, dc, col], ALU.mult, ['Ktm', ('EK0', d)], ['KO0'])
                    self.ts(KO1[:], Ktm[:, dc, :], EK1[d][:, dc, col], ALU.mult, ['Ktm', ('EK1', d)], ['KO1'])
                    mmsb(U[:], TT[:], VB[:], ['TT', 'VB'], ['U'])
                    mmsb(WT[:], KBG[:], TT[:], ['KBG', 'TT'], ['WT'])
                    for hh in halves:
                        rows = slice(64 * hh, 64 * hh + 64)
                        KOh, kok = (KO0, 'KO0') if hh == 0 else (KO1, 'KO1')
                        ELh = (EL0 if hh == 0 else EL1)[d][:, dc, col]
                        elk = ('EL0' if hh == 0 else 'EL1', d)
                        pv = self.bank('d')
                        self.mm(cx.ps[pv][:, 0:128], WT[:], Sst[:], ['WT', 'Sst'], [('ps', pv)])
                        self.tt(VN[:], U[:], cx.ps[pv][:, 0:128], ALU.subtract, ['U', ('ps', pv)], ['VN'])
                        pa = self.bank('a')
                        self.mm(cx.ps[pa][:, 0:128], Q[:, tok], Sst[:], ['Q', 'Sst'], [('ps', pa)])
                        self.act(OA[:], cx.ps[pa][:, 0:128], AF.Copy, [('ps', pa), ('EG', d)], ['OA'], scale=EG[d][:, dc, col])
                        pb = self.bank('d')
                        self.mm(cx.ps[pb][:, 0:128], QKT[:], VN[:], ['QKT', 'VN'], [('ps', pb)])
                        self.tt(O[rows, dc, :], OA[rows, :], cx.ps[pb][rows, 0:128], ALU.add, ['OA', ('ps', pb)], [('O', dc)])
                        pss = self.bank('d')
                        self.mm(cx.ps[pss][:, 0:128], KOh[:], VN[:], [kok, 'VN'], [('ps', pss)])
                        self.stt(Sst[:], Sst[:], ELh, cx.ps[pss][:, 0:128], ALU.mult, ALU.add, ['Sst', elk, ('ps', pss)], ['Sst'])
                chains.append(self.s.ops)
                self.s.ops = saved_ops
                self.chain = None
                self.s.keyfn = None
            assert len(chains[0]) == len(chains[1])
            for oa_, ob_ in zip(chains[0], chains[1]):
                self.s.ops.append(oa_)
                self.s.ops.append(ob_)
            zt, ot = Td[0]['zt'], Td[0]['ot']
            for dc in range(NDC):
                r0 = dc * 128
                self.tt(ot[:], Od[0][:, dc, :], Od[1][:, dc, :], ALU.add, [('O', 0, dc), ('O', 1, dc)], ['ot'])
                self.rms_rows(st, ot[:], 128, ['ot'])
                self.stt(ot[:], ot[:], st[:, 3:4], gnw[:], ALU.mult, ALU.mult, ['ot', 'st3', 'gnw'], ['ot'])
                self.dma(zt[:], Ptm[r0:r0 + 128, 1536 + j * 128:1536 + (j + 1) * 128], [], ['zt'])
                self.act(zt[:], zt[:], AF.Silu, ['zt'], ['zt'])
                self.tt(ot[:], ot[:], zt[:], ALU.mult, ['ot', 'zt'], ['ot'])
                self.dma(yout[r0:r0 + 128, 512 + j * 128:512 + (j + 1) * 128], ot[:], ['ot'], [])
        cx.pop()


def gdn_masks():
    p = np.arange(128)[:, None]
    f = np.arange(128)[None, :]
    same = (p // 64) == (f // 64)
    m = np.zeros((9, 128, 128), np.float32)
    m[0] = same & (p <= f)
    m[1] = same & (p >= f)
    m[2] = same
    m[3] = (p < 64) & (f >= 0)
    m[4] = (p >= 64) & (f >= 0)
    m[5] = same & (p > f)
    m[6] = same & (p < f)
    m[7] = same & (f >= p)
    m[8] = same & (f <= p)
    return m


def build_l2_prog(S, layer):
    lam_init = 0.8 - 0.6 * math.exp(-0.3 * layer)
    nc = bass.Bass("TRN2", target_bir_lowering=False)
    NT = S // 512
    x = dram_in(nc, "x", [S, D])
    ln = dram_in(nc, "ln", [128, 32])
    pos = dram_in(nc, "pos", [S], I32)
    wsl = dram_in(nc, "wsl", [15, D, 512])
    ident = dram_in(nc, "ident", [128, 128])
    tab = dram_in(nc, "tab", [128, 4])
    lng = dram_in(nc, "lng", [2])
    cw = dram_in(nc, "cw", [128, 60])
    alog = dram_in(nc, "alog", [8]); dtb = dram_in(nc, "dtb", [8]); gnw = dram_in(nc, "gnw", [128])
    masks = dram_in(nc, "masks", [9, 128, 128])
    dl = dram_in(nc, "dl", [512]); sub = dram_in(nc, "sub", [256])
    y = dram_out(nc, "y", [S, YW])
    hTd = dram_tmp(nc, "hTd", [NT, 128, 32 * 512], BF16)
    Pfm = dram_tmp(nc, "Pfm", [NFM, 128, S]); Ptm = dram_tmp(nc, "Ptm", [S, NTM])
    cosd = dram_tmp(nc, "cosd", [128, S]); sind = dram_tmp(nc, "sind", [128, S])
    cx = Ctx(nc)
    mx = Mix(cx, S)
    mx.rope_tables(pos, tab, cosd, sind)
    cx.push()
    b = Blocks(cx, ident)
    mx.phase0(b, x, ln, hTd)
    cx.s.barrier()
    mx.phase1(b, wsl, hTd, Pfm, Ptm)
    cx.pop()
    mx.diff_phase(Pfm, Ptm, cosd, sind, dl, sub, y, lam_init)
    mx.ret_phase(Pfm, Ptm, cosd, sind, lng, y)
    mx.gdn_phase(Pfm, Ptm, cw, alog, dtb, gnw, masks, ident, y)
    cx.s.finish()
    return nc


O_RQ, O_RK, O_RV, O_RG = 0, 1024, 2048, 4096
O_GQKV, O_GZ, O_GA, O_GB = 6144, 12288, 14336, 14368
O_DQ, O_DK, O_DV, O_GATE = 14400, 16448, 18496, 20544
_PERM = (np.arange(128) + 64) % 128


def l2_weight_slabs(w_in, g):
    out = np.zeros((15, D, 512), np.float32)
    si = 0
    for j in range(2):
        h = 2 * g + j
        for base in (O_DQ, O_DK):
            for t in range(2):
                c0 = base + (2 * h + t) * 128
                blk = w_in[:, c0:c0 + 128]
                out[si, :, (2 * t) * 128:(2 * t + 1) * 128] = blk
                out[si, :, (2 * t + 1) * 128:(2 * t + 2) * 128] = blk[:, _PERM]
            si += 1
        out[si, :, 0:256] = w_in[:, O_DV + h * 256:O_DV + (h + 1) * 256]
        si += 1
    for j in range(2):
        h = 2 * g + j
        for n, base in enumerate((O_RQ, O_RK)):
            blk = w_in[:, base + h * 128:base + (h + 1) * 128]
            out[si, :, (2 * n) * 128:(2 * n + 1) * 128] = blk
            out[si, :, (2 * n + 1) * 128:(2 * n + 2) * 128] = blk[:, _PERM]
        si += 1
        out[si, :, 0:256] = w_in[:, O_RV + h * 256:O_RV + (h + 1) * 256]
        out[si, :, 256:512] = w_in[:, O_RG + h * 256:O_RG + (h + 1) * 256]
        si += 1
    for j in range(4):
        h = 4 * g + j
        for r in range(3):
            out[si, :, r * 128:(r + 1) * 128] = w_in[:, O_GQKV + r * 2048 + h * 128:O_GQKV + r * 2048 + (h + 1) * 128]
        out[si, :, 384:512] = w_in[:, O_GZ + h * 128:O_GZ + (h + 1) * 128]
        si += 1
    for n, base in enumerate((O_GA, O_GB)):
        for d in range(2):
            c0 = base + d * 16 + 4 * g
            out[si, :, n * 8 + d * 4:n * 8 + d * 4 + 4] = w_in[:, c0:c0 + 4]
    return out


def l2_consts(inputs, layer, g):
    conv_w = np.asarray(inputs["conv_w"][layer], np.float32)
    cw = np.zeros((128, 60), np.float32)
    for j in range(4):
        h = 4 * g + j
        for r in range(3):
            for t in range(5):
                cw[:, (j * 3 + r) * 5 + t] = conv_w[t, r * 2048 + h * 128:r * 2048 + (h + 1) * 128]
    a_log = np.asarray(inputs["gdn_a_log"][layer], np.float32)
    dtb = np.asarray(inputs["gdn_dt_bias"][layer], np.float32)
    inv = (1.0 / (10000.0 ** (np.arange(0, 128, 2, dtype=np.float32) / 128))).astype(np.float32)
    tab = np.zeros((128, 4), np.float32)
    tab[:, 0] = np.concatenate([inv, inv])
    tab[:, 1] = np.where(np.arange(128) < 64, -1.0, 1.0)
    heads = np.arange(2 * g, 2 * g + 2, dtype=np.float32)
    lng = np.log(1.0 - 2.0 ** (-5.0 - heads)).astype(np.float32)
    return {
        "cw": cw,
        "alog": np.ascontiguousarray(a_log[:, 4 * g:4 * g + 4]).reshape(-1),
        "dtb": np.ascontiguousarray(dtb[:, 4 * g:4 * g + 4]).reshape(-1),
        "gnw": np.asarray(inputs["gdn_norm_w"][layer], np.float32),
        "masks": gdn_masks(),
        "dl": np.asarray(inputs["diff_lambda"][layer], np.float32).reshape(-1),
        "sub": np.asarray(inputs["diff_subln_w"][layer], np.float32),
        "tab": tab, "lng": lng, "ident": np.eye(128, dtype=np.float32),
    }
```

```python
import math
import numpy as np
import concourse.bass as bass
import concourse.mybir as mybir
from concourse.bass_utils import run_bass_kernel_spmd

F32 = mybir.dt.float32
BF16 = mybir.dt.bfloat16
I32 = mybir.dt.int32
ALU = mybir.AluOpType
AF = mybir.ActivationFunctionType

D = 4096
DFF = 8192
SEQ = 4096
NB = 2
DEPTH = 2
PLE = 256
EPS = 1e-6
NCORES = 8
SEM_LIM = 20000


class Sched:
    def __init__(self, nc):
        self.nc = nc
        self.E = dict(pe=nc.tensor, dve=nc.vector, act=nc.scalar, pool=nc.gpsimd, sp=nc.sync)
        self.ops = []
        self.keyfn = None
        self.esems = {e: [] for e in self.E}
        self.ecount = {e: 0 for e in self.E}
        self.RING = 12
        self.dsems = {}
        self.dcount = {e: 0 for e in self.E}
        self.seen = {e: {} for e in self.E}
        self.nsem = 0
        self.last_w = {}
        self.readers = {}
        self.tok = {}
        self.uid = 0

    def _newsem(self, tag):
        self.nsem += 1
        return self.nc.semaphore(f"{tag}{self.nsem}").__enter__()

    def op(self, eng, fn, reads=(), writes=(), dma=False):
        kf = self.keyfn
        if kf is not None:
            reads = [kf(k) for k in reads]
            writes = [kf(k) for k in writes]
        self.ops.append((eng, fn, tuple(reads), tuple(writes), dma))

    def emit(self):
        ops = self.ops
        self.ops = []
        n = len(ops)
        base = self.uid
        deps = []
        needs = [False] * n
        last_w, readers = self.last_w, self.readers
        for i, (eng, fn, R, W, dma) in enumerate(ops):
            u = base + i
            d = set()
            for k in R:
                if k in last_w:
                    d.add(last_w[k])
            for k in W:
                if k in last_w:
                    d.add(last_w[k])
                rs = readers.get(k)
                if rs:
                    d.update(rs)
            d.discard(u)
            dd = []
            for j in d:
                if j >= base:
                    je, _, _, _, jd = ops[j - base]
                    if je == eng and eng == 'pe' and not jd:
                        continue
                    needs[j - base] = True
                    dd.append(j)
                else:
                    t = self.tok.get(j)
                    if t is not None:
                        if t[3] == eng and eng == 'pe' and not t[4]:
                            continue
                        dd.append(j)
            deps.append(dd)
            for k in R:
                readers.setdefault(k, []).append(u)
            for k in W:
                last_w[k] = u
                readers[k] = []
        lastidx = {}
        for i, (eng, fn, R, W, dma) in enumerate(ops):
            lastidx[eng] = i
        for eng, i in lastidx.items():
            needs[i] = True
        for i, (eng, fn, R, W, dma) in enumerate(ops):
            u = base + i
            e = self.E[eng]
            seen = self.seen[eng]
            for j in deps[i]:
                t = self.tok[j]
                name, sem, val = t[0], t[1], t[2]
                if seen.get(name, 0) >= val:
                    continue
                e.wait_ge(sem, val)
                seen[name] = val
            if dma:
                ring = self.dsems.setdefault(eng, [])
                c = self.dcount[eng]
                slot = c % self.RING
                if slot >= len(ring):
                    ring.append([self._newsem(f"d{eng}"), 0])
                sem, uses = ring[slot]
                name = f"d{eng}{slot}"
                if uses > 0 and seen.get(name, 0) < 16 * uses:
                    e.wait_ge(sem, 16 * uses)
                    seen[name] = 16 * uses
                inst = fn()
                inst.then_inc(sem, 16)
                ring[slot][1] = uses + 1
                self.dcount[eng] = c + 1
                self.tok[u] = (name, sem, 16 * (uses + 1), eng, True)
            else:
                inst = fn()
                if needs[i]:
                    c = self.ecount[eng]
                    k = c // SEM_LIM
                    sl = self.esems[eng]
                    if k >= len(sl):
                        sl.append(self._newsem(f"e{eng}"))
                    sem = sl[k]
                    inst.then_inc(sem, 1)
                    self.ecount[eng] = c + 1
                    self.tok[u] = (f"e{eng}{k}", sem, c % SEM_LIM + 1, eng, False)
        self.uid = base + n
        live = set(last_w.values())
        for rs in readers.values():
            live.update(rs)
        self.tok = {k: v for k, v in self.tok.items() if k in live or k >= self.uid - 64}

    def barrier(self):
        self.emit()
        waits = []
        for eng, ring in self.dsems.items():
            for slot, (sem, uses) in enumerate(ring):
                if uses > 0:
                    waits.append((f"d{eng}{slot}", sem, 16 * uses))
        for eng, sl in self.esems.items():
            c = self.ecount[eng]
            if c > 0:
                k = (c - 1) // SEM_LIM
                waits.append((f"e{eng}{k}", sl[k], (c - 1) % SEM_LIM + 1))
        for eng, e in self.E.items():
            seen = self.seen[eng]
            for name, sem, val in waits:
                if seen.get(name, 0) < val:
                    e.wait_ge(sem, val)
                    seen[name] = val
        self.last_w.clear()
        self.readers.clear()
        self.tok.clear()

    def finish(self):
        self.emit()
        sp = self.E['sp']
        for eng, ring in self.dsems.items():
            for slot, (sem, uses) in enumerate(ring):
                if uses > 0:
                    sp.wait_ge(sem, 16 * uses)
        for eng, sl in self.esems.items():
            c = self.ecount[eng]
            if c > 0:
                k = (c - 1) // SEM_LIM
                sp.wait_ge(sl[k], (c - 1) % SEM_LIM + 1)


class Ctx:
    def __init__(self, nc):
        self.nc = nc
        self.s = Sched(nc)
        self.nbuf = 0
        self.live = []
        self.stack = []
        self.ps = [self.sb_psum(f"ps{i}") for i in range(8)]
        self.psrr = 0

    def sb(self, shape, dt, name=None):
        self.nbuf += 1
        cm = self.nc.sbuf_tensor((name or "b") + f"_{self.nbuf}", list(shape), dt)
        t = cm.__enter__()
        self.live.append(cm)
        return t

    def push(self):
        self.stack.append(len(self.live))

    def pop(self):
        self.s.barrier()
        n = self.stack.pop()
        while len(self.live) > n:
            self.live.pop().__exit__(None, None, None)

    def sb_psum(self, name):
        return self.nc.psum_tensor(name, [128, 512], F32).__enter__()


def dram_in(nc, name, shape, dt=F32):
    return nc.dram_tensor(name, list(shape), dt, kind="ExternalInput").ap()


def dram_out(nc, name, shape, dt=F32):
    return nc.dram_tensor(name, list(shape), dt, kind="ExternalOutput").ap()


def dram_tmp(nc, name, shape, dt=F32):
    return nc.dram_tensor(name, list(shape), dt, kind="Internal").ap()


class Blocks:
    def __init__(self, cx, ident_d):
        self.cx = cx
        nc, s = cx.nc, cx.s
        self.nc, self.s = nc, s
        self.lnT = cx.sb([128, 32], F32, "lnT")
        self.stat = cx.sb([128, 8], F32, "stat")
        self.identf = cx.sb([128, 128], F32, "identf")
        self.identb = cx.sb([128, 128], BF16, "identb")
        if getattr(cx, 'want_identf', False):
            s.op('sp', lambda: nc.sync.dma_start(out=self.identf[:], in_=ident_d[:, :]), writes=['identf'], dma=True)
        s.op('pool', lambda: nc.gpsimd.dma_start(out=self.identb[:], in_=ident_d[:, :]), writes=['identb'], dma=True)
        self.slabs = [cx.sb([128, 32, 512], BF16, f"slab{i}") for i in range(2)]
        self.slab_i = 0
        self.hT = cx.sb([128, 32, 512], BF16, "hT")
        self.xin = cx.sb([128, D], F32, "xin")
        self.xn = cx.sb([128, D], BF16, "xn")
        self.ps_i = 0
        self.LNW = 32
        self.tmp_i = 0
        self.tmpf = [cx.sb([128, 512], F32, f"tmpf{i}") for i in range(4)]
        self.xres = [cx.sb([128, 512], F32, f"xres{i}") for i in range(2)]
        self.xres_i = 0
        self.xo = [cx.sb([128, 512], F32, f"xo{i}") for i in range(2)]
        self.xo_i = 0

    def next_ps(self, lo=0, hi=8):
        i = lo + self.ps_i % (hi - lo)
        self.ps_i += 1
        return i

    def next_tmp(self):
        i = self.tmp_i % len(self.tmpf)
        self.tmp_i += 1
        return i

    def load_slab(self, w_ap, kc, ncols):
        nc, s = self.nc, self.s
        i = self.slab_i % len(self.slabs)
        self.slab_i += 1
        slab = self.slabs[i]
        src = w_ap.rearrange("(k p) n -> p k n", p=128)
        step = 8
        for k0 in range(0, kc, step):
            k1 = min(kc, k0 + step)
            s.op('pool', (lambda k0=k0, k1=k1: nc.gpsimd.dma_start(out=slab[:, k0:k1, 0:ncols], in_=src[:, k0:k1, :])),
                 writes=[('slab', i, k) for k in range(k0, k1)], dma=True)
        return i

    def load_ln(self, ln_ap):
        nc, s = self.nc, self.s
        s.op('sp', lambda: nc.sync.dma_start(out=self.lnT[:], in_=ln_ap), writes=['lnT'], dma=True)

    def norm_group(self, x_rows_ap, g, hT=None, rk=None, hkf=None):
        nc, s = self.nc, self.s
        hT = hT if hT is not None else self.hT
        hkf = hkf or (lambda c_, g_: ('hT', c_, g_))
        xin, xn, stat = self.xin, self.xn, self.stat
        s.op('sp', lambda: nc.sync.dma_start(out=xin[:], in_=x_rows_ap), writes=['xin'],
             reads=([('dr', rk[0], rk[1], c) for c in range(8)] if rk else []), dma=True)
        s.op('act', lambda: nc.scalar.activation(out=xn[:], in_=xin[:], func=AF.Square, accum_out=stat[:, 0:1]),
             reads=['xin'], writes=['xn', 'stat0'])
        s.op('dve', lambda: nc.vector.tensor_scalar(out=stat[:, 1:2], in0=stat[:, 0:1], scalar1=1.0 / D, scalar2=EPS,
                                                    op0=ALU.mult, op1=ALU.add), reads=['stat0'], writes=['stat1'])
        s.op('act', lambda: nc.scalar.activation(out=stat[:, 2:3], in_=stat[:, 1:2], func=AF.Sqrt),
             reads=['stat1'], writes=['stat2'])
        s.op('dve', lambda: nc.vector.reciprocal(out=stat[:, 3:4], in_=stat[:, 2:3]), reads=['stat2'], writes=['stat3'])
        s.op('dve', lambda: nc.vector.tensor_scalar(out=xn[:], in0=xin[:], scalar1=stat[:, 3:4], scalar2=None,
                                                    op0=ALU.mult), reads=['xin', 'stat3'], writes=['xn'])
        for c4 in range(getattr(self, 'NR', 8)):
            pi = self.next_ps(6, 8)
            psb = self.cx.ps[pi][:].bitcast(BF16)
            for j in range(4):
                c = c4 * 4 + j
                s.op('pe', (lambda c=c, j=j, psb=psb: nc.tensor.transpose(out=psb[:, j * 128:(j + 1) * 128],
                                                                         in_=xn[:, c * 128:(c + 1) * 128],
                                                                         identity=self.identb[:])),
                     reads=['xn', 'identb'], writes=[('ps', pi)])
            for j in range(4):
                c = c4 * 4 + j
                if c4 % 2 == 0:
                    s.op('dve', (lambda c=c, j=j, psb=psb: nc.vector.tensor_scalar(
                        out=hT[:, c, g * 128:(g + 1) * 128], in0=psb[:, j * 128:(j + 1) * 128],
                        scalar1=self.lnT[:, (c % self.LNW):(c % self.LNW) + 1], scalar2=None, op0=ALU.mult)),
                         reads=[('ps', pi), 'lnT'], writes=[hkf(c, g)])
                else:
                    s.op('act', (lambda c=c, j=j, psb=psb: nc.scalar.activation(
                        out=hT[:, c, g * 128:(g + 1) * 128], in_=psb[:, j * 128:(j + 1) * 128],
                        func=AF.Copy, scale=self.lnT[:, (c % self.LNW):(c % self.LNW) + 1])),
                         reads=[('ps', pi), 'lnT'], writes=[hkf(c, g)])

    def hT_keys(self, kc=32, ng=4):
        return [('hT', c, g) for c in range(kc) for g in range(ng)]

    def ffn(self, x_ap, out_ap, ln_ap, wg, wu, wd, gT, xk=None, ok=None):
        nc, s = self.nc, self.s
        self.load_ln(ln_ap)
        for g in range(4):
            self.norm_group(x_ap[g * 128:(g + 1) * 128, :], g, rk=(xk[0], xk[1] * 4 + g) if xk else None)
        hk = self.hT_keys()
        for sl in range(DFF // 512):
            ig = self.load_slab(wg[:, sl * 512:(sl + 1) * 512], 32, 512)
            iu = self.load_slab(wu[:, sl * 512:(sl + 1) * 512], 32, 512)
            for c in range(4):
                pg, pu = self.next_ps(0, 6), self.next_ps(0, 6)
                for (pi, si) in ((pg, ig), (pu, iu)):
                    slab = self.slabs[si]
                    for k in range(32):
                        s.op('pe', (lambda k=k, c=c, pi=pi, slab=slab: nc.tensor.matmul(
                            self.cx.ps[pi][:], slab[:, k, c * 128:(c + 1) * 128], self.hT[:, k, :],
                            start=(k == 0), stop=(k == 31))),
                             reads=[('slab', si, k)] + ([('hT', k, g) for g in range(4)]),
                             writes=[('ps', pi)])
                ti = self.next_tmp()
                tmp = self.tmpf[ti]
                s.op('act', (lambda pg=pg, tmp=tmp: nc.scalar.activation(out=tmp[:], in_=self.cx.ps[pg][:], func=AF.Silu)),
                     reads=[('ps', pg)], writes=[('tmpf', ti)])
                fc = sl * 4 + c
                s.op('dve', (lambda pu=pu, tmp=tmp, fc=fc: nc.vector.tensor_tensor(
                    out=gT[:, fc, :], in0=tmp[:], in1=self.cx.ps[pu][:], op=ALU.mult)),
                     reads=[('ps', pu), ('tmpf', ti)], writes=[('gT', fc)])
        nF = DFF // 128
        npiece = max(1, nF // 32)
        kcp = nF // npiece
        for dsl in range(D // 512):
            pis = [self.next_ps(0, 6) for _ in range(4)]
            for piece in range(npiece):
                si = self.load_slab(wd[piece * kcp * 128:(piece + 1) * kcp * 128, dsl * 512:(dsl + 1) * 512], kcp, 512)
                slab = self.slabs[si]
                for g in range(4):
                    for k in range(kcp):
                        fk = piece * kcp + k
                        s.op('pe', (lambda g=g, k=k, fk=fk, slab=slab, pi=pis[g]: nc.tensor.matmul(
                            self.cx.ps[pi][:], gT[:, fk, g * 128:(g + 1) * 128], slab[:, k, :],
                            start=(fk == 0), stop=(fk == nF - 1))),
                             reads=[('slab', si, k), ('gT', fk)], writes=[('ps', pis[g])])
            for g in range(4):
                self.residual_out(x_ap[g * 128:(g + 1) * 128, dsl * 512:(dsl + 1) * 512],
                                  out_ap[g * 128:(g + 1) * 128, dsl * 512:(dsl + 1) * 512], pis[g], 0.5,
                                  sk=('dr', xk[0], xk[1] * 4 + g, dsl) if xk else None,
                                  dk=('dr', ok[0], ok[1] * 4 + g, dsl) if ok else None)

    def mix_merge(self, x_ap, yT_ap, ln_ap, wgate, wbr, gT, xk=None):
        nc, s = self.nc, self.s
        self.load_ln(ln_ap)
        for g in range(4):
            self.norm_group(x_ap[g * 128:(g + 1) * 128, :], g, rk=(xk[0], xk[1] * 4 + g) if xk else None)
        ybufs = [gT[:, 32:48, :], gT[:, 48:64, :]]
        macc = [self.xin[:, c * 512:(c + 1) * 512] for c in range(4)]
        yb_i = 0
        for dsl in range(D // 512):
            for b in range(3):
                yi = yb_i % 2
                yb_i += 1
                ybuf = ybufs[yi]
                if dsl == 0 or True:
                    src = yT_ap[b * 2048:(b + 1) * 2048, :].rearrange("(k p) n -> p k n", p=128)
                    for k0 in (0, 8):
                        s.op('pool', (lambda k0=k0, ybuf=ybuf, src=src: nc.gpsimd.dma_start(
                            out=ybuf[:, k0:k0 + 8, :], in_=src[:, k0:k0 + 8, :])),
                             writes=[('gT', 32 + 16 * yi + k) for k in range(k0, k0 + 8)], dma=True)
                ig = self.load_slab(wgate[:, b * D + dsl * 512: b * D + (dsl + 1) * 512], 32, 512)
                ib = self.load_slab(wbr[b, :, dsl * 512:(dsl + 1) * 512], 16, 512)
                sg, sbr = self.slabs[ig], self.slabs[ib]
                for c in range(4):
                    pg, py = self.next_ps(0, 6), self.next_ps(0, 6)
                    for k in range(32):
                        s.op('pe', (lambda k=k, c=c, pg=pg, sg=sg: nc.tensor.matmul(
                            self.cx.ps[pg][:], sg[:, k, c * 128:(c + 1) * 128], self.hT[:, k, :],
                            start=(k == 0), stop=(k == 31))),
                             reads=[('slab', ig, k)] + [('hT', k, g) for g in range(4)], writes=[('ps', pg)])
                    for k in range(16):
                        s.op('pe', (lambda k=k, c=c, py=py, sbr=sbr, ybuf=ybuf: nc.tensor.matmul(
                            self.cx.ps[py][:], sbr[:, k, c * 128:(c + 1) * 128], ybuf[:, k, :],
                            start=(k == 0), stop=(k == 15))),
                             reads=[('slab', ib, k), ('gT', 32 + 16 * yi + k)], writes=[('ps', py)])
                    ti = self.next_tmp()
                    tmp = self.tmpf[ti]
                    s.op('act', (lambda pg=pg, tmp=tmp: nc.scalar.activation(out=tmp[:], in_=self.cx.ps[pg][:], func=AF.Sigmoid)),
                         reads=[('ps', pg)], writes=[('tmpf', ti)])
                    if b == 0:
                        s.op('dve', (lambda py=py, tmp=tmp, c=c: nc.vector.tensor_tensor(
                            out=macc[c], in0=tmp[:], in1=self.cx.ps[py][:], op=ALU.mult)),
                             reads=[('ps', py), ('tmpf', ti)], writes=['xin'])
                    else:
                        s.op('dve', (lambda py=py, tmp=tmp: nc.vector.tensor_tensor(
                            out=tmp[:], in0=tmp[:], in1=self.cx.ps[py][:], op=ALU.mult)),
                             reads=[('ps', py), ('tmpf', ti)], writes=[('tmpf', ti)])
                        s.op('dve', (lambda tmp=tmp, c=c: nc.vector.tensor_tensor(
                            out=macc[c], in0=macc[c], in1=tmp[:], op=ALU.add)),
                             reads=[('tmpf', ti), 'xin'], writes=['xin'])
                    if b == 2:
                        mc = dsl * 4 + c
                        s.op('act', (lambda c=c, mc=mc: nc.scalar.copy(out=gT[:, mc, :], in_=macc[c])),
                             reads=['xin'], writes=[('gT', mc)])

    def wo_proj(self, x_ap, out_ap, wo, gT, xk=None, ok=None):
        nc, s = self.nc, self.s
        for dsl in range(D // 512):
            si = self.load_slab(wo[:, dsl * 512:(dsl + 1) * 512], 32, 512)
            slab = self.slabs[si]
            for g in range(4):
                pi = self.next_ps(0, 6)
                for k in range(32):
                    s.op('pe', (lambda g=g, k=k, slab=slab, pi=pi: nc.tensor.matmul(
                        self.cx.ps[pi][:], gT[:, k, g * 128:(g + 1) * 128], slab[:, k, :],
                        start=(k == 0), stop=(k == 31))),
                         reads=[('slab', si, k), ('gT', k)], writes=[('ps', pi)])
                self.residual_out(x_ap[g * 128:(g + 1) * 128, dsl * 512:(dsl + 1) * 512],
                                  out_ap[g * 128:(g + 1) * 128, dsl * 512:(dsl + 1) * 512], pi, 1.0,
                                  sk=('dr', xk[0], xk[1] * 4 + g, dsl) if xk else None,
                                  dk=('dr', ok[0], ok[1] * 4 + g, dsl) if ok else None)

    def ple(self, x_ap, out_ap, ln_ap, pT_ap, wpg, wpp, gT, xk=None, ok=None):
        nc, s = self.nc, self.s
        self.load_ln(ln_ap)
        for g in range(4):
            self.norm_group(x_ap[g * 128:(g + 1) * 128, :], g, rk=(xk[0], xk[1] * 4 + g) if xk else None)
        pT = gT[:, 32:34, :]
        s.op('pool', lambda: nc.gpsimd.dma_start(out=pT, in_=pT_ap.rearrange("(k p) n -> p k n", p=128)),
             writes=[('gT', 32), ('gT', 33)], dma=True)
        for dsl in range(D // 512):
            ig = self.load_slab(wpg[:, dsl * 512:(dsl + 1) * 512], 32, 512)
            ip = self.load_slab(wpp[:, dsl * 512:(dsl + 1) * 512], 2, 512)
            sg, sp_ = self.slabs[ig], self.slabs[ip]
            for g in range(4):
                pg, pp = self.next_ps(0, 6), self.next_ps(0, 6)
                for k in range(32):
                    s.op('pe', (lambda g=g, k=k, sg=sg, pg=pg: nc.tensor.matmul(
                        self.cx.ps[pg][:], self.hT[:, k, g * 128:(g + 1) * 128], sg[:, k, :],
                        start=(k == 0), stop=(k == 31))),
                         reads=[('slab', ig, k), ('hT', k, g)], writes=[('ps', pg)])
                for k in range(2):
                    s.op('pe', (lambda g=g, k=k, sp_=sp_, pp=pp: nc.tensor.matmul(
                        self.cx.ps[pp][:], pT[:, k, g * 128:(g + 1) * 128], sp_[:, k, :],
                        start=(k == 0), stop=(k == 1))),
                         reads=[('slab', ip, k), ('gT', 32 + k)], writes=[('ps', pp)])
                ti = self.next_tmp()
                tmp = self.tmpf[ti]
                s.op('act', (lambda pg=pg, tmp=tmp: nc.scalar.activation(out=tmp[:], in_=self.cx.ps[pg][:], func=AF.Sigmoid)),
                     reads=[('ps', pg)], writes=[('tmpf', ti)])
                s.op('dve', (lambda pp=pp, tmp=tmp: nc.vector.tensor_tensor(
                    out=tmp[:], in0=tmp[:], in1=self.cx.ps[pp][:], op=ALU.mult)),
                     reads=[('ps', pp), ('tmpf', ti)], writes=[('tmpf', ti)])
                ri = self.xres_i % 2
                self.xres_i += 1
                oi = self.xo_i % 2
                self.xo_i += 1
                xr, xo = self.xres[ri], self.xo[oi]
                xs = x_ap[g * 128:(g + 1) * 128, dsl * 512:(dsl + 1) * 512]
                ds = out_ap[g * 128:(g + 1) * 128, dsl * 512:(dsl + 1) * 512]
                s.op('sp', (lambda xr=xr, xs=xs: nc.sync.dma_start(out=xr[:], in_=xs)), writes=[('xres', ri)],
                     reads=([('dr', xk[0], xk[1] * 4 + g, dsl)] if xk else []), dma=True)
                s.op('dve', (lambda xr=xr, xo=xo, tmp=tmp: nc.vector.tensor_tensor(out=xo[:], in0=tmp[:], in1=xr[:], op=ALU.add)),
                     reads=[('tmpf', ti), ('xres', ri)], writes=[('xo', oi)])
                s.op('sp', (lambda xo=xo, ds=ds: nc.sync.dma_start(out=ds, in_=xo[:])), reads=[('xo', oi)],
                     writes=([('dr', ok[0], ok[1] * 4 + g, dsl)] if ok else []), dma=True)

    def final_norm(self, x_ap, out_ap, fn_ap, gT, ntok, xname=None):
        nc, s = self.nc, self.s
        fnb = gT[:, 0:16, :].rearrange("p a b -> p (a b)").bitcast(F32)
        s.op('sp', lambda: nc.sync.dma_start(out=fnb, in_=fn_ap.partition_broadcast(128)),
             writes=[('gT', k) for k in range(16)], dma=True)
        xin, stat = self.xin, self.stat
        for g in range(ntok // 128):
            xs = x_ap[g * 128:(g + 1) * 128, :]
            ds = out_ap[g * 128:(g + 1) * 128, :]
            s.op('sp', (lambda xs=xs: nc.sync.dma_start(out=xin[:], in_=xs)), writes=['xin'],
                 reads=([('dr', xname, g, c) for c in range(8)] if xname else []), dma=True)
            s.op('act', lambda: nc.scalar.activation(out=self.xn[:], in_=xin[:], func=AF.Square, accum_out=stat[:, 0:1]),
                 reads=['xin'], writes=['xn', 'stat0'])
            s.op('dve', lambda: nc.vector.tensor_scalar(out=stat[:, 1:2], in0=stat[:, 0:1], scalar1=1.0 / D, scalar2=EPS,
                                                        op0=ALU.mult, op1=ALU.add), reads=['stat0'], writes=['stat1'])
            s.op('act', lambda: nc.scalar.activation(out=stat[:, 2:3], in_=stat[:, 1:2], func=AF.Sqrt),
                 reads=['stat1'], writes=['stat2'])
            s.op('dve', lambda: nc.vector.reciprocal(out=stat[:, 3:4], in_=stat[:, 2:3]), reads=['stat2'], writes=['stat3'])
            s.op('dve', lambda: nc.vector.scalar_tensor_tensor(out=xin[:], in0=xin[:], scalar=stat[:, 3:4], in1=fnb,
                                                               op0=ALU.mult, op1=ALU.mult),
                 reads=['xin', 'stat3'] + [('gT', k) for k in range(16)], writes=['xin'])
            s.op('sp', (lambda ds=ds: nc.sync.dma_start(out=ds, in_=xin[:])), reads=['xin'], dma=True)

    def ffn2(self, x_ap, out_ap, ln_ap, wg, wu, wd, gT, xk, ok):
        nc, s = self.nc, self.s
        self.load_ln(ln_ap)
        hb = [self.hT, gT[:, 0:32, :]]
        hkeys = [lambda k_: [('hT', k_, g_) for g_ in range(4)], lambda k_: [('gT', k_)]]
        for tt in range(2):
            for g in range(4):
                r0 = tt * 512 + g * 128
                self.norm_group(x_ap[r0:r0 + 128, :], g, hT=hb[tt], rk=(xk, tt * 4 + g),
                                hkf=(None if tt == 0 else (lambda c_, g_: ('gT', c_))))
        FQ = DFF // 4
        nsl = FQ // 512
        kcq = FQ // 128
        for q in range(4):
            for sl in range(nsl):
                c0 = (q * nsl + sl) * 512
                ig = self.load_slab(wg[:, c0:c0 + 512], 32, 512)
                iu = self.load_slab(wu[:, c0:c0 + 512], 32, 512)
                for c in range(4):
                    fc = sl * 4 + c
                    for tt in range(2):
                        pg, pu = self.next_ps(0, 6), self.next_ps(0, 6)
                        for (pi, si) in ((pg, ig), (pu, iu)):
                            slab = self.slabs[si]
                            for k in range(32):
                                s.op('pe', (lambda k=k, c=c, pi=pi, slab=slab, tt=tt: nc.tensor.matmul(
                                    self.cx.ps[pi][:], slab[:, k, c * 128:(c + 1) * 128], hb[tt][:, k, :],
                                    start=(k == 0), stop=(k == 31))),
                                     reads=[('slab', si, k)] + hkeys[tt](k), writes=[('ps', pi)])
                        ti = self.next_tmp()
                        tmp = self.tmpf[ti]
                        s.op('act', (lambda pg=pg, tmp=tmp: nc.scalar.activation(out=tmp[:], in_=self.cx.ps[pg][:], func=AF.Silu)),
                             reads=[('ps', pg)], writes=[('tmpf', ti)])
                        gc_ = 32 + 2 * fc + tt
                        s.op('dve', (lambda pu=pu, tmp=tmp, gc_=gc_: nc.vector.tensor_tensor(
                            out=gT[:, gc_, :], in0=tmp[:], in1=self.cx.ps[pu][:], op=ALU.mult)),
                             reads=[('ps', pu), ('tmpf', ti)], writes=[('gT', gc_)])
            for dsl in range(D // 512):
                si = self.load_slab(wd[q * FQ:(q + 1) * FQ, dsl * 512:(dsl + 1) * 512], kcq, 512)
                slab = self.slabs[si]
                for tg in range(8):
                    tt, g = tg // 4, tg % 4
                    pi = self.next_ps(0, 6)
                    for k in range(kcq):
                        gc_ = 32 + 2 * k + tt
                        s.op('pe', (lambda g=g, k=k, gc_=gc_, slab=slab, pi=pi: nc.tensor.matmul(
                            self.cx.ps[pi][:], gT[:, gc_, g * 128:(g + 1) * 128], slab[:, k, :],
                            start=(k == 0), stop=(k == kcq - 1))),
                             reads=[('slab', si, k), ('gT', gc_)], writes=[('ps', pi)])
                    src_ap, sname = (x_ap, xk) if q == 0 else (out_ap, ok)
                    self.residual_out(src_ap[tg * 128:(tg + 1) * 128, dsl * 512:(dsl + 1) * 512],
                                      out_ap[tg * 128:(tg + 1) * 128, dsl * 512:(dsl + 1) * 512], pi, 0.5,
                                      sk=('dr', sname, tg, dsl), dk=('dr', ok, tg, dsl))

    def residual_out(self, xsrc_ap, dst_ap, pi, scale, sk=None, dk=None):
        nc, s = self.nc, self.s
        ri = self.xres_i % 2
        self.xres_i += 1
        oi = self.xo_i % 2
        self.xo_i += 1
        xr, xo = self.xres[ri], self.xo[oi]
        s.op('sp', lambda: nc.sync.dma_start(out=xr[:], in_=xsrc_ap), writes=[('xres', ri)],
             reads=([sk] if sk else []), dma=True)
        s.op('dve', lambda: nc.vector.scalar_tensor_tensor(out=xo[:], in0=self.cx.ps[pi][:], scalar=scale, in1=xr[:],
                                                           op0=ALU.mult, op1=ALU.add),
             reads=[('ps', pi), ('xres', ri)], writes=[('xo', oi)])
        s.op('sp', lambda: nc.sync.dma_start(out=dst_ap, in_=xo[:]), reads=[('xo', oi)],
             writes=([dk] if dk else []), dma=True)


def build_ffn_prog(ntok):
    nc = bass.Bass("TRN2", target_bir_lowering=False)
    x = dram_in(nc, "x", [ntok, D])
    ln = dram_in(nc, "ln", [128, 32])
    wg = dram_in(nc, "wg", [D, DFF])
    wu = dram_in(nc, "wu", [D, DFF])
    wd = dram_in(nc, "wd", [DFF, D])
    ident = dram_in(nc, "ident", [128, 128])
    y = dram_out(nc, "y", [ntok, D])
    cx = Ctx(nc)
    b = Blocks(cx, ident)
    gT = cx.sb([128, 64, 512], BF16, "gT")
    if ntok == 1024:
        b.ffn2(x, y, ln, wg, wu, wd, gT, 'x', 'y')
    else:
        for t in range(ntok // 512):
            b.ffn(x[t * 512:(t + 1) * 512, :], y[t * 512:(t + 1) * 512, :], ln, wg, wu, wd, gT)
    cx.s.finish()
    return nc


def build_l3_prog(ntok, last):
    nc = bass.Bass("TRN2", target_bir_lowering=False)
    x = dram_in(nc, "x", [ntok, D])
    yT = dram_in(nc, "yT", [3 * 2048, ntok])
    pT = dram_in(nc, "pT", [PLE, ntok])
    ident = dram_in(nc, "ident", [128, 128])
    ln_mix = dram_in(nc, "ln_mix", [128, 32])
    wgate = dram_in(nc, "wgate", [D, 3 * D])
    wbr = dram_in(nc, "wbr", [3, 2048, D])
    wo = dram_in(nc, "wo", [D, D])
    ln1 = dram_in(nc, "ln1", [128, 32])
    wg1 = dram_in(nc, "wg1", [D, DFF]); wu1 = dram_in(nc, "wu1", [D, DFF]); wd1 = dram_in(nc, "wd1", [DFF, D])
    ln_ple = dram_in(nc, "ln_ple", [128, 32])
    wpg = dram_in(nc, "wpg", [D, D]); wpp = dram_in(nc, "wpp", [PLE, D])
    if last:
        fn = dram_in(nc, "fn", [D])
    else:
        ln2 = dram_in(nc, "ln2", [128, 32])
        wg2 = dram_in(nc, "wg2", [D, DFF]); wu2 = dram_in(nc, "wu2", [D, DFF]); wd2 = dram_in(nc, "wd2", [DFF, D])
    y = dram_out(nc, "y", [ntok, D])
    x2 = dram_tmp(nc, "x2", [ntok, D]); x3 = dram_tmp(nc, "x3", [ntok, D]); x4 = dram_tmp(nc, "x4", [ntok, D])
    cx = Ctx(nc)
    b = Blocks(cx, ident)
    gT = cx.sb([128, 64, 512], BF16, "gT")
    if ntok == 1024:
        for t in range(2):
            sl = slice(t * 512, (t + 1) * 512)
            b.mix_merge(x[sl, :], yT[:, sl], ln_mix, wgate, wbr, gT, xk=('x', t))
            b.wo_proj(x[sl, :], x2[sl, :], wo, gT, xk=('x', t), ok=('x2', t))
        b.ffn2(x2, x3, ln1, wg1, wu1, wd1, gT, 'x2', 'x3')
        for t in range(2):
            sl = slice(t * 512, (t + 1) * 512)
            b.ple(x3[sl, :], x4[sl, :], ln_ple, pT[:, sl], wpg, wpp, gT, xk=('x3', t), ok=('x4', t))
        if not last:
            b.ffn2(x4, y, ln2, wg2, wu2, wd2, gT, 'x4', 'y')
    else:
      for t in range(ntok // 512):
        sl = slice(t * 512, (t + 1) * 512)
        b.mix_merge(x[sl, :], yT[:, sl], ln_mix, wgate, wbr, gT, xk=('x', t))
        b.wo_proj(x[sl, :], x2[sl, :], wo, gT, xk=('x', t), ok=('x2', t))
        b.ffn(x2[sl, :], x3[sl, :], ln1, wg1, wu1, wd1, gT, xk=('x2', t), ok=('x3', t))
        b.ple(x3[sl, :], x4[sl, :], ln_ple, pT[:, sl], wpg, wpp, gT, xk=('x3', t), ok=('x4', t))
        if not last:
            b.ffn(x4[sl, :], y[sl, :], ln2, wg2, wu2, wd2, gT, xk=('x4', t), ok=('y', t))
    if last:
        b.final_norm(x4, y, fn, gT, ntok, xname='x4')
    cx.s.finish()
    return nc


def _pT(v):
    return np.ascontiguousarray(np.asarray(v, np.float32).reshape(-1, 128).T)


def _launch(nc, in_maps):
    res = run_bass_kernel_spmd(nc, in_maps, core_ids=list(range(NCORES)))
    return res.results


def _c(a):
    return np.ascontiguousarray(np.asarray(a, np.float32))


def kernel(**inputs):
    TS = NB * SEQ // NCORES
    x = _c(inputs["x"]).reshape(NB * SEQ, D)
    ident = np.eye(128, dtype=np.float32)
    pos = np.asarray(inputs["positions"]).astype(np.int32)
    nc1 = build_ffn_prog(TS)
    w = {"ln": _pT(inputs["ln_ffn"][0, 0]), "wg": _c(inputs["ffn_w_gate"][0, 0]), "wu": _c(inputs["ffn_w_up"][0, 0]),
         "wd": _c(inputs["ffn_w_down"][0, 0]), "ident": ident}
    res = _launch(nc1, [dict(w, x=x[c * TS:(c + 1) * TS]) for c in range(NCORES)])
    x1 = np.concatenate([np.asarray(r["y"], np.float32) for r in res], axis=0)
    del w
    for i in range(DEPTH):
        last = (i == DEPTH - 1)
        w_in = np.asarray(inputs["w_in"][i], np.float32)
        nc2 = build_l2_prog(SEQ, i)
        ln_mix = _pT(inputs["ln_mix"][i])
        groups = []
        for g in range(4):
            d = l2_consts(inputs, i, g)
            d["wsl"] = l2_weight_slabs(w_in, g)
            d["ln"] = ln_mix
            groups.append(d)
        in_maps = []
        for c in range(NCORES):
            b, g = c // 4, c % 4
            in_maps.append(dict(groups[g], x=x1[b * SEQ:(b + 1) * SEQ], pos=np.ascontiguousarray(pos[b])))
        res = _launch(nc2, in_maps)
        del groups, in_maps
        yfull = np.zeros((NB, SEQ, 3 * 2048), np.float32)
        for c in range(NCORES):
            b, g = c // 4, c % 4
            yc = np.asarray(res[c]["y"], np.float32)
            yfull[b, :, g * 512:(g + 1) * 512] = yc[:, 0:512]
            yfull[b, :, 2048 + g * 512:2048 + (g + 1) * 512] = yc[:, 512:1024]
            yfull[b, :, 4096 + g * 512:4096 + (g + 1) * 512] = yc[:, 1024:1536]
        nc3 = build_l3_prog(TS, last)
        w = {"ident": ident, "ln_mix": ln_mix, "wgate": np.ascontiguousarray(w_in[:, O_GATE:]),
             "wbr": _c(inputs["w_branch"][i]), "wo": _c(inputs["w_out"][i]),
             "ln1": _pT(inputs["ln_ffn"][i, 1]), "wg1": _c(inputs["ffn_w_gate"][i, 1]), "wu1": _c(inputs["ffn_w_up"][i, 1]),
             "wd1": _c(inputs["ffn_w_down"][i, 1]), "ln_ple": _pT(inputs["ln_ple"][i]),
             "wpg": _c(inputs["w_ple_gate"][i]), "wpp": _c(inputs["w_ple_proj"][i])}
        if last:
            w["fn"] = _c(inputs["final_norm"])
        else:
            w.update({"ln2": _pT(inputs["ln_ffn"][i + 1, 0]), "wg2": _c(inputs["ffn_w_gate"][i + 1, 0]),
                      "wu2": _c(inputs["ffn_w_up"][i + 1, 0]), "wd2": _c(inputs["ffn_w_down"][i + 1, 0])})
        del w_in
        p_i = np.asarray(inputs["p"][i], np.float32)
        in_maps = []
        for c in range(NCORES):
            b, t0 = c // 4, (c % 4) * TS
            in_maps.append(dict(w, x=x1[c * TS:(c + 1) * TS],
                                yT=np.ascontiguousarray(yfull[b, t0:t0 + TS, :].T),
                                pT=np.ascontiguousarray(p_i[b, t0:t0 + TS, :].T)))
        res = _launch(nc3, in_maps)
        x1 = np.concatenate([np.asarray(r["y"], np.float32) for r in res], axis=0)
        del w, in_maps, yfull
    return x1.reshape(NB, SEQ, D)


NFM = 36
NTM = 2064
YW = 1536
TWO_PI = 2.0 * math.pi


def l2_slab_plan():
    plan = []
    for j in range(2):
        plan.append([('fm', c * 128, j * 8 + c) for c in range(4)])
        plan.append([('fm', c * 128, j * 8 + 4 + c) for c in range(4)])
        plan.append([('tm', 0, 256, j * 256)])
    for j in range(2):
        plan.append([('fm', c * 128, 16 + j * 4 + c) for c in range(4)])
        plan.append([('tm', 0, 512, 512 + j * 512)])
    for j in range(4):
        plan.append([('fm', c * 128, 24 + j * 3 + c) for c in range(3)] + [('tm', 384, 128, 1536 + j * 128)])
    plan.append([('tm', 0, 16, 2048)])
    return plan


class Mix:
    def __init__(self, cx, S):
        self.cx, self.nc, self.s, self.S = cx, cx.nc, cx.s, S
        self.bi = {'d': 0, 'a': 0}

    def bank(self, kind='d'):
        ch = getattr(self, 'chain', None)
        if ch is not None:
            if kind == 'a':
                i = 6 + ch
            elif self.stage == 'pre':
                i = ch * 3 + self.lane
            else:
                i = ch * 3 + 2
        elif kind == 'd':
            i = self.bi['d'] % 6
        else:
            i = 6 + self.bi['a'] % 2
        self.bi[kind] += 1
        return i

    def mm(self, out, lhsT, rhs, r, w, start=True, stop=True):
        nc = self.nc
        self.s.op('pe', lambda: nc.tensor.matmul(out, lhsT, rhs, start=start, stop=stop), r, w)

    def tr(self, out, in_, ident, r, w):
        nc = self.nc
        self.s.op('pe', lambda: nc.tensor.transpose(out=out, in_=in_, identity=ident), r, w)

    def tt(self, out, in0, in1, op, r, w, eng='dve'):
        e = self.nc.vector if eng == 'dve' else self.nc.gpsimd
        self.s.op(eng, lambda: e.tensor_tensor(out=out, in0=in0, in1=in1, op=op), r, w)

    def ts(self, out, in0, s1, op0, r, w, s2=None, op1=None):
        nc = self.nc
        if op1 is None:
            self.s.op('dve', lambda: nc.vector.tensor_scalar(out=out, in0=in0, scalar1=s1, scalar2=None, op0=op0), r, w)
        else:
            self.s.op('dve', lambda: nc.vector.tensor_scalar(out=out, in0=in0, scalar1=s1, scalar2=s2, op0=op0, op1=op1), r, w)

    def stt(self, out, in0, scalar, in1, op0, op1, r, w):
        nc = self.nc
        self.s.op('dve', lambda: nc.vector.scalar_tensor_tensor(out=out, in0=in0, scalar=scalar, in1=in1, op0=op0, op1=op1), r, w)

    def act(self, out, in_, func, r, w, scale=None, bias=None, accum=None):
        nc = self.nc
        kw = {}
        if scale is not None:
            kw['scale'] = scale
        if bias is not None:
            kw['bias'] = bias
        if accum is not None:
            kw['accum_out'] = accum
        self.s.op('act', lambda: nc.scalar.activation(out=out, in_=in_, func=func, **kw), r, w)

    def vcopy(self, out, in_, r, w):
        nc = self.nc
        self.s.op('dve', lambda: nc.vector.tensor_copy(out=out, in_=in_), r, w)

    def memset(self, ap, val, w, r=()):
        nc = self.nc
        self.s.op('dve', lambda: nc.vector.memset(ap, val), r, w)

    def recip(self, out, in_, r, w):
        nc = self.nc
        self.s.op('dve', lambda: nc.vector.reciprocal(out=out, in_=in_), r, w)

    def dma(self, out, in_, r, w, q='sp'):
        nc = self.nc
        e = {'sp': nc.sync, 'pool': nc.gpsimd, 'act': nc.scalar}[q]
        self.s.op(q, lambda: e.dma_start(out=out, in_=in_), r, w, dma=True)

    def phase0(self, b, x, ln_ap, hTd):
        cx, S = self.cx, self.S
        b.load_ln(ln_ap)
        for t in range(S // 512):
            for g in range(4):
                b.norm_group(x[t * 512 + g * 128: t * 512 + (g + 1) * 128, :], g)
            self.dma(hTd[t], b.hT[:].rearrange("p k n -> p (k n)"), b.hT_keys(), [('hTd', t)])

    def rope_tables(self, pos_ap, tab_ap, cosd, sind):
        cx, S = self.cx, self.S
        cx.push()
        tab = cx.sb([128, 4], F32, "tab")
        self.dma(tab[:], tab_ap, [], ['tab'])
        posi = cx.sb([128, S], I32, "posi")
        u = cx.sb([128, S], F32, "u")
        kf = cx.sb([128, S], F32, "kf")
        ki = cx.sb([128, S], I32, "ki")
        m = cx.sb([128, S], F32, "m")
        self.dma(posi[:], pos_ap.partition_broadcast(128), [], ['posi'])
        self.vcopy(u[:], posi[:], ['posi'], ['u'])
        self.ts(u[:], u[:], tab[:, 0:1], ALU.mult, ['u', 'tab'], ['u'])
        for which, shift, dst in (('sin', 0.0, sind), ('cos', 0.25, cosd)):
            self.ts(kf[:], u[:], 1.0 / TWO_PI, ALU.mult, ['u'], ['kf'], s2=shift, op1=ALU.add)
            self.vcopy(ki[:], kf[:], ['kf'], ['ki'])
            self.vcopy(m[:], ki[:], ['ki'], ['m'])
            self.tt(kf[:], kf[:], m[:], ALU.subtract, ['kf', 'm'], ['kf'])
            self.ts(m[:], kf[:], 0.5, ALU.is_gt, ['kf'], ['m'])
            self.tt(kf[:], kf[:], m[:], ALU.subtract, ['kf', 'm'], ['kf'])
            self.ts(m[:], kf[:], -0.5, ALU.is_lt, ['kf'], ['m'])
            self.tt(kf[:], kf[:], m[:], ALU.add, ['kf', 'm'], ['kf'])
            self.act(m[:], kf[:], AF.Sin, ['kf'], ['m'], scale=6.283185)
            if which == 'sin':
                self.ts(m[:], m[:], tab[:, 1:2], ALU.mult, ['m', 'tab'], ['m'])
            self.dma(dst, m[:], ['m'], [which + 'd'])
        cx.pop()

    def phase1(self, b, wsl, hTd, Pfm, Ptm):
        cx, S, nc = self.cx, self.S, self.nc
        plan = l2_slab_plan()
        hbufs = [(b.hT, 'hT'), (cx.sb([128, 32, 512], BF16, "hT2"), 'hT2')]
        ev = 0
        it = 0
        for si_, spec in enumerate(plan):
            ncols = max((e[1] + (128 if e[0] == 'fm' else e[2])) for e in spec)
            si = b.load_slab(wsl[si_, :, 0:ncols], 32, ncols)
            slab = b.slabs[si]
            for t in range(S // 512):
                hT, hk = hbufs[it % 2]
                it += 1
                self.dma(hT[:].rearrange("p k n -> p (k n)"), hTd[t], [('hTd', t)],
                         [(hk, c, g) for c in range(32) for g in range(4)], q='pool')
                for e in spec:
                    if e[0] == 'fm':
                        _, c0, ch = e
                        pi = self.bank('d' if ev % 2 == 0 else 'a')
                        for k in range(32):
                            self.mm(cx.ps[pi][:], slab[:, k, c0:c0 + 128], hT[:, k, :],
                                    [('slab', si, k)] + [(hk, k, g) for g in range(4)], [('ps', pi)],
                                    start=(k == 0), stop=(k == 31))
                        ti = b.next_tmp()
                        tmp = b.tmpf[ti]
                        if ev % 2 == 0:
                            self.vcopy(tmp[:], cx.ps[pi][:], [('ps', pi)], [('tmpf', ti)])
                        else:
                            self.act(tmp[:], cx.ps[pi][:], AF.Copy, [('ps', pi)], [('tmpf', ti)])
                        ev += 1
                        self.dma(Pfm[ch, :, t * 512:(t + 1) * 512], tmp[:], [('tmpf', ti)], [('Pfm', ch, t)])
                    else:
                        _, c0, n, d0 = e
                        for g in range(4):
                            pi = self.bank('d' if ev % 2 == 0 else 'a')
                            for k in range(32):
                                self.mm(cx.ps[pi][:, 0:n], hT[:, k, g * 128:(g + 1) * 128], slab[:, k, c0:c0 + n],
                                        [('slab', si, k), (hk, k, g)], [('ps', pi)],
                                        start=(k == 0), stop=(k == 31))
                            ti = b.next_tmp()
                            tmp = b.tmpf[ti]
                            if ev % 2 == 0:
                                self.vcopy(tmp[:, 0:n], cx.ps[pi][:, 0:n], [('ps', pi)], [('tmpf', ti)])
                            else:
                                self.act(tmp[:, 0:n], cx.ps[pi][:, 0:n], AF.Copy, [('ps', pi)], [('tmpf', ti)])
                            ev += 1
                            r0 = t * 512 + g * 128
                            self.dma(Ptm[r0:r0 + 128, d0:d0 + n], tmp[:, 0:n], [('tmpf', ti)], [('Ptm', d0, r0 // 128)])

    def rope_load(self, dst, chx, chp, Pfm, cos, sin, R1, R2, tag):
        self.dma(R1[:], Pfm[chx], [], ['R1'])
        self.dma(R2[:], Pfm[chp], [], ['R2'], q='act')
        self.tt(R1[:], R1[:], cos[:], ALU.mult, ['R1', 'cos'], ['R1'])
        self.tt(R2[:], R2[:], sin[:], ALU.mult, ['R2', 'sin'], ['R2'], eng='pool')
        self.tt(dst, R1[:], R2[:], ALU.add, ['R1', 'R2'], [tag])

    def rms_rows(self, stat, src, n, r, extra_scale=1.0):
        cx = self.cx
        junk = self.junk
        self.act(junk[:, 0:n], src, AF.Square, r, ['junk', 'st0'], accum=stat[:, 0:1])
        self.ts(stat[:, 1:2], stat[:, 0:1], extra_scale * extra_scale / n, ALU.mult, ['st0'], ['st1'], s2=EPS, op1=ALU.add)
        self.act(stat[:, 2:3], stat[:, 1:2], AF.Sqrt, ['st1'], ['st2'])
        self.recip(stat[:, 3:4], stat[:, 2:3], ['st2'], ['st3'])
        if extra_scale != 1.0:
            self.ts(stat[:, 3:4], stat[:, 3:4], extra_scale, ALU.mult, ['st3'], ['st3'])

    def diff_phase(self, Pfm, Ptm, cosd, sind, dl_ap, sub_ap, yout, lam_init):
        cx, S, nc = self.cx, self.S, self.nc
        NQ, NKB = S // 512, S // 128
        cx.push()
        cos = cx.sb([128, S], F32, "cos"); sin = cx.sb([128, S], F32, "sin")
        R1 = cx.sb([128, S], F32, "R1"); R2 = cx.sb([128, S], F32, "R2")
        qk = [cx.sb([128, S], BF16, f"qk{i}") for i in range(4)]
        V = cx.sb([128, NKB, 258], BF16, "Vext")
        E = [cx.sb([128, 512], BF16, f"E{i}") for i in range(3)]
        dl = cx.sb([128, 512], F32, "dl"); sub = cx.sb([128, 256], F32, "sub")
        st = cx.sb([128, 16], F32, "dst"); self.junk = cx.sb([128, 256], F32, "junk")
        o0 = [cx.sb([128, 256], F32, f"o0_{i}") for i in range(4)]
        o1 = [cx.sb([128, 256], F32, f"o1_{i}") for i in range(2)]
        self.dma(cos[:], cosd, [], ['cos']); self.dma(sin[:], sind, [], ['sin'])
        self.dma(dl[:], dl_ap.partition_broadcast(128), [], ['dl'])
        self.dma(sub[:], sub_ap.partition_broadcast(128), [], ['sub'])
        self.tt(self.junk[:, 0:128], dl[:, 0:128], dl[:, 128:256], ALU.mult, ['dl'], ['junk'])
        self.s.op('dve', lambda: nc.vector.reduce_sum(out=st[:, 4:5], in_=self.junk[:, 0:128], axis=mybir.AxisListType.X), ['junk'], ['l4'])
        self.tt(self.junk[:, 128:256], dl[:, 256:384], dl[:, 384:512], ALU.mult, ['dl'], ['junk'])
        self.s.op('dve', lambda: nc.vector.reduce_sum(out=st[:, 5:6], in_=self.junk[:, 128:256], axis=mybir.AxisListType.X), ['junk'], ['l5'])
        self.act(st[:, 6:7], st[:, 4:5], AF.Exp, ['l4'], ['l6'])
        self.act(st[:, 7:8], st[:, 5:6], AF.Exp, ['l5'], ['l7'])
        self.tt(st[:, 8:9], st[:, 7:8], st[:, 6:7], ALU.subtract, ['l6', 'l7'], ['l8'])
        self.ts(st[:, 8:9], st[:, 8:9], -lam_init, ALU.add, ['l8'], ['l8'])
        scale = 128.0 ** -0.5
        ei = 0
        for j in range(2):
            base = j * 8
            self.rope_load(qk[0][:], base + 0, base + 1, Pfm, cos, sin, R1, R2, 'qk0')
            self.rope_load(qk[1][:], base + 2, base + 3, Pfm, cos, sin, R1, R2, 'qk1')
            self.rope_load(qk[2][:], base + 4, base + 5, Pfm, cos, sin, R1, R2, 'qk2')
            self.rope_load(qk[3][:], base + 6, base + 7, Pfm, cos, sin, R1, R2, 'qk3')
            self.dma(V[:, :, 0:256], Ptm[:, j * 256:(j + 1) * 256].rearrange("(kb p) c -> p kb c", p=128), [], ['V'], q='pool')
            self.memset(V[:, :, 256:257], 1.0, ['V1'])
            for qb in range(NQ):
                for t in range(2):
                    accs = [0, 1, 2, 3]
                    qT, kT = qk[t], qk[2 + t]
                    pend = []
                    def pv(kb_, e_):
                        for qs in range(4):
                            self.mm(cx.ps[accs[qs]][:, 0:257], E[e_][:, qs * 128:(qs + 1) * 128], V[:, kb_, 0:257],
                                    [('E', e_), 'V', 'V1'], [('ps', accs[qs])], start=(kb_ == 0), stop=(kb_ == NKB - 1))
                    for kb in range(NKB):
                        ps = 4 + (ei % 4)
                        self.mm(cx.ps[ps][:], kT[:, kb * 128:(kb + 1) * 128], qT[:, qb * 512:(qb + 1) * 512],
                                [f'qk{t}', f'qk{2 + t}'], [('ps', ps)])
                        e = ei % 3
                        ei += 1
                        self.act(E[e][:], cx.ps[ps][:], AF.Exp, [('ps', ps)], [('E', e)], scale=scale)
                        pend.append((kb, e))
                        if len(pend) > 2:
                            pv(*pend.pop(0))
                    for it_ in pend:
                        pv(*it_)
                    for qs in range(4):
                        acc = cx.ps[accs[qs]]
                        self.recip(st[:, 9:10], acc[:, 256:257], [('ps', accs[qs])], ['r0'])
                        if t == 0:
                            self.ts(o0[qs][:], acc[:, 0:256], st[:, 9:10], ALU.mult, [('ps', accs[qs]), 'r0'], [('o0', qs)])
                        else:
                            oo = o1[qs % 2]
                            self.ts(oo[:], acc[:, 0:256], st[:, 9:10], ALU.mult, [('ps', accs[qs]), 'r0'], [('o1', qs % 2)])
                            self.stt(oo[:], oo[:], st[:, 8:9], o0[qs][:], ALU.mult, ALU.add,
                                     [('o1', qs % 2), ('o0', qs), 'l8'], [('o1', qs % 2)])
                            self.rms_rows(st, oo[:], 256, [('o1', qs % 2)])
                            self.ts(oo[:], oo[:], st[:, 3:4], ALU.mult, [('o1', qs % 2), 'st3'], [('o1', qs % 2)],
                                    s2=(1.0 - lam_init), op1=ALU.mult)
                            self.tt(oo[:], oo[:], sub[:], ALU.mult, [('o1', qs % 2), 'sub'], [('o1', qs % 2)])
                            r0 = qb * 512 + qs * 128
                            self.dma(yout[r0:r0 + 128, 1024 + j * 256:1024 + (j + 1) * 256], oo[:], [('o1', qs % 2)], [])
        cx.pop()

    def ret_phase(self, Pfm, Ptm, cosd, sind, lng_ap, yout):
        cx, S, nc = self.cx, self.S, self.nc
        NQ, NKB = S // 512, S // 128
        GW = 2 * S - 128
        OFF = S - 128
        cx.push()
        cos = cx.sb([128, S], F32, "cos"); sin = cx.sb([128, S], F32, "sin")
        R1 = cx.sb([128, S], F32, "R1"); R2 = cx.sb([128, S], F32, "R2")
        qT = cx.sb([128, S], BF16, "rq"); kT = cx.sb([128, S], BF16, "rk")
        V = cx.sb([128, NKB, 256], BF16, "rV")
        G = cx.sb([128, GW], F32, "G")
        E = [cx.sb([128, 512], BF16, f"E{i}") for i in range(3)]
        lng = cx.sb([128, 2], F32, "lng")
        st = cx.sb([128, 16], F32, "rst"); self.junk = cx.sb([128, 256], F32, "junk")
        oo = [cx.sb([128, 256], F32, f"ro{i}") for i in range(2)]
        gg = [cx.sb([128, 256], F32, f"rg{i}") for i in range(2)]
        self.dma(cos[:], cosd, [], ['cos']); self.dma(sin[:], sind, [], ['sin'])
        self.dma(lng[:], lng_ap.partition_broadcast(128), [], ['lng'])
        ei = 0
        oi = 0
        for j in range(2):
            base = 16 + j * 4
            self.rope_load(qT[:], base + 0, base + 1, Pfm, cos, sin, R1, R2, 'rq')
            self.rope_load(kT[:], base + 2, base + 3, Pfm, cos, sin, R1, R2, 'rk')
            c0 = 512 + j * 512
            self.dma(V[:], Ptm[:, c0:c0 + 256].rearrange("(kb p) c -> p kb c", p=128), [], ['V'], q='pool')
            self.s.op('pool', lambda: nc.gpsimd.iota(G[:], pattern=[[1, GW]], base=-OFF, channel_multiplier=-1,
                                                     allow_small_or_imprecise_dtypes=True), [], ['G'])
            self.stt(G[:], G[:], -1.0, G[:], ALU.mult, ALU.max, ['G'], ['G'])
            self.act(G[:], G[:], AF.Exp, ['G', 'lng'], ['G'], scale=lng[:, j:j + 1])
            for qb in range(NQ):
                accs = [0, 1, 2, 3]
                pend = []
                def pv(kb_, e_):
                    for qs in range(4):
                        self.mm(cx.ps[accs[qs]][:, 0:256], E[e_][:, qs * 128:(qs + 1) * 128], V[:, kb_, :],
                                [('E', e_), 'V'], [('ps', accs[qs])], start=(kb_ == 0), stop=(kb_ == NKB - 1))
                for kb in range(NKB):
                    ps = 4 + (ei % 4)
                    self.mm(cx.ps[ps][:], kT[:, kb * 128:(kb + 1) * 128], qT[:, qb * 512:(qb + 1) * 512],
                            ['rq', 'rk'], [('ps', ps)])
                    e = ei % 3
                    ei += 1
                    g0 = qb * 512 - kb * 128 + OFF
                    self.tt(E[e][:], cx.ps[ps][:], G[:, g0:g0 + 512], ALU.mult, [('ps', ps), 'G'], [('E', e)])
                    pend.append((kb, e))
                    if len(pend) > 2:
                        pv(*pend.pop(0))
                for it_ in pend:
                    pv(*it_)
                for qs in range(4):
                    acc = cx.ps[accs[qs]]
                    o = oo[oi % 2]; gt = gg[oi % 2]; ok_ = ('ro', oi % 2); gk = ('rg', oi % 2)
                    oi += 1
                    self.act(o[:], acc[:, 0:256], AF.Copy, [('ps', accs[qs])], [ok_])
                    self.rms_rows(st, o[:], 256, [ok_], extra_scale=128.0 ** -0.5)
                    r0 = qb * 512 + qs * 128
                    self.dma(gt[:], Ptm[r0:r0 + 128, c0 + 256:c0 + 512], [], [gk])
                    self.act(gt[:], gt[:], AF.Silu, [gk], [gk])
                    self.stt(o[:], o[:], st[:, 3:4], gt[:], ALU.mult, ALU.mult, [ok_, gk, 'st3'], [ok_])
                    self.dma(yout[r0:r0 + 128, j * 256:(j + 1) * 256], o[:], [ok_], [])
        cx.pop()

    def gdn_phase(self, Pfm, Ptm, cw_ap, alog_ap, dt_ap, gnw_ap, masks_ap, ident_ap, yout):
        cx, S, nc = self.cx, self.S, self.nc
        NDC = S // 128
        NC4 = NDC * 4
        cx.push()
        MK = cx.sb([128, 9, 128], F32, "MK")
        IDF = cx.sb([128, 128], F32, "IDF"); ONES = cx.sb([128, 128], F32, "ONES")
        cw = cx.sb([128, 60], F32, "cw"); alog = cx.sb([128, 8], F32, "alog"); dtb = cx.sb([128, 8], F32, "dtb")
        nega = cx.sb([128, 8], F32, "nega"); gnw = cx.sb([128, 128], F32, "gnw")
        AB = cx.sb([128, NDC, 16], F32, "AB"); Gm = cx.sb([128, NDC, 8], F32, "Gm"); Bm = cx.sb([128, NDC, 8], F32, "Bm")
        tA = cx.sb([128, NDC], F32, "tA")
        def gt(name):
            return [cx.sb([128, NDC, 4], F32, f"{name}{d}") for d in range(2)]
        GC, GLb, GL0, GL1, EG, EK0, EK1, EL0, EL1, BE = (gt(n) for n in
                                                       ("GC", "GLb", "GL0", "GL1", "EG", "EK0", "EK1", "EL0", "EL1", "BE"))
        R = cx.sb([128, S + 4], F32, "R"); X = cx.sb([128, S], F32, "X")
        Q = cx.sb([128, S], F32, "Q"); Kf = cx.sb([128, S], F32, "Kf")
        Ktm = cx.sb([128, NDC, 128], F32, "Ktm"); Vtm = cx.sb([128, NDC, 128], F32, "Vtm")
        Od = [cx.sb([128, NDC, 128], F32, f"O{d}") for d in range(2)]
        st = cx.sb([128, 16], F32, "gst"); self.junk = cx.sb([128, 256], F32, "junk")
        sq = [cx.sb([128, 512], F32, f"sq{i}") for i in range(2)]
        names = ("A", "B", "P0", "P1", "Q0", "Q1", "TT", "Tm", "DG", "DEC", "DQ", "QKT", "VB", "KBG", "KO0", "KO1",
                 "U", "WT", "VN", "OA", "Sst", "zt", "ot")
        Td = [[{n: cx.sb([128, 128], F32, f"g{n}{d}{l}") for n in names if n != "Sst"} for l in range(2)] for d in range(2)]
        Sd = [cx.sb([128, 128], F32, f"gSst{d}") for d in range(2)]
        local = set(names) - {"Sst"}

        self.dma(MK[:], masks_ap.rearrange("m p f -> p m f"), [], ['MK'])
        self.dma(IDF[:], ident_ap, [], ['IDF'])
        self.memset(ONES[:], 1.0, ['ONES'])
        self.dma(cw[:], cw_ap, [], ['cw'])
        self.dma(alog[:], alog_ap.partition_broadcast(128), [], ['alog'])
        self.dma(dtb[:], dt_ap.partition_broadcast(128), [], ['dtb'])
        self.dma(gnw[:], gnw_ap.partition_broadcast(128), [], ['gnw'])
        self.dma(AB[:], Ptm[:, 2048:2064].rearrange("(dc p) c -> p dc c", p=128), [], ['AB'])
        self.act(nega[:], alog[:], AF.Exp, ['alog'], ['nega'])
        self.ts(nega[:], nega[:], -1.0, ALU.mult, ['nega'], ['nega'])
        for c in range(8):
            self.act(tA[:], AB[:, :, c], AF.Exp, ['AB', 'dtb'], ['tA'], bias=dtb[:, c:c + 1])
            self.ts(tA[:], tA[:], 1.0, ALU.add, ['tA'], ['tA'])
            self.act(tA[:], tA[:], AF.Ln, ['tA'], ['tA'])
            self.ts(Gm[:, :, c], tA[:], nega[:, c:c + 1], ALU.mult, ['tA', 'nega'], ['Gm'])
            self.act(Bm[:, :, c], AB[:, :, 8 + c], AF.Sigmoid, ['AB'], ['Bm'])
        rm0, rm1 = MK[:, 3, 0:1], MK[:, 4, 0:1]
        for d in range(2):
            rhs = Gm[:, :, 4 * d:4 * d + 4]
            for (mi, dst, nm) in ((d, GC[d], 'GC'), (2, GLb[d], 'GLb'), (3, GL0[d], 'GL0'), (4, GL1[d], 'GL1')):
                pi = self.bank('d')
                self.mm(cx.ps[pi][:, 0:NC4], MK[:, mi, :], rhs, ['MK', 'Gm'], [('ps', pi)])
                self.vcopy(dst[:].rearrange("p a b -> p (a b)"), cx.ps[pi][:, 0:NC4], [('ps', pi)], [(nm, d)])
            fl = lambda t_: t_[:].rearrange("p a b -> p (a b)")
            self.act(fl(EG[d]), fl(GC[d]), AF.Exp, [('GC', d)], [('EG', d)])
            self.tt(fl(EK0[d]), fl(GLb[d]), fl(GC[d]), ALU.subtract, [('GLb', d), ('GC', d)], [('EK0', d)])
            self.act(fl(EK0[d]), fl(EK0[d]), AF.Exp, [('EK0', d)], [('EK0', d)])
            self.ts(fl(EK1[d]), fl(EK0[d]), rm1, ALU.mult, [('EK0', d), 'MK'], [('EK1', d)])
            self.ts(fl(EK0[d]), fl(EK0[d]), rm0, ALU.mult, [('EK0', d), ('EK1', d), 'MK'], [('EK0', d)])
            self.act(fl(EL0[d]), fl(GL0[d]), AF.Exp, [('GL0', d)], [('EL0', d)])
            self.act(fl(EL1[d]), fl(GL1[d]), AF.Exp, [('GL1', d)], [('EL1', d)])
            self.tt(BE[d][:], Bm[:, :, 4 * d:4 * d + 4], EG[d][:], ALU.mult, ['Bm', ('EG', d)], [('BE', d)])
        self.memset(R[:, 0:2], 0.0, ['Rpad'])
        self.memset(R[:, S + 2:S + 4], 0.0, ['Rpad'])

        def conv_silu(j, r, dst, dk_):
            self.dma(R[:, 2:S + 2], Pfm[24 + j * 3 + r], [dk_], ['R'])
            w0 = (j * 3 + r) * 5
            self.ts(dst[:], R[:, 0:S], cw[:, w0:w0 + 1], ALU.mult, ['R', 'Rpad', 'cw'], [dk_])
            for t in range(1, 5):
                self.stt(dst[:], R[:, t:S + t], cw[:, w0 + t:w0 + t + 1], dst[:], ALU.mult, ALU.add, ['R', 'Rpad', 'cw', dk_], [dk_])
            self.act(dst[:], dst[:], AF.Silu, [dk_], [dk_])

        def l2norm(dst, dk_, scale):
            for t in range(S // 512):
                sl = slice(t * 512, (t + 1) * 512)
                sqt = sq[t % 2]
                self.act(sqt[:], dst[:, sl], AF.Square, [dk_], [('sq', t % 2)])
                pi = self.bank('d')
                self.mm(cx.ps[pi][:], ONES[:], sqt[:], ['ONES', ('sq', t % 2)], [('ps', pi)])
                self.ts(sqt[:], cx.ps[pi][:], EPS, ALU.add, [('ps', pi)], [('sq', t % 2)])
                self.act(sqt[:], sqt[:], AF.Sqrt, [('sq', t % 2)], [('sq', t % 2)])
                self.recip(sqt[:], sqt[:], [('sq', t % 2)], [('sq', t % 2)])
                self.stt(dst[:, sl], dst[:, sl], scale, sqt[:], ALU.mult, ALU.mult, [dk_, ('sq', t % 2)], [dk_])

        def to_tm(src, sk_, dst, dk_):
            for d4 in range(NDC // 4):
                pi = self.bank('d')
                for i in range(4):
                    dc = d4 * 4 + i
                    self.tr(cx.ps[pi][:, i * 128:(i + 1) * 128], src[:, dc * 128:(dc + 1) * 128], IDF[:], [sk_, 'IDF'], [('ps', pi)])
                self.vcopy(dst[:, d4 * 4:(d4 + 1) * 4, :].rearrange("p a b -> p (a b)"), cx.ps[pi][:], [('ps', pi)], [dk_])

        def mmsb(dst, lhsT, rhs, r, w, acc=None, accop=None):
            pi = self.bank('d')
            self.mm(cx.ps[pi][:, 0:128], lhsT, rhs, r, [('ps', pi)])
            if acc is None:
                if getattr(self, 'chain', None) is not None:
                    self.act(dst, cx.ps[pi][:, 0:128], AF.Copy, [('ps', pi)], w)
                else:
                    self.vcopy(dst, cx.ps[pi][:, 0:128], [('ps', pi)], w)
            else:
                self.tt(dst, acc, cx.ps[pi][:, 0:128], accop, [('ps', pi)] + r[:0] + w, w)

        for j in range(4):
            conv_silu(j, 2, X, 'X')
            to_tm(X, 'X', Vtm, 'Vtm')
            conv_silu(j, 0, Q, 'Q')
            l2norm(Q, 'Q', 128.0 ** -0.5)
            conv_silu(j, 1, Kf, 'Kf')
            l2norm(Kf, 'Kf', 1.0)
            to_tm(Kf, 'Kf', Ktm, 'Ktm')
            chains = []
            for d in range(2):
                O = Od[d]
                saved_ops = self.s.ops
                self.chain = d
                col = slice(j, j + 1)
                self.s.ops = []
                self.s.keyfn = (lambda k, d=d: (k, d) if k == 'Sst' else k)
                self.memset(Sd[d][:], 0.0, ['Sst'])
                seq = self.s.ops
                order = list(range(NDC)) if d == 0 else list(range(NDC - 1, -1, -1))
                halves = (0, 1) if d == 0 else (1, 0)
                pres, scans = [], []
                for ui, dc in enumerate(order):
                    lane = ui % 2
                    T = dict(Td[d][lane]); T['Sst'] = Sd[d]
                    self.s.keyfn = (lambda k, d=d, lane=lane: (k, d, lane) if (isinstance(k, str) and k in local) else
                                    ((k, d) if k == 'Sst' else
                                     (('O', d, k[1]) if (isinstance(k, tuple) and k[0] == 'O') else k)))
                    self.s.ops = []
                    self.lane, self.stage = lane, 'pre'
                    tok = slice(dc * 128, (dc + 1) * 128)
                    gcp = GC[d][:, dc, col]; beta = Bm[:, dc, 4 * d + j:4 * d + j + 1]
                    A, B_, TT, Tm, DG, DEC, DQ, QKT = (T[n] for n in ("A", "B", "TT", "Tm", "DG", "DEC", "DQ", "QKT"))
                    self.ts(DG[:], IDF[:], gcp, ALU.mult, ['IDF', ('GC', d)], ['DG'])
                    p2 = self.bank('d')
                    self.mm(cx.ps[p2][:, 0:128], ONES[:], DG[:], ['ONES', 'DG'], [('ps', p2)])
                    self.ts(DEC[:], cx.ps[p2][:, 0:128], gcp, ALU.subtract, [('ps', p2), ('GC', d)], ['DEC'], s2=0.0, op1=ALU.max)
                    self.ts(DQ[:], cx.ps[p2][:, 0:128], gcp, ALU.subtract, [('ps', p2), ('GC', d)], ['DQ'], s2=0.0, op1=ALU.min)
                    self.act(DEC[:], DEC[:], AF.Exp, ['DEC'], ['DEC'], scale=-1.0)
                    self.act(DQ[:], DQ[:], AF.Exp, ['DQ'], ['DQ'])
                    self.stt(DEC[:], DEC[:], beta, MK[:, 5 + d, :], ALU.mult, ALU.mult, ['DEC', 'Bm', 'MK'], ['DEC'])
                    self.tt(DQ[:], DQ[:], MK[:, 7 + d, :], ALU.mult, ['DQ', 'MK'], ['DQ'])
                    p1 = self.bank('d')
                    self.mm(cx.ps[p1][:, 0:128], Kf[:, tok], Kf[:, tok], ['Kf'], [('ps', p1)])
                    self.tt(A[:], cx.ps[p1][:, 0:128], DEC[:], ALU.mult, [('ps', p1), 'DEC'], ['A'])
                    p3 = self.bank('d')
                    self.mm(cx.ps[p3][:, 0:128], Kf[:, tok], Q[:, tok], ['Kf', 'Q'], [('ps', p3)])
                    self.tt(QKT[:], cx.ps[p3][:, 0:128], DQ[:], ALU.mult, [('ps', p3), 'DQ'], ['QKT'])
                    p4 = self.bank('d')
                    self.tr(cx.ps[p4][:, 0:128], A[:], IDF[:], ['A', 'IDF'], [('ps', p4)])
                    self.act(B_[:], cx.ps[p4][:, 0:128], AF.Copy, [('ps', p4)], ['B'])
                    self.tt(TT[:], IDF[:], B_[:], ALU.subtract, ['IDF', 'B'], ['TT'], eng='pool')
                    self.tt(Tm[:], IDF[:], A[:], ALU.subtract, ['IDF', 'A'], ['Tm'], eng='pool')
                    P_, Q_, pk, qk_ = B_, A, 'B', 'A'
                    for lvl in range(5):
                        Pn, pnk = T[f"P{lvl % 2}"], f"P{lvl % 2}"
                        Qn, qnk = T[f"Q{lvl % 2}"], f"Q{lvl % 2}"
                        mmsb(Pn[:], Q_[:], P_[:], [pk, qk_], [pnk])
                        if lvl < 4:
                            mmsb(Qn[:], P_[:], Q_[:], [pk, qk_], [qnk])
                        if lvl < 4:
                            mmsb(DG[:], TT[:], Qn[:], ['TT', qnk], ['DG'])
                        pa = self.bank('d')
                        self.mm(cx.ps[pa][:, 0:128], Tm[:], Pn[:], ['Tm', pnk], [('ps', pa)])
                        self.tt(TT[:], TT[:], cx.ps[pa][:, 0:128], ALU.add, ['TT', ('ps', pa)], ['TT'])
                        if lvl < 4:
                            self.tt(Tm[:], Tm[:], DG[:], ALU.add, ['Tm', 'DG'], ['Tm'])
                        P_, Q_, pk, qk_ = Pn, Qn, pnk, qnk
                    VB, KBG, KO0, KO1, U, WT, VN, OA, Sst = (T[n] for n in ("VB", "KBG", "KO0", "KO1", "U", "WT", "VN", "OA", "Sst"))
                    self.ts(VB[:], Vtm[:, dc, :], beta, ALU.mult, ['Vtm', 'Bm'], ['VB'])
                    self.ts(KBG[:], Ktm[:, dc, :], BE[d][:, dc, col], ALU.mult, ['Ktm', ('BE', d)], ['KBG'])
                    self.ts(KO0[:], Ktm[:, dc, :], EK0[d][:, dc, col], ALU.mult, ['Ktm', ('EK0', d)], ['KO0'])
                    self.ts(KO1[:], Ktm[:, dc, :], EK1[d][:, dc, col], ALU.mult, ['Ktm', ('EK1', d)], ['KO1'])
                    mmsb(U[:], TT[:], VB[:], ['TT', 'VB'], ['U'])
                    mmsb(WT[:], KBG[:], TT[:], ['KBG', 'TT'], ['WT'])
                    pres.append(self.s.ops)
                    self.s.ops = []
                    self.stage = 'scan'
                    for hh in halves:
                        rows = slice(64 * hh, 64 * hh + 64)
                        KOh, kok = (KO0, 'KO0') if hh == 0 else (KO1, 'KO1')
                        ELh = (EL0 if hh == 0 else EL1)[d][:, dc, col]
                        elk = ('EL0' if hh == 0 else 'EL1', d)
                        pv = self.bank('d')
                        self.mm(cx.ps[pv][:, 0:128], WT[:], Sst[:], ['WT', 'Sst'], [('ps', pv)])
                        self.tt(VN[:], U[:], cx.ps[pv][:, 0:128], ALU.subtract, ['U', ('ps', pv)], ['VN'])
                        pa = self.bank('a')
                        self.mm(cx.ps[pa][:, 0:128], Q[:, tok], Sst[:], ['Q', 'Sst'], [('ps', pa)])
                        self.act(OA[:], cx.ps[pa][:, 0:128], AF.Copy, [('ps', pa), ('EG', d)], ['OA'], scale=EG[d][:, dc, col])
                        pb = self.bank('d')
                        self.mm(cx.ps[pb][:, 0:128], QKT[:], VN[:], ['QKT', 'VN'], [('ps', pb)])
                        self.tt(O[rows, dc, :], OA[rows, :], cx.ps[pb][rows, 0:128], ALU.add, ['OA', ('ps', pb)], [('O', dc)])
                        pss = self.bank('d')
                        self.mm(cx.ps[pss][:, 0:128], KOh[:], VN[:], [kok, 'VN'], [('ps', pss)])
                        self.stt(Sst[:], Sst[:], ELh, cx.ps[pss][:, 0:128], ALU.mult, ALU.add, ['Sst', elk, ('ps', pss)], ['Sst'])
                    scans.append(self.s.ops)
                for u in range(0, len(order), 2):
                    for oa_, ob_ in zip(pres[u], pres[u + 1]):
                        seq.append(oa_)
                        seq.append(ob_)
                    seq.extend(scans[u])
                    seq.extend(scans[u + 1])
                chains.append(seq)
                self.s.ops = saved_ops
                self.chain = None
                self.s.keyfn = None
            assert len(chains[0]) == len(chains[1])
            for oa_, ob_ in zip(chains[0], chains[1]):
                self.s.ops.append(oa_)
                self.s.ops.append(ob_)
            zt, ot = Td[0][0]['zt'], Td[0][0]['ot']
            for dc in range(NDC):
                r0 = dc * 128
                self.tt(ot[:], Od[0][:, dc, :], Od[1][:, dc, :], ALU.add, [('O', 0, dc), ('O', 1, dc)], ['ot'])
                self.rms_rows(st, ot[:], 128, ['ot'])
                self.stt(ot[:], ot[:], st[:, 3:4], gnw[:], ALU.mult, ALU.mult, ['ot', 'st3', 'gnw'], ['ot'])
                self.dma(zt[:], Ptm[r0:r0 + 128, 1536 + j * 128:1536 + (j + 1) * 128], [], ['zt'])
                self.act(zt[:], zt[:], AF.Silu, ['zt'], ['zt'])
                self.tt(ot[:], ot[:], zt[:], ALU.mult, ['ot', 'zt'], ['ot'])
                self.dma(yout[r0:r0 + 128, 512 + j * 128:512 + (j + 1) * 128], ot[:], ['ot'], [])
        cx.pop()


def gdn_masks():
    p = np.arange(128)[:, None]
    f = np.arange(128)[None, :]
    same = (p // 64) == (f // 64)
    m = np.zeros((9, 128, 128), np.float32)
    m[0] = same & (p <= f)
    m[1] = same & (p >= f)
    m[2] = same
    m[3] = (p < 64) & (f >= 0)
    m[4] = (p >= 64) & (f >= 0)
    m[5] = same & (p > f)
    m[6] = same & (p < f)
    m[7] = same & (f >= p)
    m[8] = same & (f <= p)
    return m


def build_l2_prog(S, layer):
    lam_init = 0.8 - 0.6 * math.exp(-0.3 * layer)
    nc = bass.Bass("TRN2", target_bir_lowering=False)
    NT = S // 512
    x = dram_in(nc, "x", [S, D])
    ln = dram_in(nc, "ln", [128, 32])
    pos = dram_in(nc, "pos", [S], I32)
    wsl = dram_in(nc, "wsl", [15, D, 512])
    ident = dram_in(nc, "ident", [128, 128])
    tab = dram_in(nc, "tab", [128, 4])
    lng = dram_in(nc, "lng", [2])
    cw = dram_in(nc, "cw", [128, 60])
    alog = dram_in(nc, "alog", [8]); dtb = dram_in(nc, "dtb", [8]); gnw = dram_in(nc, "gnw", [128])
    masks = dram_in(nc, "masks", [9, 128, 128])
    dl = dram_in(nc, "dl", [512]); sub = dram_in(nc, "sub", [256])
    y = dram_out(nc, "y", [S, YW])
    hTd = dram_tmp(nc, "hTd", [NT, 128, 32 * 512], BF16)
    Pfm = dram_tmp(nc, "Pfm", [NFM, 128, S]); Ptm = dram_tmp(nc, "Ptm", [S, NTM])
    cosd = dram_tmp(nc, "cosd", [128, S]); sind = dram_tmp(nc, "sind", [128, S])
    cx = Ctx(nc)
    mx = Mix(cx, S)
    mx.rope_tables(pos, tab, cosd, sind)
    cx.push()
    b = Blocks(cx, ident)
    mx.phase0(b, x, ln, hTd)
    cx.s.barrier()
    mx.phase1(b, wsl, hTd, Pfm, Ptm)
    cx.pop()
    mx.diff_phase(Pfm, Ptm, cosd, sind, dl, sub, y, lam_init)
    mx.ret_phase(Pfm, Ptm, cosd, sind, lng, y)
    mx.gdn_phase(Pfm, Ptm, cw, alog, dtb, gnw, masks, ident, y)
    cx.s.finish()
    return nc


O_RQ, O_RK, O_RV, O_RG = 0, 1024, 2048, 4096
O_GQKV, O_GZ, O_GA, O_GB = 6144, 12288, 14336, 14368
O_DQ, O_DK, O_DV, O_GATE = 14400, 16448, 18496, 20544
_PERM = (np.arange(128) + 64) % 128


def l2_weight_slabs(w_in, g):
    out = np.zeros((15, D, 512), np.float32)
    si = 0
    for j in range(2):
        h = 2 * g + j
        for base in (O_DQ, O_DK):
            for t in range(2):
                c0 = base + (2 * h + t) * 128
                blk = w_in[:, c0:c0 + 128]
                out[si, :, (2 * t) * 128:(2 * t + 1) * 128] = blk
                out[si, :, (2 * t + 1) * 128:(2 * t + 2) * 128] = blk[:, _PERM]
            si += 1
        out[si, :, 0:256] = w_in[:, O_DV + h * 256:O_DV + (h + 1) * 256]
        si += 1
    for j in range(2):
        h = 2 * g + j
        for n, base in enumerate((O_RQ, O_RK)):
            blk = w_in[:, base + h * 128:base + (h + 1) * 128]
            out[si, :, (2 * n) * 128:(2 * n + 1) * 128] = blk
            out[si, :, (2 * n + 1) * 128:(2 * n + 2) * 128] = blk[:, _PERM]
        si += 1
        out[si, :, 0:256] = w_in[:, O_RV + h * 256:O_RV + (h + 1) * 256]
        out[si, :, 256:512] = w_in[:, O_RG + h * 256:O_RG + (h + 1) * 256]
        si += 1
    for j in range(4):
        h = 4 * g + j
        for r in range(3):
            out[si, :, r * 128:(r + 1) * 128] = w_in[:, O_GQKV + r * 2048 + h * 128:O_GQKV + r * 2048 + (h + 1) * 128]
        out[si, :, 384:512] = w_in[:, O_GZ + h * 128:O_GZ + (h + 1) * 128]
        si += 1
    for n, base in enumerate((O_GA, O_GB)):
        for d in range(2):
            c0 = base + d * 16 + 4 * g
            out[si, :, n * 8 + d * 4:n * 8 + d * 4 + 4] = w_in[:, c0:c0 + 4]
    return out


def l2_consts(inputs, layer, g):
    conv_w = np.asarray(inputs["conv_w"][layer], np.float32)
    cw = np.zeros((128, 60), np.float32)
    for j in range(4):
        h = 4 * g + j
        for r in range(3):
            for t in range(5):
                cw[:, (j * 3 + r) * 5 + t] = conv_w[t, r * 2048 + h * 128:r * 2048 + (h + 1) * 128]
    a_log = np.asarray(inputs["gdn_a_log"][layer], np.float32)
    dtb = np.asarray(inputs["gdn_dt_bias"][layer], np.float32)
    inv = (1.0 / (10000.0 ** (np.arange(0, 128, 2, dtype=np.float32) / 128))).astype(np.float32)
    tab = np.zeros((128, 4), np.float32)
    tab[:, 0] = np.concatenate([inv, inv])
    tab[:, 1] = np.where(np.arange(128) < 64, -1.0, 1.0)
    heads = np.arange(2 * g, 2 * g + 2, dtype=np.float32)
    lng = np.log(1.0 - 2.0 ** (-5.0 - heads)).astype(np.float32)
    return {
        "cw": cw,
        "alog": np.ascontiguousarray(a_log[:, 4 * g:4 * g + 4]).reshape(-1),
        "dtb": np.ascontiguousarray(dtb[:, 4 * g:4 * g + 4]).reshape(-1),
        "gnw": np.asarray(inputs["gdn_norm_w"][layer], np.float32),
        "masks": gdn_masks(),
        "dl": np.asarray(inputs["diff_lambda"][layer], np.float32).reshape(-1),
        "sub": np.asarray(inputs["diff_subln_w"][layer], np.float32),
        "tab": tab, "lng": lng, "ident": np.eye(128, dtype=np.float32),
    }
```

```python
import math
import numpy as np
import concourse.bass as bass
import concourse.mybir as mybir
from concourse.bass_utils import run_bass_kernel_spmd

F32 = mybir.dt.float32
BF16 = mybir.dt.bfloat16
I32 = mybir.dt.int32
ALU = mybir.AluOpType
AF = mybir.ActivationFunctionType

D = 4096
DFF = 8192
SEQ = 4096
NB = 2
DEPTH = 2
PLE = 256
EPS = 1e-6
NCORES = 8
SEM_LIM = 20000


class Sched:
    def __init__(self, nc):
        self.nc = nc
        self.E = dict(pe=nc.tensor, dve=nc.vector, act=nc.scalar, pool=nc.gpsimd, sp=nc.sync)
        self.ops = []
        self.keyfn = None
        self.esems = {e: [] for e in self.E}
        self.ecount = {e: 0 for e in self.E}
        self.RING = 12
        self.dsems = {}
        self.dcount = {e: 0 for e in self.E}
        self.seen = {e: {} for e in self.E}
        self.nsem = 0
        self.last_w = {}
        self.readers = {}
        self.tok = {}
        self.uid = 0

    def _newsem(self, tag):
        self.nsem += 1
        return self.nc.semaphore(f"{tag}{self.nsem}").__enter__()

    def op(self, eng, fn, reads=(), writes=(), dma=False):
        kf = self.keyfn
        if kf is not None:
            reads = [kf(k) for k in reads]
            writes = [kf(k) for k in writes]
        self.ops.append((eng, fn, tuple(reads), tuple(writes), dma))

    def emit(self):
        ops = self.ops
        self.ops = []
        n = len(ops)
        base = self.uid
        deps = []
        needs = [False] * n
        last_w, readers = self.last_w, self.readers
        for i, (eng, fn, R, W, dma) in enumerate(ops):
            u = base + i
            d = set()
            for k in R:
                if k in last_w:
                    d.add(last_w[k])
            for k in W:
                if k in last_w:
                    d.add(last_w[k])
                rs = readers.get(k)
                if rs:
                    d.update(rs)
            d.discard(u)
            dd = []
            for j in d:
                if j >= base:
                    je, _, _, _, jd = ops[j - base]
                    if je == eng and eng == 'pe' and not jd:
                        continue
                    needs[j - base] = True
                    dd.append(j)
                else:
                    t = self.tok.get(j)
                    if t is not None:
                        if t[3] == eng and eng == 'pe' and not t[4]:
                            continue
                        dd.append(j)
            deps.append(dd)
            for k in R:
                readers.setdefault(k, []).append(u)
            for k in W:
                last_w[k] = u
                readers[k] = []
        lastidx = {}
        for i, (eng, fn, R, W, dma) in enumerate(ops):
            lastidx[eng] = i
        for eng, i in lastidx.items():
            needs[i] = True
        for i, (eng, fn, R, W, dma) in enumerate(ops):
            u = base + i
            e = self.E[eng]
            seen = self.seen[eng]
            for j in deps[i]:
                t = self.tok[j]
                name, sem, val = t[0], t[1], t[2]
                if seen.get(name, 0) >= val:
                    continue
                e.wait_ge(sem, val)
                seen[name] = val
            if dma:
                ring = self.dsems.setdefault(eng, [])
                c = self.dcount[eng]
                slot = c % self.RING
                if slot >= len(ring):
                    ring.append([self._newsem(f"d{eng}"), 0])
                sem, uses = ring[slot]
                name = f"d{eng}{slot}"
                if uses > 0 and seen.get(name, 0) < 16 * uses:
                    e.wait_ge(sem, 16 * uses)
                    seen[name] = 16 * uses
                inst = fn()
                inst.then_inc(sem, 16)
                ring[slot][1] = uses + 1
                self.dcount[eng] = c + 1
                self.tok[u] = (name, sem, 16 * (uses + 1), eng, True)
            else:
                inst = fn()
                if needs[i]:
                    c = self.ecount[eng]
                    k = c // SEM_LIM
                    sl = self.esems[eng]
                    if k >= len(sl):
                        sl.append(self._newsem(f"e{eng}"))
                    sem = sl[k]
                    inst.then_inc(sem, 1)
                    self.ecount[eng] = c + 1
                    self.tok[u] = (f"e{eng}{k}", sem, c % SEM_LIM + 1, eng, False)
        self.uid = base + n
        live = set(last_w.values())
        for rs in readers.values():
            live.update(rs)
        self.tok = {k: v for k, v in self.tok.items() if k in live or k >= self.uid - 64}

    def barrier(self):
        self.emit()
        waits = []
        for eng, ring in self.dsems.items():
            for slot, (sem, uses) in enumerate(ring):
                if uses > 0:
                    waits.append((f"d{eng}{slot}", sem, 16 * uses))
        for eng, sl in self.esems.items():
            c = self.ecount[eng]
            if c > 0:
                k = (c - 1) // SEM_LIM
                waits.append((f"e{eng}{k}", sl[k], (c - 1) % SEM_LIM + 1))
        for eng, e in self.E.items():
            seen = self.seen[eng]
            for name, sem, val in waits:
                if seen.get(name, 0) < val:
                    e.wait_ge(sem, val)
                    seen[name] = val
        self.last_w.clear()
        self.readers.clear()
        self.tok.clear()

    def finish(self):
        self.emit()
        sp = self.E['sp']
        for eng, ring in self.dsems.items():
            for slot, (sem, uses) in enumerate(ring):
                if uses > 0:
                    sp.wait_ge(sem, 16 * uses)
        for eng, sl in self.esems.items():
            c = self.ecount[eng]
            if c > 0:
                k = (c - 1) // SEM_LIM
                sp.wait_ge(sl[k], (c - 1) % SEM_LIM + 1)


class Ctx:
    def __init__(self, nc):
        self.nc = nc
        self.s = Sched(nc)
        self.nbuf = 0
        self.live = []
        self.stack = []
        self.ps = [self.sb_psum(f"ps{i}") for i in range(8)]
        self.psrr = 0

    def sb(self, shape, dt, name=None):
        self.nbuf += 1
        cm = self.nc.sbuf_tensor((name or "b") + f"_{self.nbuf}", list(shape), dt)
        t = cm.__enter__()
        self.live.append(cm)
        return t

    def push(self):
        self.stack.append(len(self.live))

    def pop(self):
        self.s.barrier()
        n = self.stack.pop()
        while len(self.live) > n:
            self.live.pop().__exit__(None, None, None)

    def sb_psum(self, name):
        return self.nc.psum_tensor(name, [128, 512], F32).__enter__()


def dram_in(nc, name, shape, dt=F32):
    return nc.dram_tensor(name, list(shape), dt, kind="ExternalInput").ap()


def dram_out(nc, name, shape, dt=F32):
    return nc.dram_tensor(name, list(shape), dt, kind="ExternalOutput").ap()


def dram_tmp(nc, name, shape, dt=F32):
    return nc.dram_tensor(name, list(shape), dt, kind="Internal").ap()


class Blocks:
    def __init__(self, cx, ident_d):
        self.cx = cx
        nc, s = cx.nc, cx.s
        self.nc, self.s = nc, s
        self.lnT = cx.sb([128, 32], F32, "lnT")
        self.stat = cx.sb([128, 8], F32, "stat")
        self.identf = cx.sb([128, 128], F32, "identf")
        self.identb = cx.sb([128, 128], BF16, "identb")
        if getattr(cx, 'want_identf', False):
            s.op('sp', lambda: nc.sync.dma_start(out=self.identf[:], in_=ident_d[:, :]), writes=['identf'], dma=True)
        s.op('pool', lambda: nc.gpsimd.dma_start(out=self.identb[:], in_=ident_d[:, :]), writes=['identb'], dma=True)
        self.slabs = [cx.sb([128, 32, 512], BF16, f"slab{i}") for i in range(2)]
        self.slab_i = 0
        self.hT = cx.sb([128, 32, 512], BF16, "hT")
        self.xin = cx.sb([128, D], F32, "xin")
        self.xn = cx.sb([128, D], BF16, "xn")
        self.ps_i = 0
        self.LNW = 32
        self.tmp_i = 0
        self.tmpf = [cx.sb([128, 512], F32, f"tmpf{i}") for i in range(4)]
        self.xres = [cx.sb([128, 512], F32, f"xres{i}") for i in range(2)]
        self.xres_i = 0
        self.xo = [cx.sb([128, 512], F32, f"xo{i}") for i in range(2)]
        self.xo_i = 0

    def next_ps(self, lo=0, hi=8):
        i = lo + self.ps_i % (hi - lo)
        self.ps_i += 1
        return i

    def next_tmp(self):
        i = self.tmp_i % len(self.tmpf)
        self.tmp_i += 1
        return i

    def load_slab(self, w_ap, kc, ncols):
        nc, s = self.nc, self.s
        i = self.slab_i % len(self.slabs)
        self.slab_i += 1
        slab = self.slabs[i]
        src = w_ap.rearrange("(k p) n -> p k n", p=128)
        step = 8
        for k0 in range(0, kc, step):
            k1 = min(kc, k0 + step)
            s.op('pool', (lambda k0=k0, k1=k1: nc.gpsimd.dma_start(out=slab[:, k0:k1, 0:ncols], in_=src[:, k0:k1, :])),
                 writes=[('slab', i, k) for k in range(k0, k1)], dma=True)
        return i

    def load_ln(self, ln_ap):
        nc, s = self.nc, self.s
        s.op('sp', lambda: nc.sync.dma_start(out=self.lnT[:], in_=ln_ap), writes=['lnT'], dma=True)

    def norm_group(self, x_rows_ap, g, hT=None, rk=None, hkf=None):
        nc, s = self.nc, self.s
        hT = hT if hT is not None else self.hT
        hkf = hkf or (lambda c_, g_: ('hT', c_, g_))
        xin, xn, stat = self.xin, self.xn, self.stat
        s.op('sp', lambda: nc.sync.dma_start(out=xin[:], in_=x_rows_ap), writes=['xin'],
             reads=([('dr', rk[0], rk[1], c) for c in range(8)] if rk else []), dma=True)
        s.op('act', lambda: nc.scalar.activation(out=xn[:], in_=xin[:], func=AF.Square, accum_out=stat[:, 0:1]),
             reads=['xin'], writes=['xn', 'stat0'])
        s.op('dve', lambda: nc.vector.tensor_scalar(out=stat[:, 1:2], in0=stat[:, 0:1], scalar1=1.0 / D, scalar2=EPS,
                                                    op0=ALU.mult, op1=ALU.add), reads=['stat0'], writes=['stat1'])
        s.op('act', lambda: nc.scalar.activation(out=stat[:, 2:3], in_=stat[:, 1:2], func=AF.Sqrt),
             reads=['stat1'], writes=['stat2'])
        s.op('dve', lambda: nc.vector.reciprocal(out=stat[:, 3:4], in_=stat[:, 2:3]), reads=['stat2'], writes=['stat3'])
        s.op('dve', lambda: nc.vector.tensor_scalar(out=xn[:], in0=xin[:], scalar1=stat[:, 3:4], scalar2=None,
                                                    op0=ALU.mult), reads=['xin', 'stat3'], writes=['xn'])
        for c4 in range(getattr(self, 'NR', 8)):
            pi = self.next_ps(6, 8)
            psb = self.cx.ps[pi][:].bitcast(BF16)
            for j in range(4):
                c = c4 * 4 + j
                s.op('pe', (lambda c=c, j=j, psb=psb: nc.tensor.transpose(out=psb[:, j * 128:(j + 1) * 128],
                                                                         in_=xn[:, c * 128:(c + 1) * 128],
                                                                         identity=self.identb[:])),
                     reads=['xn', 'identb'], writes=[('ps', pi)])
            for j in range(4):
                c = c4 * 4 + j
                if c4 % 2 == 0:
                    s.op('dve', (lambda c=c, j=j, psb=psb: nc.vector.tensor_scalar(
                        out=hT[:, c, g * 128:(g + 1) * 128], in0=psb[:, j * 128:(j + 1) * 128],
                        scalar1=self.lnT[:, (c % self.LNW):(c % self.LNW) + 1], scalar2=None, op0=ALU.mult)),
                         reads=[('ps', pi), 'lnT'], writes=[hkf(c, g)])
                else:
                    s.op('act', (lambda c=c, j=j, psb=psb: nc.scalar.activation(
                        out=hT[:, c, g * 128:(g + 1) * 128], in_=psb[:, j * 128:(j + 1) * 128],
                        func=AF.Copy, scale=self.lnT[:, (c % self.LNW):(c % self.LNW) + 1])),
                         reads=[('ps', pi), 'lnT'], writes=[hkf(c, g)])

    def hT_keys(self, kc=32, ng=4):
        return [('hT', c, g) for c in range(kc) for g in range(ng)]

    def ffn(self, x_ap, out_ap, ln_ap, wg, wu, wd, gT, xk=None, ok=None):
        nc, s = self.nc, self.s
        self.load_ln(ln_ap)
        for g in range(4):
            self.norm_group(x_ap[g * 128:(g + 1) * 128, :], g, rk=(xk[0], xk[1] * 4 + g) if xk else None)
        hk = self.hT_keys()
        for sl in range(DFF // 512):
            ig = self.load_slab(wg[:, sl * 512:(sl + 1) * 512], 32, 512)
            iu = self.load_slab(wu[:, sl * 512:(sl + 1) * 512], 32, 512)
            for c in range(4):
                pg, pu = self.next_ps(0, 6), self.next_ps(0, 6)
                for (pi, si) in ((pg, ig), (pu, iu)):
                    slab = self.slabs[si]
                    for k in range(32):
                        s.op('pe', (lambda k=k, c=c, pi=pi, slab=slab: nc.tensor.matmul(
                            self.cx.ps[pi][:], slab[:, k, c * 128:(c + 1) * 128], self.hT[:, k, :],
                            start=(k == 0), stop=(k == 31))),
                             reads=[('slab', si, k)] + ([('hT', k, g) for g in range(4)]),
                             writes=[('ps', pi)])
                ti = self.next_tmp()
                tmp = self.tmpf[ti]
                s.op('act', (lambda pg=pg, tmp=tmp: nc.scalar.activation(out=tmp[:], in_=self.cx.ps[pg][:], func=AF.Silu)),
                     reads=[('ps', pg)], writes=[('tmpf', ti)])
                fc = sl * 4 + c
                s.op('dve', (lambda pu=pu, tmp=tmp, fc=fc: nc.vector.tensor_tensor(
                    out=gT[:, fc, :], in0=tmp[:], in1=self.cx.ps[pu][:], op=ALU.mult)),
                     reads=[('ps', pu), ('tmpf', ti)], writes=[('gT', fc)])
        nF = DFF // 128
        npiece = max(1, nF // 32)
        kcp = nF // npiece
        for dsl in range(D // 512):
            pis = [self.next_ps(0, 6) for _ in range(4)]
            for piece in range(npiece):
                si = self.load_slab(wd[piece * kcp * 128:(piece + 1) * kcp * 128, dsl * 512:(dsl + 1) * 512], kcp, 512)
                slab = self.slabs[si]
                for g in range(4):
                    for k in range(kcp):
                        fk = piece * kcp + k
                        s.op('pe', (lambda g=g, k=k, fk=fk, slab=slab, pi=pis[g]: nc.tensor.matmul(
                            self.cx.ps[pi][:], gT[:, fk, g * 128:(g + 1) * 128], slab[:, k, :],
                            start=(fk == 0), stop=(fk == nF - 1))),
                             reads=[('slab', si, k), ('gT', fk)], writes=[('ps', pis[g])])
            for g in range(4):
                self.residual_out(x_ap[g * 128:(g + 1) * 128, dsl * 512:(dsl + 1) * 512],
                                  out_ap[g * 128:(g + 1) * 128, dsl * 512:(dsl + 1) * 512], pis[g], 0.5,
                                  sk=('dr', xk[0], xk[1] * 4 + g, dsl) if xk else None,
                                  dk=('dr', ok[0], ok[1] * 4 + g, dsl) if ok else None)

    def mix_merge(self, x_ap, yT_ap, ln_ap, wgate, wbr, gT, xk=None):
        nc, s = self.nc, self.s
        self.load_ln(ln_ap)
        for g in range(4):
            self.norm_group(x_ap[g * 128:(g + 1) * 128, :], g, rk=(xk[0], xk[1] * 4 + g) if xk else None)
        ybufs = [gT[:, 32:48, :], gT[:, 48:64, :]]
        macc = [self.xin[:, c * 512:(c + 1) * 512] for c in range(4)]
        yb_i = 0
        for dsl in range(D // 512):
            for b in range(3):
                yi = yb_i % 2
                yb_i += 1
                ybuf = ybufs[yi]
                if dsl == 0 or True:
                    src = yT_ap[b * 2048:(b + 1) * 2048, :].rearrange("(k p) n -> p k n", p=128)
                    for k0 in (0, 8):
                        s.op('pool', (lambda k0=k0, ybuf=ybuf, src=src: nc.gpsimd.dma_start(
                            out=ybuf[:, k0:k0 + 8, :], in_=src[:, k0:k0 + 8, :])),
                             writes=[('gT', 32 + 16 * yi + k) for k in range(k0, k0 + 8)], dma=True)
                ig = self.load_slab(wgate[:, b * D + dsl * 512: b * D + (dsl + 1) * 512], 32, 512)
                ib = self.load_slab(wbr[b, :, dsl * 512:(dsl + 1) * 512], 16, 512)
                sg, sbr = self.slabs[ig], self.slabs[ib]
                for c in range(4):
                    pg, py = self.next_ps(0, 6), self.next_ps(0, 6)
                    for k in range(32):
                        s.op('pe', (lambda k=k, c=c, pg=pg, sg=sg: nc.tensor.matmul(
                            self.cx.ps[pg][:], sg[:, k, c * 128:(c + 1) * 128], self.hT[:, k, :],
                            start=(k == 0), stop=(k == 31))),
                             reads=[('slab', ig, k)] + [('hT', k, g) for g in range(4)], writes=[('ps', pg)])
                    for k in range(16):
                        s.op('pe', (lambda k=k, c=c, py=py, sbr=sbr, ybuf=ybuf: nc.tensor.matmul(
                            self.cx.ps[py][:], sbr[:, k, c * 128:(c + 1) * 128], ybuf[:, k, :],
                            start=(k == 0), stop=(k == 15))),
                             reads=[('slab', ib, k), ('gT', 32 + 16 * yi + k)], writes=[('ps', py)])
                    ti = self.next_tmp()
                    tmp = self.tmpf[ti]
                    s.op('act', (lambda pg=pg, tmp=tmp: nc.scalar.activation(out=tmp[:], in_=self.cx.ps[pg][:], func=AF.Sigmoid)),
                         reads=[('ps', pg)], writes=[('tmpf', ti)])
                    if b == 0:
                        s.op('dve', (lambda py=py, tmp=tmp, c=c: nc.vector.tensor_tensor(
                            out=macc[c], in0=tmp[:], in1=self.cx.ps[py][:], op=ALU.mult)),
                             reads=[('ps', py), ('tmpf', ti)], writes=['xin'])
                    else:
                        s.op('dve', (lambda py=py, tmp=tmp: nc.vector.tensor_tensor(
                            out=tmp[:], in0=tmp[:], in1=self.cx.ps[py][:], op=ALU.mult)),
                             reads=[('ps', py), ('tmpf', ti)], writes=[('tmpf', ti)])
                        s.op('dve', (lambda tmp=tmp, c=c: nc.vector.tensor_tensor(
                            out=macc[c], in0=macc[c], in1=tmp[:], op=ALU.add)),
                             reads=[('tmpf', ti), 'xin'], writes=['xin'])
                    if b == 2:
                        mc = dsl * 4 + c
                        s.op('act', (lambda c=c, mc=mc: nc.scalar.copy(out=gT[:, mc, :], in_=macc[c])),
                             reads=['xin'], writes=[('gT', mc)])

    def wo_proj(self, x_ap, out_ap, wo, gT, xk=None, ok=None):
        nc, s = self.nc, self.s
        for dsl in range(D // 512):
            si = self.load_slab(wo[:, dsl * 512:(dsl + 1) * 512], 32, 512)
            slab = self.slabs[si]
            for g in range(4):
                pi = self.next_ps(0, 6)
                for k in range(32):
                    s.op('pe', (lambda g=g, k=k, slab=slab, pi=pi: nc.tensor.matmul(
                        self.cx.ps[pi][:], gT[:, k, g * 128:(g + 1) * 128], slab[:, k, :],
                        start=(k == 0), stop=(k == 31))),
                         reads=[('slab', si, k), ('gT', k)], writes=[('ps', pi)])
                self.residual_out(x_ap[g * 128:(g + 1) * 128, dsl * 512:(dsl + 1) * 512],
                                  out_ap[g * 128:(g + 1) * 128, dsl * 512:(dsl + 1) * 512], pi, 1.0,
                                  sk=('dr', xk[0], xk[1] * 4 + g, dsl) if xk else None,
                                  dk=('dr', ok[0], ok[1] * 4 + g, dsl) if ok else None)

    def ple(self, x_ap, out_ap, ln_ap, pT_ap, wpg, wpp, gT, xk=None, ok=None):
        nc, s = self.nc, self.s
        self.load_ln(ln_ap)
        for g in range(4):
            self.norm_group(x_ap[g * 128:(g + 1) * 128, :], g, rk=(xk[0], xk[1] * 4 + g) if xk else None)
        pT = gT[:, 32:34, :]
        s.op('pool', lambda: nc.gpsimd.dma_start(out=pT, in_=pT_ap.rearrange("(k p) n -> p k n", p=128)),
             writes=[('gT', 32), ('gT', 33)], dma=True)
        for dsl in range(D // 512):
            ig = self.load_slab(wpg[:, dsl * 512:(dsl + 1) * 512], 32, 512)
            ip = self.load_slab(wpp[:, dsl * 512:(dsl + 1) * 512], 2, 512)
            sg, sp_ = self.slabs[ig], self.slabs[ip]
            for g in range(4):
                pg, pp = self.next_ps(0, 6), self.next_ps(0, 6)
                for k in range(32):
                    s.op('pe', (lambda g=g, k=k, sg=sg, pg=pg: nc.tensor.matmul(
                        self.cx.ps[pg][:], self.hT[:, k, g * 128:(g + 1) * 128], sg[:, k, :],
                        start=(k == 0), stop=(k == 31))),
                         reads=[('slab', ig, k), ('hT', k, g)], writes=[('ps', pg)])
                for k in range(2):
                    s.op('pe', (lambda g=g, k=k, sp_=sp_, pp=pp: nc.tensor.matmul(
                        self.cx.ps[pp][:], pT[:, k, g * 128:(g + 1) * 128], sp_[:, k, :],
                        start=(k == 0), stop=(k == 1))),
                         reads=[('slab', ip, k), ('gT', 32 + k)], writes=[('ps', pp)])
                ti = self.next_tmp()
                tmp = self.tmpf[ti]
                s.op('act', (lambda pg=pg, tmp=tmp: nc.scalar.activation(out=tmp[:], in_=self.cx.ps[pg][:], func=AF.Sigmoid)),
                     reads=[('ps', pg)], writes=[('tmpf', ti)])
                s.op('dve', (lambda pp=pp, tmp=tmp: nc.vector.tensor_tensor(
                    out=tmp[:], in0=tmp[:], in1=self.cx.ps[pp][:], op=ALU.mult)),
                     reads=[('ps', pp), ('tmpf', ti)], writes=[('tmpf', ti)])
                ri = self.xres_i % 2
                self.xres_i += 1
                oi = self.xo_i % 2
                self.xo_i += 1
                xr, xo = self.xres[ri], self.xo[oi]
                xs = x_ap[g * 128:(g + 1) * 128, dsl * 512:(dsl + 1) * 512]
                ds = out_ap[g * 128:(g + 1) * 128, dsl * 512:(dsl + 1) * 512]
                s.op('sp', (lambda xr=xr, xs=xs: nc.sync.dma_start(out=xr[:], in_=xs)), writes=[('xres', ri)],
                     reads=([('dr', xk[0], xk[1] * 4 + g, dsl)] if xk else []), dma=True)
                s.op('dve', (lambda xr=xr, xo=xo, tmp=tmp: nc.vector.tensor_tensor(out=xo[:], in0=tmp[:], in1=xr[:], op=ALU.add)),
                     reads=[('tmpf', ti), ('xres', ri)], writes=[('xo', oi)])
                s.op('sp', (lambda xo=xo, ds=ds: nc.sync.dma_start(out=ds, in_=xo[:])), reads=[('xo', oi)],
                     writes=([('dr', ok[0], ok[1] * 4 + g, dsl)] if ok else []), dma=True)

    def final_norm(self, x_ap, out_ap, fn_ap, gT, ntok, xname=None):
        nc, s = self.nc, self.s
        fnb = gT[:, 0:16, :].rearrange("p a b -> p (a b)").bitcast(F32)
        s.op('sp', lambda: nc.sync.dma_start(out=fnb, in_=fn_ap.partition_broadcast(128)),
             writes=[('gT', k) for k in range(16)], dma=True)
        xin, stat = self.xin, self.stat
        for g in range(ntok // 128):
            xs = x_ap[g * 128:(g + 1) * 128, :]
            ds = out_ap[g * 128:(g + 1) * 128, :]
            s.op('sp', (lambda xs=xs: nc.sync.dma_start(out=xin[:], in_=xs)), writes=['xin'],
                 reads=([('dr', xname, g, c) for c in range(8)] if xname else []), dma=True)
            s.op('act', lambda: nc.scalar.activation(out=self.xn[:], in_=xin[:], func=AF.Square, accum_out=stat[:, 0:1]),
                 reads=['xin'], writes=['xn', 'stat0'])
            s.op('dve', lambda: nc.vector.tensor_scalar(out=stat[:, 1:2], in0=stat[:, 0:1], scalar1=1.0 / D, scalar2=EPS,
                                                        op0=ALU.mult, op1=ALU.add), reads=['stat0'], writes=['stat1'])
            s.op('act', lambda: nc.scalar.activation(out=stat[:, 2:3], in_=stat[:, 1:2], func=AF.Sqrt),
                 reads=['stat1'], writes=['stat2'])
            s.op('dve', lambda: nc.vector.reciprocal(out=stat[:, 3:4], in_=stat[:, 2:3]), reads=['stat2'], writes=['stat3'])
            s.op('dve', lambda: nc.vector.scalar_tensor_tensor(out=xin[:], in0=xin[:], scalar=stat[:, 3:4], in1=fnb,
                                                               op0=ALU.mult, op1=ALU.mult),
                 reads=['xin', 'stat3'] + [('gT', k) for k in range(16)], writes=['xin'])
            s.op('sp', (lambda ds=ds: nc.sync.dma_start(out=ds, in_=xin[:])), reads=['xin'], dma=True)

    def ffn2(self, x_ap, out_ap, ln_ap, wg, wu, wd, gT, xk, ok):
        nc, s = self.nc, self.s
        self.load_ln(ln_ap)
        hb = [self.hT, gT[:, 0:32, :]]
        hkeys = [lambda k_: [('hT', k_, g_) for g_ in range(4)], lambda k_: [('gT', k_)]]
        for tt in range(2):
            for g in range(4):
                r0 = tt * 512 + g * 128
                self.norm_group(x_ap[r0:r0 + 128, :], g, hT=hb[tt], rk=(xk, tt * 4 + g),
                                hkf=(None if tt == 0 else (lambda c_, g_: ('gT', c_))))
        FQ = DFF // 4
        nsl = FQ // 512
        kcq = FQ // 128
        for q in range(4):
            for sl in range(nsl):
                c0 = (q * nsl + sl) * 512
                ig = self.load_slab(wg[:, c0:c0 + 512], 32, 512)
                iu = self.load_slab(wu[:, c0:c0 + 512], 32, 512)
                for c in range(4):
                    fc = sl * 4 + c
                    for tt in range(2):
                        pg, pu = self.next_ps(0, 6), self.next_ps(0, 6)
                        for (pi, si) in ((pg, ig), (pu, iu)):
                            slab = self.slabs[si]
                            for k in range(32):
                                s.op('pe', (lambda k=k, c=c, pi=pi, slab=slab, tt=tt: nc.tensor.matmul(
                                    self.cx.ps[pi][:], slab[:, k, c * 128:(c + 1) * 128], hb[tt][:, k, :],
                                    start=(k == 0), stop=(k == 31))),
                                     reads=[('slab', si, k)] + hkeys[tt](k), writes=[('ps', pi)])
                        ti = self.next_tmp()
                        tmp = self.tmpf[ti]
                        s.op('act', (lambda pg=pg, tmp=tmp: nc.scalar.activation(out=tmp[:], in_=self.cx.ps[pg][:], func=AF.Silu)),
                             reads=[('ps', pg)], writes=[('tmpf', ti)])
                        gc_ = 32 + 2 * fc + tt
                        s.op('dve', (lambda pu=pu, tmp=tmp, gc_=gc_: nc.vector.tensor_tensor(
                            out=gT[:, gc_, :], in0=tmp[:], in1=self.cx.ps[pu][:], op=ALU.mult)),
                             reads=[('ps', pu), ('tmpf', ti)], writes=[('gT', gc_)])
            for dsl in range(D // 512):
                si = self.load_slab(wd[q * FQ:(q + 1) * FQ, dsl * 512:(dsl + 1) * 512], kcq, 512)
                slab = self.slabs[si]
                for tg in range(8):
                    tt, g = tg // 4, tg % 4
                    pi = self.next_ps(0, 6)
                    for k in range(kcq):
                        gc_ = 32 + 2 * k + tt
                        s.op('pe', (lambda g=g, k=k, gc_=gc_, slab=slab, pi=pi: nc.tensor.matmul(
                            self.cx.ps[pi][:], gT[:, gc_, g * 128:(g + 1) * 128], slab[:, k, :],
                            start=(k == 0), stop=(k == kcq - 1))),
                             reads=[('slab', si, k), ('gT', gc_)], writes=[('ps', pi)])
                    src_ap, sname = (x_ap, xk) if q == 0 else (out_ap, ok)
                    self.residual_out(src_ap[tg * 128:(tg + 1) * 128, dsl * 512:(dsl + 1) * 512],
                                      out_ap[tg * 128:(tg + 1) * 128, dsl * 512:(dsl + 1) * 512], pi, 0.5,
                                      sk=('dr', sname, tg, dsl), dk=('dr', ok, tg, dsl))

    def residual_out(self, xsrc_ap, dst_ap, pi, scale, sk=None, dk=None):
        nc, s = self.nc, self.s
        ri = self.xres_i % 2
        self.xres_i += 1
        oi = self.xo_i % 2
        self.xo_i += 1
        xr, xo = self.xres[ri], self.xo[oi]
        s.op('sp', lambda: nc.sync.dma_start(out=xr[:], in_=xsrc_ap), writes=[('xres', ri)],
             reads=([sk] if sk else []), dma=True)
        s.op('dve', lambda: nc.vector.scalar_tensor_tensor(out=xo[:], in0=self.cx.ps[pi][:], scalar=scale, in1=xr[:],
                                                           op0=ALU.mult, op1=ALU.add),
             reads=[('ps', pi), ('xres', ri)], writes=[('xo', oi)])
        s.op('sp', lambda: nc.sync.dma_start(out=dst_ap, in_=xo[:]), reads=[('xo', oi)],
             writes=([dk] if dk else []), dma=True)


def build_ffn_prog(ntok):
    nc = bass.Bass("TRN2", target_bir_lowering=False)
    x = dram_in(nc, "x", [ntok, D])
    ln = dram_in(nc, "ln", [128, 32])
    wg = dram_in(nc, "wg", [D, DFF])
    wu = dram_in(nc, "wu", [D, DFF])
    wd = dram_in(nc, "wd", [DFF, D])
    ident = dram_in(nc, "ident", [128, 128])
    y = dram_out(nc, "y", [ntok, D])
    cx = Ctx(nc)
    b = Blocks(cx, ident)
    gT = cx.sb([128, 64, 512], BF16, "gT")
    if ntok == 1024:
        b.ffn2(x, y, ln, wg, wu, wd, gT, 'x', 'y')
    else:
        for t in range(ntok // 512):
            b.ffn(x[t * 512:(t + 1) * 512, :], y[t * 512:(t + 1) * 512, :], ln, wg, wu, wd, gT)
    cx.s.finish()
    return nc


def build_l3_prog(ntok, last):
    nc = bass.Bass("TRN2", target_bir_lowering=False)
    x = dram_in(nc, "x", [ntok, D])
    yT = dram_in(nc, "yT", [3 * 2048, ntok])
    pT = dram_in(nc, "pT", [PLE, ntok])
    ident = dram_in(nc, "ident", [128, 128])
    ln_mix = dram_in(nc, "ln_mix", [128, 32])
    wgate = dram_in(nc, "wgate", [D, 3 * D])
    wbr = dram_in(nc, "wbr", [3, 2048, D])
    wo = dram_in(nc, "wo", [D, D])
    ln1 = dram_in(nc, "ln1", [128, 32])
    wg1 = dram_in(nc, "wg1", [D, DFF]); wu1 = dram_in(nc, "wu1", [D, DFF]); wd1 = dram_in(nc, "wd1", [DFF, D])
    ln_ple = dram_in(nc, "ln_ple", [128, 32])
    wpg = dram_in(nc, "wpg", [D, D]); wpp = dram_in(nc, "wpp", [PLE, D])
    if last:
        fn = dram_in(nc, "fn", [D])
    else:
        ln2 = dram_in(nc, "ln2", [128, 32])
        wg2 = dram_in(nc, "wg2", [D, DFF]); wu2 = dram_in(nc, "wu2", [D, DFF]); wd2 = dram_in(nc, "wd2", [DFF, D])
    y = dram_out(nc, "y", [ntok, D])
    x2 = dram_tmp(nc, "x2", [ntok, D]); x3 = dram_tmp(nc, "x3", [ntok, D]); x4 = dram_tmp(nc, "x4", [ntok, D])
    cx = Ctx(nc)
    b = Blocks(cx, ident)
    gT = cx.sb([128, 64, 512], BF16, "gT")
    if ntok == 1024:
        for t in range(2):
            sl = slice(t * 512, (t + 1) * 512)
            b.mix_merge(x[sl, :], yT[:, sl], ln_mix, wgate, wbr, gT, xk=('x', t))
            b.wo_proj(x[sl, :], x2[sl, :], wo, gT, xk=('x', t), ok=('x2', t))
        b.ffn2(x2, x3, ln1, wg1, wu1, wd1, gT, 'x2', 'x3')
        for t in range(2):
            sl = slice(t * 512, (t + 1) * 512)
            b.ple(x3[sl, :], x4[sl, :], ln_ple, pT[:, sl], wpg, wpp, gT, xk=('x3', t), ok=('x4', t))
        if not last:
            b.ffn2(x4, y, ln2, wg2, wu2, wd2, gT, 'x4', 'y')
    else:
      for t in range(ntok // 512):
        sl = slice(t * 512, (t + 1) * 512)
        b.mix_merge(x[sl, :], yT[:, sl], ln_mix, wgate, wbr, gT, xk=('x', t))
        b.wo_proj(x[sl, :], x2[sl, :], wo, gT, xk=('x', t), ok=('x2', t))
        b.ffn(x2[sl, :], x3[sl, :], ln1, wg1, wu1, wd1, gT, xk=('x2', t), ok=('x3', t))
        b.ple(x3[sl, :], x4[sl, :], ln_ple, pT[:, sl], wpg, wpp, gT, xk=('x3', t), ok=('x4', t))
        if not last:
            b.ffn(x4[sl, :], y[sl, :], ln2, wg2, wu2, wd2, gT, xk=('x4', t), ok=('y', t))
    if last:
        b.final_norm(x4, y, fn, gT, ntok, xname='x4')
    cx.s.finish()
    return nc


def _pT(v):
    return np.ascontiguousarray(np.asarray(v, np.float32).reshape(-1, 128).T)


def _launch(nc, in_maps):
    res = run_bass_kernel_spmd(nc, in_maps, core_ids=list(range(NCORES)))
    return res.results


def _c(a):
    return np.ascontiguousarray(np.asarray(a, np.float32))


def kernel(**inputs):
    TS = NB * SEQ // NCORES
    x = _c(inputs["x"]).reshape(NB * SEQ, D)
    ident = np.eye(128, dtype=np.float32)
    pos = np.asarray(inputs["positions"]).astype(np.int32)
    nc1 = build_ffn_prog(TS)
    w = {"ln": _pT(inputs["ln_ffn"][0, 0]), "wg": _c(inputs["ffn_w_gate"][0, 0]), "wu": _c(inputs["ffn_w_up"][0, 0]),
         "wd": _c(inputs["ffn_w_down"][0, 0]), "ident": ident}
    res = _launch(nc1, [dict(w, x=x[c * TS:(c + 1) * TS]) for c in range(NCORES)])
    x1 = np.concatenate([np.asarray(r["y"], np.float32) for r in res], axis=0)
    del w
    for i in range(DEPTH):
        last = (i == DEPTH - 1)
        w_in = np.asarray(inputs["w_in"][i], np.float32)
        nc2 = build_l2_prog(SEQ, i)
        ln_mix = _pT(inputs["ln_mix"][i])
        groups = []
        for g in range(4):
            d = l2_consts(inputs, i, g)
            d["wsl"] = l2_weight_slabs(w_in, g)
            d["ln"] = ln_mix
            groups.append(d)
        in_maps = []
        for c in range(NCORES):
            b, g = c // 4, c % 4
            in_maps.append(dict(groups[g], x=x1[b * SEQ:(b + 1) * SEQ], pos=np.ascontiguousarray(pos[b])))
        res = _launch(nc2, in_maps)
        del groups, in_maps
        yfull = np.zeros((NB, SEQ, 3 * 2048), np.float32)
        for c in range(NCORES):
            b, g = c // 4, c % 4
            yc = np.asarray(res[c]["y"], np.float32)
            yfull[b, :, g * 512:(g + 1) * 512] = yc[:, 0:512]
            yfull[b, :, 2048 + g * 512:2048 + (g + 1) * 512] = yc[:, 512:1024]
            yfull[b, :, 4096 + g * 512:4096 + (g + 1) * 512] = yc[:, 1024:1536]
        nc3 = build_l3_prog(TS, last)
        w = {"ident": ident, "ln_mix": ln_mix, "wgate": np.ascontiguousarray(w_in[:, O_GATE:]),
             "wbr": _c(inputs["w_branch"][i]), "wo": _c(inputs["w_out"][i]),
             "ln1": _pT(inputs["ln_ffn"][i, 1]), "wg1": _c(inputs["ffn_w_gate"][i, 1]), "wu1": _c(inputs["ffn_w_up"][i, 1]),
             "wd1": _c(inputs["ffn_w_down"][i, 1]), "ln_ple": _pT(inputs["ln_ple"][i]),
             "wpg": _c(inputs["w_ple_gate"][i]), "wpp": _c(inputs["w_ple_proj"][i])}
        if last:
            w["fn"] = _c(inputs["final_norm"])
        else:
            w.update({"ln2": _pT(inputs["ln_ffn"][i + 1, 0]), "wg2": _c(inputs["ffn_w_gate"][i + 1, 0]),
                      "wu2": _c(inputs["ffn_w_up"][i + 1, 0]), "wd2": _c(inputs["ffn_w_down"][i + 1, 0])})
        del w_in
        p_i = np.asarray(inputs["p"][i], np.float32)
        in_maps = []
        for c in range(NCORES):
            b, t0 = c // 4, (c % 4) * TS
            in_maps.append(dict(w, x=x1[c * TS:(c + 1) * TS],
                                yT=np.ascontiguousarray(yfull[b, t0:t0 + TS, :].T),
                                pT=np.ascontiguousarray(p_i[b, t0:t0 + TS, :].T)))
        res = _launch(nc3, in_maps)
        x1 = np.concatenate([np.asarray(r["y"], np.float32) for r in res], axis=0)
        del w, in_maps, yfull
    return x1.reshape(NB, SEQ, D)


NFM = 36
NSLAB = 12
NTM = 2064
YW = 1536
TWO_PI = 2.0 * math.pi


def l2_slab_plan():
    plan = []
    for j in range(2):
        plan.append([('fm', c * 128, j * 8 + 2 * c) for c in range(4)])
        plan.append([('tm', 0, 256, j * 256)])
    plan.append([('fm', c * 128, 16 + 2 * c) for c in range(4)])
    for j in range(2):
        plan.append([('tm', 0, 512, 512 + j * 512)])
    for j in range(4):
        plan.append([('fm', c * 128, 24 + j * 3 + c) for c in range(3)] + [('tm', 384, 128, 1536 + j * 128)])
    plan.append([('tm', 0, 16, 2048)])
    return plan


class Mix:
    def __init__(self, cx, S):
        self.cx, self.nc, self.s, self.S = cx, cx.nc, cx.s, S
        self.bi = {'d': 0, 'a': 0}
        self.rc = 0

    def bank(self, kind='d'):
        ch = getattr(self, 'chain', None)
        if ch is not None:
            if kind == 'a':
                i = 6 + ch
            elif self.stage == 'pre':
                i = ch * 3 + self.lane
            else:
                i = ch * 3 + 2
        elif kind == 'd':
            i = self.bi['d'] % 6
        else:
            i = 6 + self.bi['a'] % 2
        self.bi[kind] += 1
        return i

    def mm(self, out, lhsT, rhs, r, w, start=True, stop=True):
        nc = self.nc
        self.s.op('pe', lambda: nc.tensor.matmul(out, lhsT, rhs, start=start, stop=stop), r, w)

    def tr(self, out, in_, ident, r, w):
        nc = self.nc
        self.s.op('pe', lambda: nc.tensor.transpose(out=out, in_=in_, identity=ident), r, w)

    def tt(self, out, in0, in1, op, r, w, eng='dve'):
        e = self.nc.vector if eng == 'dve' else self.nc.gpsimd
        self.s.op(eng, lambda: e.tensor_tensor(out=out, in0=in0, in1=in1, op=op), r, w)

    def ts(self, out, in0, s1, op0, r, w, s2=None, op1=None):
        nc = self.nc
        if op1 is None:
            self.s.op('dve', lambda: nc.vector.tensor_scalar(out=out, in0=in0, scalar1=s1, scalar2=None, op0=op0), r, w)
        else:
            self.s.op('dve', lambda: nc.vector.tensor_scalar(out=out, in0=in0, scalar1=s1, scalar2=s2, op0=op0, op1=op1), r, w)

    def stt(self, out, in0, scalar, in1, op0, op1, r, w):
        nc = self.nc
        self.s.op('dve', lambda: nc.vector.scalar_tensor_tensor(out=out, in0=in0, scalar=scalar, in1=in1, op0=op0, op1=op1), r, w)

    def act(self, out, in_, func, r, w, scale=None, bias=None, accum=None):
        nc = self.nc
        kw = {}
        if scale is not None:
            kw['scale'] = scale
        if bias is not None:
            kw['bias'] = bias
        if accum is not None:
            kw['accum_out'] = accum
        self.s.op('act', lambda: nc.scalar.activation(out=out, in_=in_, func=func, **kw), r, w)

    def vcopy(self, out, in_, r, w):
        nc = self.nc
        self.s.op('dve', lambda: nc.vector.tensor_copy(out=out, in_=in_), r, w)

    def memset(self, ap, val, w, r=()):
        nc = self.nc
        self.s.op('dve', lambda: nc.vector.memset(ap, val), r, w)

    def recip(self, out, in_, r, w):
        nc = self.nc
        self.s.op('dve', lambda: nc.vector.reciprocal(out=out, in_=in_), r, w)

    def dma(self, out, in_, r, w, q='sp'):
        nc = self.nc
        e = {'sp': nc.sync, 'pool': nc.gpsimd, 'act': nc.scalar}[q]
        self.s.op(q, lambda: e.dma_start(out=out, in_=in_), r, w, dma=True)

    def phase0(self, b, x, ln_ap, hTd):
        cx, S = self.cx, self.S
        b.load_ln(ln_ap)
        for t in range(S // 512):
            for g in range(4):
                b.norm_group(x[t * 512 + g * 128: t * 512 + (g + 1) * 128, :], g)
            self.dma(hTd[t], b.hT[:].rearrange("p k n -> p (k n)"), b.hT_keys(), [('hTd', t)])

    def rope_tables(self, pos_ap, tab_ap, cosd, sind):
        cx, S = self.cx, self.S
        cx.push()
        tab = cx.sb([128, 4], F32, "tab")
        self.dma(tab[:], tab_ap, [], ['tab'])
        posi = cx.sb([128, S], I32, "posi")
        u = cx.sb([128, S], F32, "u")
        kf = cx.sb([128, S], F32, "kf")
        ki = cx.sb([128, S], I32, "ki")
        m = cx.sb([128, S], F32, "m")
        self.dma(posi[:], pos_ap.partition_broadcast(128), [], ['posi'])
        self.vcopy(u[:], posi[:], ['posi'], ['u'])
        self.ts(u[:], u[:], tab[:, 0:1], ALU.mult, ['u', 'tab'], ['u'])
        for which, shift, dst in (('sin', 0.0, sind), ('cos', 0.25, cosd)):
            self.ts(kf[:], u[:], 1.0 / TWO_PI, ALU.mult, ['u'], ['kf'], s2=shift, op1=ALU.add)
            self.vcopy(ki[:], kf[:], ['kf'], ['ki'])
            self.vcopy(m[:], ki[:], ['ki'], ['m'])
            self.tt(kf[:], kf[:], m[:], ALU.subtract, ['kf', 'm'], ['kf'])
            self.ts(m[:], kf[:], 0.5, ALU.is_gt, ['kf'], ['m'])
            self.tt(kf[:], kf[:], m[:], ALU.subtract, ['kf', 'm'], ['kf'])
            self.ts(m[:], kf[:], -0.5, ALU.is_lt, ['kf'], ['m'])
            self.tt(kf[:], kf[:], m[:], ALU.add, ['kf', 'm'], ['kf'])
            self.act(m[:], kf[:], AF.Sin, ['kf'], ['m'], scale=6.283185)
            if which == 'sin':
                self.ts(m[:], m[:], tab[:, 1:2], ALU.mult, ['m', 'tab'], ['m'])
            self.dma(dst, m[:], ['m'], [which + 'd'])
        cx.pop()

    def phase1(self, b, wsl, hTd, Pfm, Ptm):
        cx, S, nc = self.cx, self.S, self.nc
        plan = l2_slab_plan()
        hbufs = [(b.hT, 'hT'), (cx.sb([128, 32, 512], BF16, "hT2"), 'hT2')]
        ev = 0
        it = 0
        for si_, spec in enumerate(plan):
            ncols = max((e[1] + (128 if e[0] == 'fm' else e[2])) for e in spec)
            si = b.load_slab(wsl[si_, :, 0:ncols], 32, ncols)
            slab = b.slabs[si]
            for t in range(S // 512):
                hT, hk = hbufs[it % 2]
                it += 1
                self.dma(hT[:].rearrange("p k n -> p (k n)"), hTd[t], [('hTd', t)],
                         [(hk, c, g) for c in range(32) for g in range(4)], q='pool')
                for e in spec:
                    if e[0] == 'fm':
                        _, c0, ch = e
                        pi = self.bank('d' if ev % 2 == 0 else 'a')
                        for k in range(32):
                            self.mm(cx.ps[pi][:], slab[:, k, c0:c0 + 128], hT[:, k, :],
                                    [('slab', si, k)] + [(hk, k, g) for g in range(4)], [('ps', pi)],
                                    start=(k == 0), stop=(k == 31))
                        ti = b.next_tmp()
                        tmp = b.tmpf[ti]
                        if ev % 2 == 0:
                            self.vcopy(tmp[:], cx.ps[pi][:], [('ps', pi)], [('tmpf', ti)])
                        else:
                            self.act(tmp[:], cx.ps[pi][:], AF.Copy, [('ps', pi)], [('tmpf', ti)])
                        ev += 1
                        self.dma(Pfm[ch, :, t * 512:(t + 1) * 512], tmp[:], [('tmpf', ti)], [('Pfm', ch, t)])
                    else:
                        _, c0, n, d0 = e
                        for g in range(4):
                            pi = self.bank('d' if ev % 2 == 0 else 'a')
                            for k in range(32):
                                self.mm(cx.ps[pi][:, 0:n], hT[:, k, g * 128:(g + 1) * 128], slab[:, k, c0:c0 + n],
                                        [('slab', si, k), (hk, k, g)], [('ps', pi)],
                                        start=(k == 0), stop=(k == 31))
                            ti = b.next_tmp()
                            tmp = b.tmpf[ti]
                            if ev % 2 == 0:
                                self.vcopy(tmp[:, 0:n], cx.ps[pi][:, 0:n], [('ps', pi)], [('tmpf', ti)])
                            else:
                                self.act(tmp[:, 0:n], cx.ps[pi][:, 0:n], AF.Copy, [('ps', pi)], [('tmpf', ti)])
                            ev += 1
                            r0 = t * 512 + g * 128
                            self.dma(Ptm[r0:r0 + 128, d0:d0 + n], tmp[:, 0:n], [('tmpf', ti)], [('Ptm', d0, r0 // 128)])

    def rope_load(self, dst, chx, Pfm, cos, sin, R1, R2, tag, PM):
        cx, S = self.cx, self.S
        self.dma(R1[:], Pfm[chx], [], ['R1'])
        for t in range(S // 512):
            sl = slice(t * 512, (t + 1) * 512)
            ps = 4 + (self.rc % 4)
            self.rc += 1
            self.mm(cx.ps[ps][:], PM[:], R1[:, sl], ['PM', 'R1'], [('ps', ps)])
            self.tt(R2[:, sl], cx.ps[ps][:], sin[:, sl], ALU.mult, [('ps', ps), 'sin'], ['R2'])
        self.tt(R1[:], R1[:], cos[:], ALU.mult, ['R1', 'cos'], ['R1'])
        self.tt(dst, R1[:], R2[:], ALU.add, ['R1', 'R2'], [tag])

    def rms_rows(self, stat, src, n, r, extra_scale=1.0):
        cx = self.cx
        junk = self.junk
        self.act(junk[:, 0:n], src, AF.Square, r, ['junk', 'st0'], accum=stat[:, 0:1])
        self.ts(stat[:, 1:2], stat[:, 0:1], extra_scale * extra_scale / n, ALU.mult, ['st0'], ['st1'], s2=EPS, op1=ALU.add)
        self.act(stat[:, 2:3], stat[:, 1:2], AF.Sqrt, ['st1'], ['st2'])
        self.recip(stat[:, 3:4], stat[:, 2:3], ['st2'], ['st3'])
        if extra_scale != 1.0:
            self.ts(stat[:, 3:4], stat[:, 3:4], extra_scale, ALU.mult, ['st3'], ['st3'])

    def diff_phase(self, Pfm, Ptm, cosd, sind, dl_ap, sub_ap, yout, lam_init, pm_ap):
        cx, S, nc = self.cx, self.S, self.nc
        NQ, NKB = S // 512, S // 128
        cx.push()
        cos = cx.sb([128, S], F32, "cos"); sin = cx.sb([128, S], F32, "sin")
        R1 = cx.sb([128, S], F32, "R1"); R2 = cx.sb([128, S], F32, "R2")
        qk = [cx.sb([128, S], BF16, f"qk{i}") for i in range(4)]
        V = cx.sb([128, NKB, 258], BF16, "Vext")
        E = [cx.sb([128, 512], BF16, f"E{i}") for i in range(3)]
        dl = cx.sb([128, 512], F32, "dl"); sub = cx.sb([128, 256], F32, "sub")
        PM = cx.sb([128, 128], F32, "PM")
        self.dma(PM[:], pm_ap, [], ['PM'])
        st = cx.sb([128, 16], F32, "dst"); self.junk = cx.sb([128, 256], F32, "junk")
        o0 = [cx.sb([128, 256], F32, f"o0_{i}") for i in range(4)]
        o1 = [cx.sb([128, 256], F32, f"o1_{i}") for i in range(2)]
        self.dma(cos[:], cosd, [], ['cos']); self.dma(sin[:], sind, [], ['sin'])
        self.dma(dl[:], dl_ap.partition_broadcast(128), [], ['dl'])
        self.dma(sub[:], sub_ap.partition_broadcast(128), [], ['sub'])
        self.tt(self.junk[:, 0:128], dl[:, 0:128], dl[:, 128:256], ALU.mult, ['dl'], ['junk'])
        self.s.op('dve', lambda: nc.vector.reduce_sum(out=st[:, 4:5], in_=self.junk[:, 0:128], axis=mybir.AxisListType.X), ['junk'], ['l4'])
        self.tt(self.junk[:, 128:256], dl[:, 256:384], dl[:, 384:512], ALU.mult, ['dl'], ['junk'])
        self.s.op('dve', lambda: nc.vector.reduce_sum(out=st[:, 5:6], in_=self.junk[:, 128:256], axis=mybir.AxisListType.X), ['junk'], ['l5'])
        self.act(st[:, 6:7], st[:, 4:5], AF.Exp, ['l4'], ['l6'])
        self.act(st[:, 7:8], st[:, 5:6], AF.Exp, ['l5'], ['l7'])
        self.tt(st[:, 8:9], st[:, 7:8], st[:, 6:7], ALU.subtract, ['l6', 'l7'], ['l8'])
        self.ts(st[:, 8:9], st[:, 8:9], -lam_init, ALU.add, ['l8'], ['l8'])
        scale = 128.0 ** -0.5
        ei = 0
        for j in range(2):
            base = j * 8
            self.rope_load(qk[0][:], base + 0, Pfm, cos, sin, R1, R2, 'qk0', PM)
            self.rope_load(qk[1][:], base + 2, Pfm, cos, sin, R1, R2, 'qk1', PM)
            self.rope_load(qk[2][:], base + 4, Pfm, cos, sin, R1, R2, 'qk2', PM)
            self.rope_load(qk[3][:], base + 6, Pfm, cos, sin, R1, R2, 'qk3', PM)
            self.dma(V[:, :, 0:256], Ptm[:, j * 256:(j + 1) * 256].rearrange("(kb p) c -> p kb c", p=128), [], ['V'], q='pool')
            self.memset(V[:, :, 256:257], 1.0, ['V1'])
            for qb in range(NQ):
                for t in range(2):
                    accs = [0, 1, 2, 3]
                    qT, kT = qk[t], qk[2 + t]
                    pend = []
                    def pv(kb_, e_):
                        for qs in range(4):
                            self.mm(cx.ps[accs[qs]][:, 0:257], E[e_][:, qs * 128:(qs + 1) * 128], V[:, kb_, 0:257],
                                    [('E', e_), 'V', 'V1'], [('ps', accs[qs])], start=(kb_ == 0), stop=(kb_ == NKB - 1))
                    for kb in range(NKB):
                        ps = 4 + (ei % 4)
                        self.mm(cx.ps[ps][:], kT[:, kb * 128:(kb + 1) * 128], qT[:, qb * 512:(qb + 1) * 512],
                                [f'qk{t}', f'qk{2 + t}'], [('ps', ps)])
                        e = ei % 3
                        ei += 1
                        self.act(E[e][:], cx.ps[ps][:], AF.Exp, [('ps', ps)], [('E', e)], scale=scale)
                        pend.append((kb, e))
                        if len(pend) > 2:
                            pv(*pend.pop(0))
                    for it_ in pend:
                        pv(*it_)
                    for qs in range(4):
                        acc = cx.ps[accs[qs]]
                        self.recip(st[:, 9:10], acc[:, 256:257], [('ps', accs[qs])], ['r0'])
                        if t == 0:
                            self.ts(o0[qs][:], acc[:, 0:256], st[:, 9:10], ALU.mult, [('ps', accs[qs]), 'r0'], [('o0', qs)])
                        else:
                            oo = o1[qs % 2]
                            self.ts(oo[:], acc[:, 0:256], st[:, 9:10], ALU.mult, [('ps', accs[qs]), 'r0'], [('o1', qs % 2)])
                            self.stt(oo[:], oo[:], st[:, 8:9], o0[qs][:], ALU.mult, ALU.add,
                                     [('o1', qs % 2), ('o0', qs), 'l8'], [('o1', qs % 2)])
                            self.rms_rows(st, oo[:], 256, [('o1', qs % 2)])
                            self.ts(oo[:], oo[:], st[:, 3:4], ALU.mult, [('o1', qs % 2), 'st3'], [('o1', qs % 2)],
                                    s2=(1.0 - lam_init), op1=ALU.mult)
                            self.tt(oo[:], oo[:], sub[:], ALU.mult, [('o1', qs % 2), 'sub'], [('o1', qs % 2)])
                            r0 = qb * 512 + qs * 128
                            self.dma(yout[r0:r0 + 128, 1024 + j * 256:1024 + (j + 1) * 256], oo[:], [('o1', qs % 2)], [])
        cx.pop()

    def ret_phase(self, Pfm, Ptm, cosd, sind, lng_ap, yout, pm_ap):
        cx, S, nc = self.cx, self.S, self.nc
        NQ, NKB = S // 512, S // 128
        GW = 2 * S - 128
        OFF = S - 128
        cx.push()
        cos = cx.sb([128, S], F32, "cos"); sin = cx.sb([128, S], F32, "sin")
        R1 = cx.sb([128, S], F32, "R1"); R2 = cx.sb([128, S], F32, "R2")
        qT = cx.sb([128, S], BF16, "rq"); kT = cx.sb([128, S], BF16, "rk")
        V = cx.sb([128, NKB, 256], BF16, "rV")
        G = cx.sb([128, GW], F32, "G")
        E = [cx.sb([128, 512], BF16, f"E{i}") for i in range(3)]
        lng = cx.sb([128, 2], F32, "lng")
        PM = cx.sb([128, 128], F32, "PM")
        self.dma(PM[:], pm_ap, [], ['PM'])
        st = cx.sb([128, 16], F32, "rst"); self.junk = cx.sb([128, 256], F32, "junk")
        oo = [cx.sb([128, 256], F32, f"ro{i}") for i in range(2)]
        gg = [cx.sb([128, 256], F32, f"rg{i}") for i in range(2)]
        self.dma(cos[:], cosd, [], ['cos']); self.dma(sin[:], sind, [], ['sin'])
        self.dma(lng[:], lng_ap.partition_broadcast(128), [], ['lng'])
        ei = 0
        oi = 0
        for j in range(2):
            base = 16 + j * 4
            self.rope_load(qT[:], base + 0, Pfm, cos, sin, R1, R2, 'rq', PM)
            self.rope_load(kT[:], base + 2, Pfm, cos, sin, R1, R2, 'rk', PM)
            c0 = 512 + j * 512
            self.dma(V[:], Ptm[:, c0:c0 + 256].rearrange("(kb p) c -> p kb c", p=128), [], ['V'], q='pool')
            self.s.op('pool', lambda: nc.gpsimd.iota(G[:], pattern=[[1, GW]], base=-OFF, channel_multiplier=-1,
                                                     allow_small_or_imprecise_dtypes=True), [], ['G'])
            self.stt(G[:], G[:], -1.0, G[:], ALU.mult, ALU.max, ['G'], ['G'])
            self.act(G[:], G[:], AF.Exp, ['G', 'lng'], ['G'], scale=lng[:, j:j + 1])
            for qb in range(NQ):
                accs = [0, 1, 2, 3]
                pend = []
                def pv(kb_, e_):
                    for qs in range(4):
                        self.mm(cx.ps[accs[qs]][:, 0:256], E[e_][:, qs * 128:(qs + 1) * 128], V[:, kb_, :],
                                [('E', e_), 'V'], [('ps', accs[qs])], start=(kb_ == 0), stop=(kb_ == NKB - 1))
                for kb in range(NKB):
                    ps = 4 + (ei % 4)
                    self.mm(cx.ps[ps][:], kT[:, kb * 128:(kb + 1) * 128], qT[:, qb * 512:(qb + 1) * 512],
                            ['rq', 'rk'], [('ps', ps)])
                    e = ei % 3
                    ei += 1
                    g0 = qb * 512 - kb * 128 + OFF
                    self.tt(E[e][:], cx.ps[ps][:], G[:, g0:g0 + 512], ALU.mult, [('ps', ps), 'G'], [('E', e)])
                    pend.append((kb, e))
                    if len(pend) > 2:
                        pv(*pend.pop(0))
                for it_ in pend:
                    pv(*it_)
                for qs in range(4):
                    acc = cx.ps[accs[qs]]
                    o = oo[oi % 2]; gt = gg[oi % 2]; ok_ = ('ro', oi % 2); gk = ('rg', oi % 2)
                    oi += 1
                    self.act(o[:], acc[:, 0:256], AF.Copy, [('ps', accs[qs])], [ok_])
                    self.rms_rows(st, o[:], 256, [ok_], extra_scale=128.0 ** -0.5)
                    r0 = qb * 512 + qs * 128
                    self.dma(gt[:], Ptm[r0:r0 + 128, c0 + 256:c0 + 512], [], [gk])
                    self.act(gt[:], gt[:], AF.Silu, [gk], [gk])
                    self.stt(o[:], o[:], st[:, 3:4], gt[:], ALU.mult, ALU.mult, [ok_, gk, 'st3'], [ok_])
                    self.dma(yout[r0:r0 + 128, j * 256:(j + 1) * 256], o[:], [ok_], [])
        cx.pop()

    def gdn_phase(self, Pfm, Ptm, cw_ap, alog_ap, dt_ap, gnw_ap, masks_ap, ident_ap, yout):
        cx, S, nc = self.cx, self.S, self.nc
        NDC = S // 128
        NC4 = NDC * 4
        cx.push()
        MK = cx.sb([128, 9, 128], F32, "MK")
        IDF = cx.sb([128, 128], F32, "IDF"); ONES = cx.sb([128, 128], F32, "ONES")
        cw = cx.sb([128, 60], F32, "cw"); alog = cx.sb([128, 8], F32, "alog"); dtb = cx.sb([128, 8], F32, "dtb")
        nega = cx.sb([128, 8], F32, "nega"); gnw = cx.sb([128, 128], F32, "gnw")
        AB = cx.sb([128, NDC, 16], F32, "AB"); Gm = cx.sb([128, NDC, 8], F32, "Gm"); Bm = cx.sb([128, NDC, 8], F32, "Bm")
        tA = cx.sb([128, NDC], F32, "tA")
        def gt(name):
            return [cx.sb([128, NDC, 4], F32, f"{name}{d}") for d in range(2)]
        GC, GLb, GL0, GL1, EG, EK0, EK1, EL0, EL1, BE = (gt(n) for n in
                                                       ("GC", "GLb", "GL0", "GL1", "EG", "EK0", "EK1", "EL0", "EL1", "BE"))
        R = cx.sb([128, S + 4], F32, "R"); X = cx.sb([128, S], F32, "X")
        Q = cx.sb([128, S], F32, "Q"); Kf = cx.sb([128, S], F32, "Kf")
        Ktm = cx.sb([128, NDC, 128], F32, "Ktm"); Vtm = cx.sb([128, NDC, 128], F32, "Vtm")
        Od = [cx.sb([128, NDC, 128], F32, f"O{d}") for d in range(2)]
        st = cx.sb([128, 16], F32, "gst"); self.junk = cx.sb([128, 256], F32, "junk")
        sq = [cx.sb([128, 512], F32, f"sq{i}") for i in range(2)]
        names = ("A", "B", "P0", "P1", "Q0", "Q1", "TT", "Tm", "DG", "DEC", "DQ", "QKT", "VB", "KBG", "KO0", "KO1",
                 "U", "WT", "VN", "OA", "Sst", "zt", "ot")
        Td = [[{n: cx.sb([128, 128], F32, f"g{n}{d}{l}") for n in names if n != "Sst"} for l in range(2)] for d in range(2)]
        Sd = [cx.sb([128, 128], F32, f"gSst{d}") for d in range(2)]
        local = set(names) - {"Sst"}

        self.dma(MK[:], masks_ap.rearrange("m p f -> p m f"), [], ['MK'])
        self.dma(IDF[:], ident_ap, [], ['IDF'])
        self.memset(ONES[:], 1.0, ['ONES'])
        self.dma(cw[:], cw_ap, [], ['cw'])
        self.dma(alog[:], alog_ap.partition_broadcast(128), [], ['alog'])
        self.dma(dtb[:], dt_ap.partition_broadcast(128), [], ['dtb'])
        self.dma(gnw[:], gnw_ap.partition_broadcast(128), [], ['gnw'])
        self.dma(AB[:], Ptm[:, 2048:2064].rearrange("(dc p) c -> p dc c", p=128), [], ['AB'])
        self.act(nega[:], alog[:], AF.Exp, ['alog'], ['nega'])
        self.ts(nega[:], nega[:], -1.0, ALU.mult, ['nega'], ['nega'])
        for c in range(8):
            self.act(tA[:], AB[:, :, c], AF.Exp, ['AB', 'dtb'], ['tA'], bias=dtb[:, c:c + 1])
            self.ts(tA[:], tA[:], 1.0, ALU.add, ['tA'], ['tA'])
            self.act(tA[:], tA[:], AF.Ln, ['tA'], ['tA'])
            self.ts(Gm[:, :, c], tA[:], nega[:, c:c + 1], ALU.mult, ['tA', 'nega'], ['Gm'])
            self.act(Bm[:, :, c], AB[:, :, 8 + c], AF.Sigmoid, ['AB'], ['Bm'])
        rm0, rm1 = MK[:, 3, 0:1], MK[:, 4, 0:1]
        for d in range(2):
            rhs = Gm[:, :, 4 * d:4 * d + 4]
            for (mi, dst, nm) in ((d, GC[d], 'GC'), (2, GLb[d], 'GLb'), (3, GL0[d], 'GL0'), (4, GL1[d], 'GL1')):
                pi = self.bank('d')
                self.mm(cx.ps[pi][:, 0:NC4], MK[:, mi, :], rhs, ['MK', 'Gm'], [('ps', pi)])
                self.vcopy(dst[:].rearrange("p a b -> p (a b)"), cx.ps[pi][:, 0:NC4], [('ps', pi)], [(nm, d)])
            fl = lambda t_: t_[:].rearrange("p a b -> p (a b)")
            self.act(fl(EG[d]), fl(GC[d]), AF.Exp, [('GC', d)], [('EG', d)])
            self.tt(fl(EK0[d]), fl(GLb[d]), fl(GC[d]), ALU.subtract, [('GLb', d), ('GC', d)], [('EK0', d)])
            self.act(fl(EK0[d]), fl(EK0[d]), AF.Exp, [('EK0', d)], [('EK0', d)])
            self.ts(fl(EK1[d]), fl(EK0[d]), rm1, ALU.mult, [('EK0', d), 'MK'], [('EK1', d)])
            self.ts(fl(EK0[d]), fl(EK0[d]), rm0, ALU.mult, [('EK0', d), ('EK1', d), 'MK'], [('EK0', d)])
            self.act(fl(EL0[d]), fl(GL0[d]), AF.Exp, [('GL0', d)], [('EL0', d)])
            self.act(fl(EL1[d]), fl(GL1[d]), AF.Exp, [('GL1', d)], [('EL1', d)])
            self.tt(BE[d][:], Bm[:, :, 4 * d:4 * d + 4], EG[d][:], ALU.mult, ['Bm', ('EG', d)], [('BE', d)])
        self.memset(R[:, 0:2], 0.0, ['Rpad'])
        self.memset(R[:, S + 2:S + 4], 0.0, ['Rpad'])

        def conv_silu(j, r, dst, dk_):
            self.dma(R[:, 2:S + 2], Pfm[24 + j * 3 + r], [dk_], ['R'])
            w0 = (j * 3 + r) * 5
            self.ts(dst[:], R[:, 0:S], cw[:, w0:w0 + 1], ALU.mult, ['R', 'Rpad', 'cw'], [dk_])
            for t in range(1, 5):
                self.stt(dst[:], R[:, t:S + t], cw[:, w0 + t:w0 + t + 1], dst[:], ALU.mult, ALU.add, ['R', 'Rpad', 'cw', dk_], [dk_])
            self.act(dst[:], dst[:], AF.Silu, [dk_], [dk_])

        def l2norm(dst, dk_, scale):
            for t in range(S // 512):
                sl = slice(t * 512, (t + 1) * 512)
                sqt = sq[t % 2]
                self.act(sqt[:], dst[:, sl], AF.Square, [dk_], [('sq', t % 2)])
                pi = self.bank('d')
                self.mm(cx.ps[pi][:], ONES[:], sqt[:], ['ONES', ('sq', t % 2)], [('ps', pi)])
                self.ts(sqt[:], cx.ps[pi][:], EPS, ALU.add, [('ps', pi)], [('sq', t % 2)])
                self.act(sqt[:], sqt[:], AF.Sqrt, [('sq', t % 2)], [('sq', t % 2)])
                self.recip(sqt[:], sqt[:], [('sq', t % 2)], [('sq', t % 2)])
                self.stt(dst[:, sl], dst[:, sl], scale, sqt[:], ALU.mult, ALU.mult, [dk_, ('sq', t % 2)], [dk_])

        def to_tm(src, sk_, dst, dk_):
            for d4 in range(NDC // 4):
                pi = self.bank('d')
                for i in range(4):
                    dc = d4 * 4 + i
                    self.tr(cx.ps[pi][:, i * 128:(i + 1) * 128], src[:, dc * 128:(dc + 1) * 128], IDF[:], [sk_, 'IDF'], [('ps', pi)])
                self.vcopy(dst[:, d4 * 4:(d4 + 1) * 4, :].rearrange("p a b -> p (a b)"), cx.ps[pi][:], [('ps', pi)], [dk_])

        def mmsb(dst, lhsT, rhs, r, w, acc=None, accop=None):
            pi = self.bank('d')
            self.mm(cx.ps[pi][:, 0:128], lhsT, rhs, r, [('ps', pi)])
            if acc is None:
                if getattr(self, 'chain', None) is not None:
                    self.act(dst, cx.ps[pi][:, 0:128], AF.Copy, [('ps', pi)], w)
                else:
                    self.vcopy(dst, cx.ps[pi][:, 0:128], [('ps', pi)], w)
            else:
                self.tt(dst, acc, cx.ps[pi][:, 0:128], accop, [('ps', pi)] + r[:0] + w, w)

        for j in range(4):
            conv_silu(j, 2, X, 'X')
            to_tm(X, 'X', Vtm, 'Vtm')
            conv_silu(j, 0, Q, 'Q')
            l2norm(Q, 'Q', 128.0 ** -0.5)
            conv_silu(j, 1, Kf, 'Kf')
            l2norm(Kf, 'Kf', 1.0)
            to_tm(Kf, 'Kf', Ktm, 'Ktm')
            chains = []
            for d in range(2):
                O = Od[d]
                saved_ops = self.s.ops
                self.chain = d
                col = slice(j, j + 1)
                self.s.ops = []
                self.s.keyfn = (lambda k, d=d: (k, d) if k == 'Sst' else k)
                self.memset(Sd[d][:], 0.0, ['Sst'])
                seq = self.s.ops
                order = list(range(NDC)) if d == 0 else list(range(NDC - 1, -1, -1))
                halves = (0, 1) if d == 0 else (1, 0)
                pres, scans = [], []
                for ui, dc in enumerate(order):
                    lane = ui % 2
                    T = dict(Td[d][lane]); T['Sst'] = Sd[d]
                    self.s.keyfn = (lambda k, d=d, lane=lane: (k, d, lane) if (isinstance(k, str) and k in local) else
                                    ((k, d) if k == 'Sst' else
                                     (('O', d, k[1]) if (isinstance(k, tuple) and k[0] == 'O') else k)))
                    self.s.ops = []
                    self.lane, self.stage = lane, 'pre'
                    tok = slice(dc * 128, (dc + 1) * 128)
                    gcp = GC[d][:, dc, col]; beta = Bm[:, dc, 4 * d + j:4 * d + j + 1]
                    A, B_, TT, Tm, DG, DEC, DQ, QKT = (T[n] for n in ("A", "B", "TT", "Tm", "DG", "DEC", "DQ", "QKT"))
                    self.ts(DG[:], IDF[:], gcp, ALU.mult, ['IDF', ('GC', d)], ['DG'])
                    p2 = self.bank('d')
                    self.mm(cx.ps[p2][:, 0:128], ONES[:], DG[:], ['ONES', 'DG'], [('ps', p2)])
                    self.ts(DEC[:], cx.ps[p2][:, 0:128], gcp, ALU.subtract, [('ps', p2), ('GC', d)], ['DEC'], s2=0.0, op1=ALU.max)
                    self.ts(DQ[:], cx.ps[p2][:, 0:128], gcp, ALU.subtract, [('ps', p2), ('GC', d)], ['DQ'], s2=0.0, op1=ALU.min)
                    self.act(DEC[:], DEC[:], AF.Exp, ['DEC'], ['DEC'], scale=-1.0)
                    self.act(DQ[:], DQ[:], AF.Exp, ['DQ'], ['DQ'])
                    self.stt(DEC[:], DEC[:], beta, MK[:, 5 + d, :], ALU.mult, ALU.mult, ['DEC', 'Bm', 'MK'], ['DEC'])
                    self.tt(DQ[:], DQ[:], MK[:, 7 + d, :], ALU.mult, ['DQ', 'MK'], ['DQ'])
                    p1 = self.bank('d')
                    self.mm(cx.ps[p1][:, 0:128], Kf[:, tok], Kf[:, tok], ['Kf'], [('ps', p1)])
                    self.tt(A[:], cx.ps[p1][:, 0:128], DEC[:], ALU.mult, [('ps', p1), 'DEC'], ['A'])
                    p3 = self.bank('d')
                    self.mm(cx.ps[p3][:, 0:128], Kf[:, tok], Q[:, tok], ['Kf', 'Q'], [('ps', p3)])
                    self.tt(QKT[:], cx.ps[p3][:, 0:128], DQ[:], ALU.mult, [('ps', p3), 'DQ'], ['QKT'])
                    p4 = self.bank('d')
                    self.tr(cx.ps[p4][:, 0:128], A[:], IDF[:], ['A', 'IDF'], [('ps', p4)])
                    self.act(B_[:], cx.ps[p4][:, 0:128], AF.Copy, [('ps', p4)], ['B'])
                    self.tt(TT[:], IDF[:], B_[:], ALU.subtract, ['IDF', 'B'], ['TT'], eng='pool')
                    self.tt(Tm[:], IDF[:], A[:], ALU.subtract, ['IDF', 'A'], ['Tm'], eng='pool')
                    P_, Q_, pk, qk_ = B_, A, 'B', 'A'
                    for lvl in range(5):
                        Pn, pnk = T[f"P{lvl % 2}"], f"P{lvl % 2}"
                        Qn, qnk = T[f"Q{lvl % 2}"], f"Q{lvl % 2}"
                        mmsb(Pn[:], Q_[:], P_[:], [pk, qk_], [pnk])
                        if lvl < 4:
                            mmsb(Qn[:], P_[:], Q_[:], [pk, qk_], [qnk])
                        if lvl < 4:
                            mmsb(DG[:], TT[:], Qn[:], ['TT', qnk], ['DG'])
                        pa = self.bank('d')
                        self.mm(cx.ps[pa][:, 0:128], Tm[:], Pn[:], ['Tm', pnk], [('ps', pa)])
                        self.tt(TT[:], TT[:], cx.ps[pa][:, 0:128], ALU.add, ['TT', ('ps', pa)], ['TT'])
                        if lvl < 4:
                            self.tt(Tm[:], Tm[:], DG[:], ALU.add, ['Tm', 'DG'], ['Tm'])
                        P_, Q_, pk, qk_ = Pn, Qn, pnk, qnk
                    VB, KBG, KO0, KO1, U, WT, VN, OA, Sst = (T[n] for n in ("VB", "KBG", "KO0", "KO1", "U", "WT", "VN", "OA", "Sst"))
                    self.ts(VB[:], Vtm[:, dc, :], beta, ALU.mult, ['Vtm', 'Bm'], ['VB'])
                    self.ts(KBG[:], Ktm[:, dc, :], BE[d][:, dc, col], ALU.mult, ['Ktm', ('BE', d)], ['KBG'])
                    self.ts(KO0[:], Ktm[:, dc, :], EK0[d][:, dc, col], ALU.mult, ['Ktm', ('EK0', d)], ['KO0'])
                    self.ts(KO1[:], Ktm[:, dc, :], EK1[d][:, dc, col], ALU.mult, ['Ktm', ('EK1', d)], ['KO1'])
                    mmsb(U[:], TT[:], VB[:], ['TT', 'VB'], ['U'])
                    mmsb(WT[:], KBG[:], TT[:], ['KBG', 'TT'], ['WT'])
                    pres.append(self.s.ops)
                    self.s.ops = []
                    self.stage = 'scan'
                    for hh in halves:
                        rows = slice(64 * hh, 64 * hh + 64)
                        KOh, kok = (KO0, 'KO0') if hh == 0 else (KO1, 'KO1')
                        ELh = (EL0 if hh == 0 else EL1)[d][:, dc, col]
                        elk = ('EL0' if hh == 0 else 'EL1', d)
                        pv = self.bank('d')
                        self.mm(cx.ps[pv][:, 0:128], WT[:], Sst[:], ['WT', 'Sst'], [('ps', pv)])
                        self.tt(VN[:], U[:], cx.ps[pv][:, 0:128], ALU.subtract, ['U', ('ps', pv)], ['VN'])
                        pa = self.bank('a')
                        self.mm(cx.ps[pa][:, 0:128], Q[:, tok], Sst[:], ['Q', 'Sst'], [('ps', pa)])
                        self.act(OA[:], cx.ps[pa][:, 0:128], AF.Copy, [('ps', pa), ('EG', d)], ['OA'], scale=EG[d][:, dc, col])
                        pb = self.bank('d')
                        self.mm(cx.ps[pb][:, 0:128], QKT[:], VN[:], ['QKT', 'VN'], [('ps', pb)])
                        self.tt(O[rows, dc, :], OA[rows, :], cx.ps[pb][rows, 0:128], ALU.add, ['OA', ('ps', pb)], [('O', dc)])
                        pss = self.bank('d')
                        self.mm(cx.ps[pss][:, 0:128], KOh[:], VN[:], [kok, 'VN'], [('ps', pss)])
                        self.stt(Sst[:], Sst[:], ELh, cx.ps[pss][:, 0:128], ALU.mult, ALU.add, ['Sst', elk, ('ps', pss)], ['Sst'])
                    scans.append(self.s.ops)
                for u in range(0, len(order), 2):
                    for oa_, ob_ in zip(pres[u], pres[u + 1]):
                        seq.append(oa_)
                        seq.append(ob_)
                    seq.extend(scans[u])
                    seq.extend(scans[u + 1])
                chains.append(seq)
                self.s.ops = saved_ops
                self.chain = None
                self.s.keyfn = None
            assert len(chains[0]) == len(chains[1])
            for oa_, ob_ in zip(chains[0], chains[1]):
                self.s.ops.append(oa_)
                self.s.ops.append(ob_)
            zt, ot = Td[0][0]['zt'], Td[0][0]['ot']
            for dc in range(NDC):
                r0 = dc * 128
                self.tt(ot[:], Od[0][:, dc, :], Od[1][:, dc, :], ALU.add, [('O', 0, dc), ('O', 1, dc)], ['ot'])
                self.rms_rows(st, ot[:], 128, ['ot'])
                self.stt(ot[:], ot[:], st[:, 3:4], gnw[:], ALU.mult, ALU.mult, ['ot', 'st3', 'gnw'], ['ot'])
                self.dma(zt[:], Ptm[r0:r0 + 128, 1536 + j * 128:1536 + (j + 1) * 128], [], ['zt'])
                self.act(zt[:], zt[:], AF.Silu, ['zt'], ['zt'])
                self.tt(ot[:], ot[:], zt[:], ALU.mult, ['ot', 'zt'], ['ot'])
                self.dma(yout[r0:r0 + 128, 512 + j * 128:512 + (j + 1) * 128], ot[:], ['ot'], [])
        cx.pop()


def gdn_masks():
    p = np.arange(128)[:, None]
    f = np.arange(128)[None, :]
    same = (p // 64) == (f // 64)
    m = np.zeros((9, 128, 128), np.float32)
    m[0] = same & (p <= f)
    m[1] = same & (p >= f)
    m[2] = same
    m[3] = (p < 64) & (f >= 0)
    m[4] = (p >= 64) & (f >= 0)
    m[5] = same & (p > f)
    m[6] = same & (p < f)
    m[7] = same & (f >= p)
    m[8] = same & (f <= p)
    return m


def build_l2_prog(S, layer):
    lam_init = 0.8 - 0.6 * math.exp(-0.3 * layer)
    nc = bass.Bass("TRN2", target_bir_lowering=False)
    NT = S // 512
    x = dram_in(nc, "x", [S, D])
    ln = dram_in(nc, "ln", [128, 32])
    pos = dram_in(nc, "pos", [S], I32)
    wsl = dram_in(nc, "wsl", [NSLAB, D, 512])
    pm = dram_in(nc, "pm", [128, 128])
    ident = dram_in(nc, "ident", [128, 128])
    tab = dram_in(nc, "tab", [128, 4])
    lng = dram_in(nc, "lng", [2])
    cw = dram_in(nc, "cw", [128, 60])
    alog = dram_in(nc, "alog", [8]); dtb = dram_in(nc, "dtb", [8]); gnw = dram_in(nc, "gnw", [128])
    masks = dram_in(nc, "masks", [9, 128, 128])
    dl = dram_in(nc, "dl", [512]); sub = dram_in(nc, "sub", [256])
    y = dram_out(nc, "y", [S, YW])
    hTd = dram_tmp(nc, "hTd", [NT, 128, 32 * 512], BF16)
    Pfm = dram_tmp(nc, "Pfm", [NFM, 128, S]); Ptm = dram_tmp(nc, "Ptm", [S, NTM])
    cosd = dram_tmp(nc, "cosd", [128, S]); sind = dram_tmp(nc, "sind", [128, S])
    cx = Ctx(nc)
    mx = Mix(cx, S)
    mx.rope_tables(pos, tab, cosd, sind)
    cx.push()
    b = Blocks(cx, ident)
    mx.phase0(b, x, ln, hTd)
    cx.s.barrier()
    mx.phase1(b, wsl, hTd, Pfm, Ptm)
    cx.pop()
    mx.diff_phase(Pfm, Ptm, cosd, sind, dl, sub, y, lam_init, pm)
    mx.ret_phase(Pfm, Ptm, cosd, sind, lng, y, pm)
    mx.gdn_phase(Pfm, Ptm, cw, alog, dtb, gnw, masks, ident, y)
    cx.s.finish()
    return nc


O_RQ, O_RK, O_RV, O_RG = 0, 1024, 2048, 4096
O_GQKV, O_GZ, O_GA, O_GB = 6144, 12288, 14336, 14368
O_DQ, O_DK, O_DV, O_GATE = 14400, 16448, 18496, 20544
_PERM = (np.arange(128) + 64) % 128


def l2_weight_slabs(w_in, g):
    out = np.zeros((NSLAB, D, 512), np.float32)
    si = 0
    for j in range(2):
        h = 2 * g + j
        for n, base in enumerate((O_DQ, O_DK)):
            for t in range(2):
                c0 = base + (2 * h + t) * 128
                out[si, :, (2 * n + t) * 128:(2 * n + t + 1) * 128] = w_in[:, c0:c0 + 128]
        si += 1
        out[si, :, 0:256] = w_in[:, O_DV + h * 256:O_DV + (h + 1) * 256]
        si += 1
    for j in range(2):
        h = 2 * g + j
        for n, base in enumerate((O_RQ, O_RK)):
            out[si, :, (2 * j + n) * 128:(2 * j + n + 1) * 128] = w_in[:, base + h * 128:base + (h + 1) * 128]
    si += 1
    for j in range(2):
        h = 2 * g + j
        out[si, :, 0:256] = w_in[:, O_RV + h * 256:O_RV + (h + 1) * 256]
        out[si, :, 256:512] = w_in[:, O_RG + h * 256:O_RG + (h + 1) * 256]
        si += 1
    for j in range(4):
        h = 4 * g + j
        for r in range(3):
            out[si, :, r * 128:(r + 1) * 128] = w_in[:, O_GQKV + r * 2048 + h * 128:O_GQKV + r * 2048 + (h + 1) * 128]
        out[si, :, 384:512] = w_in[:, O_GZ + h * 128:O_GZ + (h + 1) * 128]
        si += 1
    for n, base in enumerate((O_GA, O_GB)):
        for d in range(2):
            c0 = base + d * 16 + 4 * g
            out[si, :, n * 8 + d * 4:n * 8 + d * 4 + 4] = w_in[:, c0:c0 + 4]
    return out


def l2_consts(inputs, layer, g):
    conv_w = np.asarray(inputs["conv_w"][layer], np.float32)
    cw = np.zeros((128, 60), np.float32)
    for j in range(4):
        h = 4 * g + j
        for r in range(3):
            for t in range(5):
                cw[:, (j * 3 + r) * 5 + t] = conv_w[t, r * 2048 + h * 128:r * 2048 + (h + 1) * 128]
    a_log = np.asarray(inputs["gdn_a_log"][layer], np.float32)
    dtb = np.asarray(inputs["gdn_dt_bias"][layer], np.float32)
    inv = (1.0 / (10000.0 ** (np.arange(0, 128, 2, dtype=np.float32) / 128))).astype(np.float32)
    tab = np.zeros((128, 4), np.float32)
    tab[:, 0] = np.concatenate([inv, inv])
    tab[:, 1] = np.where(np.arange(128) < 64, -1.0, 1.0)
    heads = np.arange(2 * g, 2 * g + 2, dtype=np.float32)
    lng = np.log(1.0 - 2.0 ** (-5.0 - heads)).astype(np.float32)
    return {
        "cw": cw,
        "alog": np.ascontiguousarray(a_log[:, 4 * g:4 * g + 4]).reshape(-1),
        "dtb": np.ascontiguousarray(dtb[:, 4 * g:4 * g + 4]).reshape(-1),
        "gnw": np.asarray(inputs["gdn_norm_w"][layer], np.float32),
        "masks": gdn_masks(),
        "dl": np.asarray(inputs["diff_lambda"][layer], np.float32).reshape(-1),
        "sub": np.asarray(inputs["diff_subln_w"][layer], np.float32),
        "tab": tab, "lng": lng, "ident": np.eye(128, dtype=np.float32),
        "pm": np.eye(128, dtype=np.float32)[_PERM].T.copy(),
    }
```
